# Optimizing a Trainium2 kernel written in Bass

```python
import jax
import jax.numpy as jnp
from jax import lax
import numpy as np

D_MODEL = 1024
BATCH = 2
SEQ = 16384
DEPTH = 2

CHUNK = 64
Q_BLOCK = 128
EPS = 1e-6
ROPE_BASE = 10000.0
RET_HEADS = 4
RET_DK = 128
RET_DV = 256
RET_QK = RET_HEADS * RET_DK
RET_V = RET_HEADS * RET_DV
LRU_WIDTH = D_MODEL
LRU_BLOCKS = 8
LRU_BW = LRU_WIDTH // LRU_BLOCKS
CONV_W = 4
LRU_C = 8.0
FOX_HEADS = 8
FOX_DH = 128
FOX_W = FOX_HEADS * FOX_DH
D_FF = 4 * D_MODEL
N_BRANCH = 3
SPLITS = (RET_QK, RET_QK, RET_V, RET_V, LRU_WIDTH, LRU_WIDTH, FOX_W, FOX_W, FOX_W, FOX_HEADS, N_BRANCH * D_MODEL)
D_IN = 2 * RET_QK + 2 * RET_V + 2 * LRU_WIDTH + 3 * FOX_W + FOX_HEADS + N_BRANCH * D_MODEL

kernel_name = 'chunk_causal_hybrid_retention_rglru_fox_block'


def rms_norm(x, g):
    xf = x.astype(jnp.float32)
    y = xf * lax.rsqrt(jnp.mean(xf * xf, axis=-1, keepdims=True) + EPS)
    return (y * g.astype(jnp.float32)).astype(x.dtype)


def head_norm(o):
    mu = jnp.mean(o, axis=-1, keepdims=True)
    var = jnp.mean(jnp.square(o - mu), axis=-1, keepdims=True)
    return (o - mu) * lax.rsqrt(var + EPS)


def rope(x, pos):
    half = x.shape[-1] // 2
    inv = ROPE_BASE ** (-jnp.arange(half, dtype=jnp.float32) / half)
    ang = pos.astype(jnp.float32)[:, None] * inv[None, :]
    cos = jnp.cos(ang)[None, :, None, :]
    sin = jnp.sin(ang)[None, :, None, :]
    xf = x.astype(jnp.float32)
    x1, x2 = xf[..., :half], xf[..., half:]
    return jnp.concatenate([x1 * cos - x2 * sin, x1 * sin + x2 * cos], axis=-1)


def retention(q, k, v):
    B, S, H, dk = q.shape
    nc = S // CHUNK
    log_g = jnp.log(1.0 - 2.0 ** (-5.0 - jnp.arange(H, dtype=jnp.float32)))
    idx = jnp.arange(CHUNK, dtype=jnp.float32)
    inner = jnp.exp(jnp.abs(idx[:, None] - idx[None, :])[None] * log_g[:, None, None])
    q_dec = jnp.exp((idx + 1.0)[None, :] * log_g[:, None])[None, :, :, None]
    k_dec = jnp.exp((CHUNK - 1.0 - idx)[None, :] * log_g[:, None])[None, :, :, None]
    chunk_dec = jnp.exp(CHUNK * log_g)[None, :, None, None]

    def to_chunks(t):
        return t.astype(jnp.float32).reshape(B, nc, CHUNK, H, t.shape[-1]).transpose(1, 0, 3, 2, 4)

    qc, kc, vc = to_chunks(q), to_chunks(k * RET_DK ** -0.5), to_chunks(v)

    def step(state, xs):
        qi, ki, vi = xs
        s = jnp.einsum('bhqd,bhkd->bhqk', qi, ki) * inner[None]
        o = jnp.einsum('bhqk,bhkv->bhqv', s, vi) + jnp.einsum('bhqd,bhdv->bhqv', qi * q_dec, state)
        state = state * chunk_dec + jnp.einsum('bhkd,bhkv->bhdv', ki * k_dec, vi)
        return state, o

    state0 = jnp.zeros((B, H, dk, v.shape[-1]), jnp.float32)
    _, o = lax.scan(step, state0, (qc, kc, vc))
    return o.transpose(1, 0, 3, 2, 4).reshape(B, S, H, v.shape[-1])


def causal_conv(x, w, b):
    S = x.shape[1]
    xp = jnp.pad(x, ((0, 0), (CONV_W - 1, 0), (0, 0)))
    out = xp[:, 0:S] * w[0]
    for i in range(1, CONV_W):
        out = out + xp[:, i:i + S] * w[i]
    return out + b


def rg_lru(x, w_a, b_a, w_x, b_x, lam):
    B, S, W = x.shape
    xb = x.reshape(B, S, LRU_BLOCKS, LRU_BW)
    r = jax.nn.sigmoid(jnp.einsum('bsni,nio->bsno', xb, w_a).reshape(B, S, W) + b_a)
    i = jax.nn.sigmoid(jnp.einsum('bsni,nio->bsno', xb, w_x).reshape(B, S, W) + b_x)
    log_a = -LRU_C * r.astype(jnp.float32) * jax.nn.softplus(-lam.astype(jnp.float32))
    a = jnp.exp(log_a)
    u = jnp.sqrt(jnp.maximum(-jnp.expm1(2.0 * log_a), 0.0)) * (i * x).astype(jnp.float32)

    def combine(left, right):
        a1, b1 = left
        a2, b2 = right
        return a1 * a2, a2 * b1 + b2

    _, h = lax.associative_scan(combine, (a, u), axis=1)
    return h.astype(x.dtype)


def forgetting_attention(q, k, v, log_f):
    B, S, H, D = q.shape
    nb = S // Q_BLOCK
    cum = jnp.cumsum(log_f, axis=1).transpose(0, 2, 1)
    kf = k.astype(jnp.float32)
    vf = v.astype(jnp.float32)
    qb = q.astype(jnp.float32).reshape(B, nb, Q_BLOCK, H, D).transpose(1, 0, 2, 3, 4)
    cq = cum.reshape(B, H, nb, Q_BLOCK).transpose(2, 0, 1, 3)
    kpos = jnp.arange(S)
    scale = D ** -0.5

    def block(args):
        qi, ci, start = args
        s = jnp.einsum('bqhd,bkhd->bhqk', qi, kf) * scale + (ci[..., None] - cum[:, :, None, :])
        qpos = start + jnp.arange(Q_BLOCK)
        s = jnp.where((kpos[None, :] <= qpos[:, None])[None, None], s, -jnp.inf)
        p = jax.nn.softmax(s, axis=-1)
        return jnp.einsum('bhqk,bkhd->bqhd', p, vf)

    o = lax.map(block, (qb, cq, jnp.arange(nb) * Q_BLOCK))
    return o.transpose(1, 0, 2, 3, 4).reshape(B, S, H, D)


def hybrid_layer(x, c, ada_w, ada_b, g_pre_mix, g_post_mix, g_pre_mlp, g_post_mlp, w_in, conv_w, conv_b,
                 w_a, b_a, w_x, b_x, lam, b_f, ret_w_o, lru_w_o, fox_w_o, w_out, w1, w2):
    B, S, _ = x.shape
    mod = jax.nn.silu(c) @ ada_w + ada_b
    sh_m, sc_m, gt_m, sh_f, sc_f, gt_f = [t[:, None, :] for t in jnp.split(mod, 6, axis=-1)]

    h = rms_norm(x, g_pre_mix) * (1.0 + sc_m) + sh_m
    proj = h @ w_in
    offs = [int(o) for o in np.cumsum(SPLITS)[:-1]]
    rq, rk, rv, rg, lx, ly, fq, fk, fv, ff, gates = jnp.split(proj, offs, axis=-1)
    pos = jnp.arange(S)

    rq = rope(rq.reshape(B, S, RET_HEADS, RET_DK), pos)
    rk = rope(rk.reshape(B, S, RET_HEADS, RET_DK), pos)
    ret = head_norm(retention(rq, rk, rv.reshape(B, S, RET_HEADS, RET_DV)))
    y_ret = jax.nn.silu(rg) * ret.reshape(B, S, RET_V).astype(x.dtype)

    u = causal_conv(lx, conv_w, conv_b)
    y_lru = jax.nn.gelu(ly) * rg_lru(u, w_a, b_a, w_x, b_x, lam)

    log_f = jax.nn.log_sigmoid((ff + b_f).astype(jnp.float32))
    fo = forgetting_attention(fq.reshape(B, S, FOX_HEADS, FOX_DH), fk.reshape(B, S, FOX_HEADS, FOX_DH),
                              fv.reshape(B, S, FOX_HEADS, FOX_DH), log_f)
    y_fox = fo.reshape(B, S, FOX_W).astype(x.dtype)

    g = jax.nn.sigmoid(gates).reshape(B, S, N_BRANCH, D_MODEL)
    merged = g[:, :, 0] * (y_ret @ ret_w_o) + g[:, :, 1] * (y_lru @ lru_w_o) + g[:, :, 2] * (y_fox @ fox_w_o)
    x = x + gt_m * rms_norm(merged @ w_out, g_post_mix)

    h = rms_norm(x, g_pre_mlp) * (1.0 + sc_f) + sh_f
    f = jnp.square(jax.nn.relu(h @ w1)) @ w2
    return x + gt_f * rms_norm(f, g_post_mlp)


def setup_inputs(seed: int = 0) -> dict:
    key = jax.random.key(seed)
    ks = jax.random.split(key, 28)
    f32 = jnp.float32
    L, D = DEPTH, D_MODEL

    def nrm(k, shape, scale):
        return jax.random.normal(k, shape, f32) * scale

    a0 = jax.random.uniform(ks[14], (L, LRU_WIDTH), f32, 0.9, 0.999)
    s = a0 ** (1.0 / LRU_C)
    lam = jnp.log(s) - jnp.log1p(-s)
    return {
        'x': nrm(ks[0], (BATCH, SEQ, D), 1.0),
        'c': nrm(ks[1], (BATCH, D), 1.0),
        'ada_w': nrm(ks[2], (L, D, 6 * D), D ** -0.5),
        'ada_b': nrm(ks[3], (L, 6 * D), 0.02),
        'norm_pre_mix': 1.0 + nrm(ks[4], (L, D), 0.05),
        'norm_post_mix': 1.0 + nrm(ks[5], (L, D), 0.05),
        'norm_pre_mlp': 1.0 + nrm(ks[6], (L, D), 0.05),
        'norm_post_mlp': 1.0 + nrm(ks[7], (L, D), 0.05),
        'w_in': nrm(ks[8], (L, D, D_IN), D ** -0.5),
        'conv_w': nrm(ks[9], (L, CONV_W, LRU_WIDTH), CONV_W ** -0.5),
        'conv_b': nrm(ks[10], (L, LRU_WIDTH), 0.02),
        'lru_w_a': nrm(ks[11], (L, LRU_BLOCKS, LRU_BW, LRU_BW), LRU_BW ** -0.5),
        'lru_b_a': nrm(ks[12], (L, LRU_WIDTH), 0.02),
        'lru_w_x': nrm(ks[13], (L, LRU_BLOCKS, LRU_BW, LRU_BW), LRU_BW ** -0.5),
        'lru_b_x': nrm(ks[15], (L, LRU_WIDTH), 0.02),
        'lru_lambda': lam,
        'fox_b_f': jax.random.uniform(ks[16], (L, FOX_HEADS), f32, 1.0, 5.0),
        'ret_w_o': nrm(ks[17], (L, RET_V, D), RET_V ** -0.5),
        'lru_w_o': nrm(ks[18], (L, LRU_WIDTH, D), LRU_WIDTH ** -0.5),
        'fox_w_o': nrm(ks[19], (L, FOX_W, D), FOX_W ** -0.5),
        'w_out': nrm(ks[20], (L, D, D), D ** -0.5),
        'mlp_w1': nrm(ks[21], (L, D, D_FF), D ** -0.5),
        'mlp_w2': nrm(ks[22], (L, D_FF, D), D_FF ** -0.5),
    }


def reference(x, c, ada_w, ada_b, norm_pre_mix, norm_post_mix, norm_pre_mlp, norm_post_mlp, w_in, conv_w,
              conv_b, lru_w_a, lru_b_a, lru_w_x, lru_b_x, lru_lambda, fox_b_f, ret_w_o, lru_w_o, fox_w_o,
              w_out, mlp_w1, mlp_w2):
    for l in range(DEPTH):
        x = hybrid_layer(x, c, ada_w[l], ada_b[l], norm_pre_mix[l], norm_post_mix[l], norm_pre_mlp[l],
                         norm_post_mlp[l], w_in[l], conv_w[l], conv_b[l], lru_w_a[l], lru_b_a[l], lru_w_x[l],
                         lru_b_x[l], lru_lambda[l], fox_b_f[l], ret_w_o[l], lru_w_o[l], fox_w_o[l], w_out[l],
                         mlp_w1[l], mlp_w2[l])
    return x
```

```python
import os
import numpy as np
from contextlib import ExitStack
import concourse.bass as bass
import concourse.mybir as mybir
from concourse.bass_utils import run_bass_kernel_spmd

F32 = mybir.dt.float32
BF16 = mybir.dt.bfloat16
AF = mybir.ActivationFunctionType
ALU = mybir.AluOpType
AX = mybir.AxisListType
BIG = 1 << 40


class Buf:
    __slots__ = ("name", "lw", "rd", "lane", "bulk")

    def __init__(self, name="", bulk=False):
        self.name = name
        self.lw = None
        self.rd = {}
        self.lane = None
        self.bulk = bulk


class Sched:
    ENG = ("pe", "act", "dve", "pool", "sp")

    def __init__(self, nc, stack):
        self.nc = nc
        self.stack = stack
        self.q = {k: [] for k in self.ENG}
        self.sem = {k: stack.enter_context(nc.semaphore("s_" + k)) for k in self.ENG}
        self.cnt = {k: 0 for k in self.ENG}
        self.waited = {k: {} for k in self.ENG}
        self.lanes = []
        self.nlane = 0
        self.out_lanes = []
        self.need_rank = False
        self.free_lanes = []
        self.live_lanes = []
        self.nepoch = 1
        self.q_rank = None

    def _lane(self, buf):
        if buf.lane is None:
            if self.free_lanes and not buf.bulk:
                buf.lane = self.free_lanes.pop()
            else:
                sem = self.stack.enter_context(self.nc.semaphore("l%d" % self.nlane))
                self.nlane += 1
                buf.lane = [sem, 0, buf.bulk]
                self.lanes.append(buf.lane)
            self.live_lanes.append(buf.lane)
        return buf.lane

    def lane_mark(self):
        return len(self.live_lanes)

    def lane_release(self, mark):
        self.free_lanes.extend(self.live_lanes[mark:])
        del self.live_lanes[mark:]

    def new_epoch(self):
        for k in self.ENG:
            self.sem[k] = self.stack.enter_context(self.nc.semaphore("s%d_%s" % (self.nepoch, k)))
            self.cnt[k] = 0
        self.nepoch += 1

    def _deps(self, eng, reads, writes):
        d = {}

        def add(t):
            if t is None:
                return
            s, v = t
            if d.get(s, 0) < v:
                d[s] = v
        own = self.sem[eng]
        for b in reads:
            add(b.lw)
        for b in writes:
            if b.lw is not None and b.lw[0] is not own:
                add(b.lw)
            for s, v in b.rd.items():
                if s is not own:
                    add((s, v))
        w = self.waited[eng]
        waits = []
        for s, v in d.items():
            if eng == "pe" and s is own:
                continue
            if w.get(s, 0) < v:
                w[s] = v
                waits.append((s, v))
        return waits

    def _mark(self, tok, reads, writes):
        s, v = tok
        for b in reads:
            if b.rd.get(s, 0) < v:
                b.rd[s] = v
        for b in writes:
            b.lw = tok
            b.rd = {}

    def op(self, eng, fn, reads=(), writes=()):
        waits = self._deps(eng, reads, writes)
        self.cnt[eng] += 1
        tok = (self.sem[eng], self.cnt[eng])
        self.q[eng].append((waits, fn, self.sem[eng], 1))
        self._mark(tok, reads, writes)
        return tok

    def dma(self, out_ap, in_ap, lane_buf, reads=(), writes=(), q="sp", is_output=False, **kw):
        waits = self._deps(q, reads, writes)
        lane = self._lane(lane_buf)
        lane[1] += 16
        tok = (lane[0], BIG if lane[2] else lane[1])
        self.q[q].append((waits, lambda e: e.dma_start(out=out_ap, in_=(in_ap(self, e) if callable(in_ap) else in_ap), **kw), lane[0], 16))
        self._mark(tok, reads, writes)
        if is_output and lane not in self.out_lanes:
            self.out_lanes.append(lane)
        return tok

    def coll(self, fn, lane_buf, reads=(), writes=()):
        waits = self._deps("pool", reads, writes)
        lane = self._lane(lane_buf)
        lane[1] += 1
        tok = (lane[0], lane[1])
        self.q["pool"].append((waits, fn, lane[0], 1))
        self._mark(tok, reads, writes)
        return tok

    def barrier(self):
        targets = [(self.sem[k], self.cnt[k]) for k in ("pe", "act", "dve", "pool") if self.cnt[k] > 0]
        targets += [(l[0], l[1]) for l in self.lanes if l[1] > 0 and not l[2]]
        targets += [(l[0], BIG) for l in self.lanes if l[1] > 0 and l[2]]
        for eng in self.ENG:
            w = self.waited[eng]
            waits = []
            for s, v in targets:
                if w.get(s, 0) < v:
                    w[s] = v
                    waits.append((s, v))
            if waits:
                self.q[eng].append((waits, None, None, 0))

    def emit(self):
        nc = self.nc
        final = {id(l[0]): l[1] for l in self.lanes}
        lanes_by_sem = {id(l[0]): l for l in self.lanes}

        def run(eng_name, e):
            if eng_name == "sp" and self.need_rank:
                self.q_rank = e.snap(e.partition_id() % 4)
            for waits, fn, sem, inc in self.q[eng_name]:
                for s, v in waits:
                    if v >= BIG:
                        v = final[id(s)]
                    e.wait_ge(s, v)
                if fn is not None:
                    fn(e).then_inc(sem, inc)
            if eng_name == "sp":
                for l in self.lanes:
                    if l[1] > 0:
                        e.wait_ge(l[0], l[1])
                for k in ("pe", "act", "dve", "pool"):
                    if self.cnt[k] > 0:
                        e.wait_ge(self.sem[k], self.cnt[k])

        with nc.Block() as block:
            @block.sync
            def _(e):
                run("sp", e)

            @block.tensor
            def _(e):
                run("pe", e)

            @block.scalar
            def _(e):
                run("act", e)

            @block.vector
            def _(e):
                run("dve", e)

            @block.gpsimd
            def _(e):
                run("pool", e)

    def stats(self):
        return {k: len(v) for k, v in self.q.items()}


EPS = 1e-6
D = 1024
DECL = []


class Tl:
    __slots__ = ("t", "b")

    def __init__(self, t, name):
        self.t = t
        self.b = Buf(name)

    def __getitem__(self, k):
        return self.t[k]


class Cx:
    def __init__(self, nc, stack):
        self.nc = nc
        self.S = Sched(nc, stack)
        self.root = stack
        self.stack = stack
        self.n = 0

    def sb(self, shape, dt, name=None, n=1):
        r = []
        for _ in range(n):
            self.n += 1
            nm = "%s_%d" % (name or "sb", self.n)
            r.append(Tl(self.stack.enter_context(self.nc.sbuf_tensor(nm, list(shape), dt)), nm))
        return r[0] if n == 1 else r

    def ps(self, shape, dt, name=None, n=1):
        r = []
        for _ in range(n):
            self.n += 1
            nm = "%s_%d" % (name or "ps", self.n)
            r.append(Tl(self.stack.enter_context(self.nc.psum_tensor(nm, list(shape), dt)), nm))
        return r[0] if n == 1 else r

    def mm(self, out, lhsT, rhs, start, stop, rd, wr, **kw):
        self.S.op("pe", lambda e: e.matmul(out, lhsT=lhsT, rhs=rhs, start=start, stop=stop, **kw), rd, wr)

    def tr(self, out, in_, ident, rd, wr):
        self.S.op("pe", lambda e: e.transpose(out=out, in_=in_, identity=ident), rd, wr)

    def act(self, out, in_, func, rd, wr, bias=None, scale=None, accum=None):
        kw = {}
        if bias is not None:
            kw["bias"] = bias
        if scale is not None:
            kw["scale"] = scale
        if accum is not None:
            kw["accum_out"] = accum
        self.S.op("act", lambda e: e.activation(out=out, in_=in_, func=func, **kw), rd, wr)

    def tt(self, eng, out, a, b, op, rd, wr):
        self.S.op(eng, lambda e: e.tensor_tensor(out=out, in0=a, in1=b, op=op), rd, wr)

    def ts(self, eng, out, a, s1, s2, op0, op1, rd, wr):
        if s2 is None:
            self.S.op(eng, lambda e: e.tensor_single_scalar(out=out, in_=a, scalar=s1, op=op0), rd, wr)
        else:
            self.S.op(eng, lambda e: e.tensor_scalar(out=out, in0=a, scalar1=s1, scalar2=s2, op0=op0, op1=op1), rd, wr)

    def stt(self, eng, out, a, s, b, op0, op1, rd, wr):
        self.S.op(eng, lambda e: e.scalar_tensor_tensor(out=out, in0=a, scalar=s, in1=b, op0=op0, op1=op1), rd, wr)

    def cp(self, eng, out, in_, rd, wr):
        if eng == "act":
            self.S.op("act", lambda e: e.copy(out=out, in_=in_), rd, wr)
        else:
            self.S.op(eng, lambda e: e.tensor_copy(out=out, in_=in_), rd, wr)

    def recip(self, out, in_, rd, wr):
        self.S.op("dve", lambda e: e.reciprocal(out=out, in_=in_), rd, wr)

    def memset(self, eng, ap, v, wr):
        self.S.op(eng, lambda e: e.memset(ap, v), (), wr)

    def load(self, out, in_, tl, q="sp", rd=()):
        self.S.dma(out, in_, tl.b, reads=rd, writes=[tl.b], q=q)

    def store(self, out, in_, tl, dbuf=None, q="pool", is_output=False):
        self.S.dma(out, in_, tl.b, reads=[tl.b], writes=[dbuf] if dbuf is not None else (), q=q, is_output=is_output)


def load_w_bf16(cx, dst, dst_c0, dram, kchunks, ncols, stg):
    src = dram.rearrange("(c p) n -> p c n", p=128)
    i = 0
    for c0 in range(0, kchunks, 8):
        c1 = min(kchunks, c0 + 8)
        for n0 in range(0, ncols, 512):
            n1 = min(ncols, n0 + 512)
            s = stg[i % len(stg)]
            i += 1
            cx.load(s[:, 0:c1 - c0, 0:n1 - n0], src[:, c0:c1, n0:n1], s)
            eng = "pool" if i % 2 == 0 else "dve"
            cx.cp(eng, dst[:, dst_c0 + c0:dst_c0 + c1, n0:n1], s[:, 0:c1 - c0, 0:n1 - n0], [s.b], [dst.b])


def compute_mod(cx, cvec, adaw, adab, col0, ncols, MOD, CON, stg, pbank):
    crep = cx.sb([128, D], F32, "crep")
    sB = cx.sb([128, 8, 128], F32, "sB")
    brep = cx.sb([128, 512], F32, "brep", n=2)
    cx.load(crep[:, :], cvec.partition_broadcast(128), crep)
    cx.act(crep[:, :], crep[:, :], AF.Silu, [crep.b], [crep.b])
    for c in range(8):
        p = pbank[c % 2]
        cx.tr(p[:, 0:128], crep[:, c * 128:(c + 1) * 128], CON[:, 384:512], [crep.b, CON.b], [p.b])
        cx.cp("dve", sB[:, c, :], p[:, 0:128], [p.b], [sB.b])
    aw = adaw.rearrange("(c p) n -> p c n", p=128)
    for i, n0 in enumerate(range(0, ncols, 512)):
        s = stg[i % len(stg)]
        br = brep[i % 2]
        p = pbank[i % 2]
        cx.load(s[:, :, :], aw[:, :, col0 + n0:col0 + n0 + 512], s)
        cx.load(br[:, :], adab[col0 + n0:col0 + n0 + 512].partition_broadcast(128), br)
        for c in range(8):
            cx.mm(p[:, :], sB[:, c, :], s[:, c, :], c == 0, c == 7, [sB.b, s.b], [p.b])
        cx.tt("dve", MOD[:, n0:n0 + 512], p[:, :], br[:, :], ALU.add, [p.b, br.b], [MOD.b])


def norm_mod_tile(cx, xt, G2, SH, modb, hb, ss, rs, t1, junk):
    cx.act(junk[:, :], xt[:, :], AF.Square, [xt.b], [junk.b, ss.b], accum=ss[:, 0:1])
    cx.act(rs[:, 0:1], ss[:, 0:1], AF.Sqrt, [ss.b], [rs.b], bias=EPS, scale=1.0 / D)
    cx.recip(rs[:, 0:1], rs[:, 0:1], [rs.b], [rs.b])
    cx.stt("dve", t1[:, :], xt[:, :], rs[:, 0:1], G2, ALU.mult, ALU.mult, [xt.b, rs.b, modb], [t1.b])
    cx.tt("pool", hb[:, :], t1[:, :], SH, ALU.add, [t1.b, modb], [hb.b])


def declare_B(nc, SL, sfx="", x_ap=None, y_internal=False):
    def din(name, shape, dt=F32):
        DECL.append((name + sfx, tuple(shape)))
        return nc.dram_tensor(name + sfx, list(shape), dt, kind="ExternalInput").ap()
    A = {}
    A["xb"] = din("xb", [SL, D]) if x_ap is None else x_ap
    A["cvec"] = din("cvec", [D])
    A["adaw"] = din("adaw", [D, 2048])
    A["adab"] = din("adab", [2048])
    A["gpre"] = din("gpre", [D])
    A["w1"] = din("w1", [D, 1024])
    A["w2"] = din("w2", [D, 512])
    A["w3"] = din("w3", [2, D, 385])
    A["rc"] = din("rc", [128, 258])
    A["lruv"] = din("lruv", [128, 2, 8])
    A["lruw"] = din("lruw", [4, 128, 128])
    A["bf"] = din("bf", [2])
    A["yT"] = nc.dram_tensor("yT" + sfx, [768, SL], BF16, kind=("Internal" if y_internal else "ExternalOutput")).ap()
    A["hTd"] = nc.dram_tensor("hTd" + sfx, [D, SL], BF16, kind="Internal").ap()
    return A


def make_consts(cx, cond):
    CON = cx.sb([128, 512], F32, "CON")
    IDB = cx.sb([128, 128], BF16, "IDB")
    MASKB = cx.sb([128, 128], BF16, "MASKB")
    ONESB = cx.sb([128, 512], F32, "ONESB")
    NEGM = cx.sb([128, 128], F32, "NEGM")
    cx.load(CON[:, :], cond, CON)
    cx.cp("dve", IDB[:, :], CON[:, 384:512], [CON.b], [IDB.b])
    cx.cp("dve", MASKB[:, :], CON[:, 0:128], [CON.b], [MASKB.b])
    cx.memset("pool", ONESB[:, :], 1.0, [ONESB.b])
    cx.ts("dve", NEGM[:, :], CON[:, 0:128], 30000.0, -30000.0, ALU.mult, ALU.add, [CON.b], [NEGM.b])
    return {"CON": CON, "IDB": IDB, "MASKB": MASKB, "ONESB": ONESB, "NEGM": NEGM}


def emit_B(cx, SL, A, K, parts="0123"):
    nc, S, base = cx.nc, cx.S, cx.stack
    xb, cvec, adaw, adab, gpre = A["xb"], A["cvec"], A["adaw"], A["adab"], A["gpre"]
    w1d, w2d, w3d, cosd, sind = A["w1"], A["w2"], A["w3"], A["cosT"], A["sinT"]
    rcd, lvd, lwd, bfd, yT, hTd = A["rc"], A["lruv"], A["lruw"], A["bf"], A["yT"], A["hTd"]
    x_rd = list(A.get("x_rd", ()))
    xb_tile = A.get("xb_tile") or (lambda i: xb[i * 128:(i + 1) * 128, :])
    y_dst = A.get("y_dst") or (lambda r0, nr, j: yT[r0:r0 + nr, j * 512:(j + 1) * 512])
    CON, IDB, MASKB, ONESB, NEGM = K["CON"], K["IDB"], K["MASKB"], K["ONESB"], K["NEGM"]
    TRI, ONES, SEL, IDF = CON[:, 0:128], CON[:, 128:256], CON[:, 256:384], CON[:, 384:512]
    hTd_b = Buf("hTd")
    yT_b = A.setdefault("yT_b", Buf("yTd"))
    hTv = hTd.rearrange("(c p) t -> p c t", p=128)
    NT = SL // 512
    NB = SL // 128
    lmark = S.lane_mark()
    with ExitStack() as root:
        cx.stack = root
        with ExitStack() as st:
            cx.stack = st
            MOD = cx.sb([128, 2048], F32, "MOD")
            GB = cx.sb([128, D], F32, "GB")
            stg = cx.sb([128, 8, 512], F32, "stg", n=2)
            pb = cx.ps([128, 512], F32, "pmod", n=2)
            compute_mod(cx, cvec, adaw, adab, 0, 2048, MOD, CON, stg, pb)
            cx.load(GB[:, :], gpre.partition_broadcast(128), GB)
            cx.stt("dve", MOD[:, 1024:2048], MOD[:, 1024:2048], 1.0, GB[:, :], ALU.add, ALU.mult, [MOD.b, GB.b], [MOD.b])
            xt = cx.sb([128, D], F32, "xt", n=3)
            junk = cx.sb([128, D], BF16, "junk")
            ss = cx.sb([128, 1], F32, "ss", n=3)
            rs = cx.sb([128, 1], F32, "rs", n=3)
            t1 = cx.sb([128, D], F32, "t1", n=2)
            hb = cx.sb([128, D], BF16, "hb", n=2)
            pT = cx.ps([128, 8, 128], BF16, "pT", n=2)
            hTt = cx.sb([128, 8, 512], BF16, "hTt", n=2)
            cx.load(xt[0][:, :], xb_tile(0), xt[0], rd=x_rd)
            for i in range(NB):
                if i + 1 < NB:
                    cx.load(xt[(i + 1) % 3][:, :], xb_tile(i + 1), xt[(i + 1) % 3], rd=x_rd)
                x_, h_, p_, g_ = xt[i % 3], hb[i % 2], pT[i % 2], hTt[(i // 4) % 2]
                norm_mod_tile(cx, x_, MOD[:, 1024:2048], MOD[:, 0:1024], MOD.b, h_, ss[i % 3], rs[i % 3], t1[i % 2], junk)
                for c in range(8):
                    cx.tr(p_[:, c, :], h_[:, c * 128:(c + 1) * 128], IDB[:, :], [h_.b, IDB.b], [p_.b])
                cx.cp("act", g_[:, :, (i % 4) * 128:(i % 4 + 1) * 128], p_[:, :, :], [p_.b], [g_.b])
                if i % 4 == 3:
                    j = i // 4
                    cx.store(hTv[:, :, j * 512:(j + 1) * 512], g_[:, :, :], g_, hTd_b)
            S.barrier()
        cx.stack = root

        with ExitStack() as st:
          if "1" in parts:
            cx.stack = st
            W1 = cx.sb([128, 8, 1024], BF16, "W1")
            stg = cx.sb([128, 8, 512], F32, "stg", n=2)
            load_w_bf16(cx, W1, 0, w1d, 8, 1024, stg)
            RC = cx.sb([128, 258], F32, "RC")
            cx.load(RC[:, :], rcd, RC)
            WT, DQ, DKC, G128 = RC[:, 0:128], RC[:, 128:256], RC[:, 256:257], RC[:, 257:258]
            St = cx.sb([128, 256], F32, "St")
            Stb = cx.sb([128, 256], BF16, "Stb", n=2)
            cx.memset("dve", St[:, :], 0.0, [St.b])
            cx.memset("dve", Stb[0][:, :], 0.0, [Stb[0].b])
            hT = cx.sb([128, 8, 512], BF16, "hT", n=2)
            cs = cx.sb([128, 512], F32, "cs", n=2)
            sn = cx.sb([128, 512], F32, "sn", n=2)
            pA = cx.ps([128, 512], F32, "pA")
            pB = cx.ps([128, 512], F32, "pB")
            pVG = cx.ps([128, 512], F32, "pVG", n=2)
            pSO = cx.ps([128, 512], F32, "pSO")
            pU = cx.ps([128, 512], F32, "pU")
            pTr = cx.ps([128, 1024], BF16, "pTr")
            pTrK, pTrY = Buf("pTrK"), Buf("pTrY")
            pST, pO = Buf("pST"), Buf("pO")
            tA = cx.sb([128, 512], F32, "tA", n=2)
            tB = cx.sb([128, 512], F32, "tB", n=2)
            qT = cx.sb([128, 512], BF16, "qT", n=2)
            kT = cx.sb([128, 512], BF16, "kT", n=2)
            V = cx.sb([128, 256], BF16, "V", n=4)
            SG = cx.sb([128, 256], F32, "SG", n=4)
            SW = cx.sb([128, 128], BF16, "SW", n=2)
            qd = cx.sb([128, 128], BF16, "qd", n=2)
            kd = cx.sb([128, 128], BF16, "kd", n=2)
            stt_ = cx.sb([128, 6], F32, "bst", n=2)
            mv = cx.sb([128, 2], F32, "mv", n=2)
            rs = cx.sb([128, 1], F32, "rs", n=2)
            nb = cx.sb([128, 1], F32, "nb", n=2)
            ret = cx.sb([128, 256], F32, "ret", n=2)
            yb = cx.sb([128, 256], BF16, "yb", n=2)
            yTt = cx.sb([128, 2, 512], BF16, "yTt", n=2)

            def ld(j):
                cx.load(hT[j % 2][:, :, :], hTv[:, :, j * 512:(j + 1) * 512], hT[j % 2], rd=[hTd_b])
                cx.load(cs[j % 2][:, :], cosd[:, j * 512:(j + 1) * 512], cs[j % 2])
                cx.load(sn[j % 2][:, :], sind[:, j * 512:(j + 1) * 512], sn[j % 2])
            ld(0)
            blk = 0
            for j in range(NT):
                if j + 1 < NT:
                    ld(j + 1)
                h_, c_, s_ = hT[j % 2], cs[j % 2], sn[j % 2]
                for which, dst in ((0, qT[j % 2]), (1, kT[j % 2])):
                    for c in range(8):
                        cx.mm(pA[:, :], W1[:, c, which * 256:which * 256 + 128], h_[:, c, :], c == 0, c == 7, [W1.b, h_.b], [pA.b])
                    for c in range(8):
                        cx.mm(pB[:, :], W1[:, c, which * 256 + 128:which * 256 + 256], h_[:, c, :], c == 0, c == 7, [W1.b, h_.b], [pB.b])
                    a_, b_ = tA[which], tB[which]
                    cx.tt("dve", a_[:, :], pA[:, :], c_[:, :], ALU.mult, [pA.b, c_.b], [a_.b])
                    cx.tt("dve", b_[:, :], pB[:, :], s_[:, :], ALU.mult, [pB.b, s_.b], [b_.b])
                    cx.tt("pool", dst[:, :], a_[:, :], b_[:, :], ALU.add, [a_.b, b_.b], [dst.b])
                q_, k_ = qT[j % 2], kT[j % 2]
                yt_ = yTt[j % 2]
                for sub in range(4):
                    sl = slice(sub * 128, (sub + 1) * 128)
                    pv = pVG[sub % 2]
                    v_, sg_ = V[sub], SG[sub]
                    for c in range(8):
                        cx.mm(pv[:, :], h_[:, c, sl], W1[:, c, 512:1024], c == 0, c == 7, [W1.b, h_.b], [pv.b])
                    cx.cp("act", v_[:, :], pv[:, 0:256], [pv.b], [v_.b])
                    cx.act(sg_[:, :], pv[:, 256:512], AF.Silu, [pv.b], [sg_.b])
                    sw_, qd_, kd_ = SW[blk % 2], qd[blk % 2], kd[blk % 2]
                    cx.mm(pSO[:, 0:128], k_[:, sl], q_[:, sl], True, True, [k_.b, q_.b], [pST])
                    cx.tt("dve", sw_[:, :], pSO[:, 0:128], WT, ALU.mult, [pST, RC.b], [sw_.b])
                    cx.tt("pool", qd_[:, :], q_[:, sl], DQ, ALU.mult, [q_.b, RC.b], [qd_.b])
                    cx.mm(pSO[:, 128:384], sw_[:, :], v_[:, :], True, False, [sw_.b, v_.b], [pO])
                    cx.mm(pSO[:, 128:384], qd_[:, :], Stb[blk % 2][:, :], False, True, [qd_.b, Stb[blk % 2].b], [pO])
                    cx.tr(pTr[:, 0:128], k_[:, sl], IDB[:, :], [k_.b, IDB.b], [pTrK])
                    cx.act(kd_[:, :], pTr[:, 0:128], AF.Copy, [pTrK, RC.b], [kd_.b], scale=DKC)
                    cx.mm(pU[:, 0:256], kd_[:, :], v_[:, :], True, True, [kd_.b, v_.b], [pU.b])
                    cx.stt("dve", St[:, :], St[:, :], G128, pU[:, 0:256], ALU.mult, ALU.add, [St.b, pU.b, RC.b], [St.b])
                    cx.cp("pool", Stb[(blk + 1) % 2][:, :], St[:, :], [St.b], [Stb[(blk + 1) % 2].b])
                    b6, m2, r1, n1 = stt_[blk % 2], mv[blk % 2], rs[blk % 2], nb[blk % 2]
                    cx.S.op("dve", lambda e, o=b6[:, :], i=pSO[:, 128:384]: e.bn_stats(out=o, in_=i), [pO], [b6.b])
                    cx.S.op("dve", lambda e, o=m2[:, :], i=b6[:, :]: e.bn_aggr(out=o, in_=i), [b6.b], [m2.b])
                    cx.act(r1[:, 0:1], m2[:, 1:2], AF.Sqrt, [m2.b], [r1.b], bias=EPS)
                    cx.recip(r1[:, 0:1], r1[:, 0:1], [r1.b], [r1.b])
                    cx.stt("dve", n1[:, 0:1], m2[:, 0:1], -1.0, r1[:, 0:1], ALU.mult, ALU.mult, [m2.b, r1.b], [n1.b])
                    rt, y_ = ret[blk % 2], yb[blk % 2]
                    cx.act(rt[:, :], pSO[:, 128:384], AF.Identity, [pO, r1.b, n1.b], [rt.b], bias=n1[:, 0:1], scale=r1[:, 0:1])
                    cx.tt("pool", y_[:, :], rt[:, :], sg_[:, :], ALU.mult, [rt.b, sg_.b], [y_.b])
                    for f in range(2):
                        cx.tr(pTr[:, 128 + f * 128:256 + f * 128], y_[:, f * 128:(f + 1) * 128], IDB[:, :], [y_.b, IDB.b], [pTrY])
                    cx.cp("act", yt_[:, :, sl], pTr[:, 128:384].rearrange("p (f t) -> p f t", f=2), [pTrY], [yt_.b])
                    blk += 1
                cx.store(y_dst(0, 256, j).rearrange("(f p) t -> p f t", p=128), yt_[:, :, :], yt_, yT_b, is_output=True)
            S.barrier()
        cx.stack = root

        with ExitStack() as st:
          if "2" in parts:
            cx.stack = st
            W2 = cx.sb([128, 8, 512], BF16, "W2")
            stg = cx.sb([128, 8, 512], F32, "stg", n=2)
            load_w_bf16(cx, W2, 0, w2d, 8, 512, stg)
            WAX = cx.sb([128, 4, 128], BF16, "WAX")
            waxs = cx.sb([128, 4, 128], F32, "waxs")
            cx.load(waxs[:, :, :], lwd.rearrange("n i o -> i n o"), waxs)
            cx.cp("dve", WAX[:, :, :], waxs[:, :, :], [waxs.b], [WAX.b])
            LV = cx.sb([128, 2, 8], F32, "LV")
            cx.load(LV[:, :, :], lvd, LV)
            EL = cx.sb([128, 2], F32, "EL")
            SPL = cx.sb([128, 2], F32, "SPL")
            M8 = cx.sb([128, 2], F32, "M8")
            M16 = cx.sb([128, 2], F32, "M16")
            cx.act(EL[:, :], LV[:, :, 7], AF.Exp, [LV.b], [EL.b], scale=-1.0)
            cx.act(SPL[:, :], EL[:, :], AF.Ln, [EL.b], [SPL.b], bias=1.0)
            cx.ts("dve", M8[:, :], SPL[:, :], -8.0, None, ALU.mult, None, [SPL.b], [M8.b])
            cx.ts("dve", M16[:, :], SPL[:, :], -16.0, None, ALU.mult, None, [SPL.b], [M16.b])
            H0 = cx.sb([128, 1], F32, "H0")
            cx.memset("dve", H0[:, :], 0.0, [H0.b])
            hT = cx.sb([128, 8, 512], BF16, "hT", n=2)
            LX = [cx.sb([128, 515], F32, "LX", n=2) for _ in range(2)]
            for b in range(2):
                cx.memset("pool", LX[b][0][:, 0:3], 0.0, [LX[b][0].b])
            pX = cx.ps([128, 512], F32, "pX", n=2)
            pR = cx.ps([128, 512], F32, "pR", n=2)
            pI = cx.ps([128, 512], F32, "pI", n=2)
            pY = cx.ps([128, 512], F32, "pY", n=2)
            u = cx.sb([128, 512], F32, "u", n=2)
            ub = cx.sb([128, 512], BF16, "ub", n=2)
            r_ = cx.sb([128, 512], F32, "r", n=2)
            ig = cx.sb([128, 512], F32, "ig", n=2)
            a_ = cx.sb([128, 512], F32, "a", n=2)
            a2 = cx.sb([128, 512], F32, "a2", n=2)
            sq = cx.sb([128, 512], F32, "sq", n=2)
            uu = cx.sb([128, 512], F32, "uu", n=2)
            hs = [cx.sb([128, 512], F32, "hs", n=2) for _ in range(2)]
            lyS = cx.sb([128, 512], F32, "lyS", n=2)
            x2 = cx.sb([128, 512], F32, "x2", n=2)
            inn = cx.sb([128, 512], F32, "inn", n=2)
            sgm = cx.sb([128, 512], F32, "sgm", n=2)
            yL = cx.sb([128, 2, 512], BF16, "yL", n=2)
            cx.load(hT[0][:, :, :], hTv[:, :, 0:512], hT[0], rd=[hTd_b])
            it = 0
            for j in range(NT):
                if j + 1 < NT:
                    cx.load(hT[(j + 1) % 2][:, :, :], hTv[:, :, (j + 1) * 512:(j + 2) * 512], hT[(j + 1) % 2], rd=[hTd_b])
                h_ = hT[j % 2]
                yl = yL[j % 2]
                for b in range(2):
                    k = it % 2
                    it += 1
                    lx, lxn = LX[b][j % 2], LX[b][(j + 1) % 2]
                    px, pr, pi, py = pX[k], pR[k], pI[k], pY[k]
                    for c in range(8):
                        cx.mm(px[:, :], W2[:, c, b * 128:(b + 1) * 128], h_[:, c, :], c == 0, c == 7, [W2.b, h_.b], [px.b])
                    for c in range(8):
                        cx.mm(py[:, :], W2[:, c, 256 + b * 128:256 + (b + 1) * 128], h_[:, c, :], c == 0, c == 7, [W2.b, h_.b], [py.b])
                    cx.cp("act", lx[:, 3:515], px[:, :], [px.b], [lx.b])
                    cx.cp("pool", lxn[:, 0:3], lx[:, 512:515], [lx.b], [lxn.b])
                    u_ = u[k]
                    cx.ts("dve", u_[:, :], lx[:, 0:512], LV[:, b, 0:1], LV[:, b, 4:5], ALU.mult, ALU.add, [lx.b, LV.b], [u_.b])
                    for i in range(1, 4):
                        cx.stt("dve", u_[:, :], lx[:, i:i + 512], LV[:, b, i:i + 1], u_[:, :], ALU.mult, ALU.add, [lx.b, LV.b, u_.b], [u_.b])
                    cx.cp("pool", ub[k][:, :], u_[:, :], [u_.b], [ub[k].b])
                    cx.mm(pr[:, :], WAX[:, b, :], ub[k][:, :], True, True, [WAX.b, ub[k].b], [pr.b])
                    cx.mm(pi[:, :], WAX[:, 2 + b, :], ub[k][:, :], True, True, [WAX.b, ub[k].b], [pi.b])
                    cx.act(r_[k][:, :], pr[:, :], AF.Sigmoid, [pr.b, LV.b], [r_[k].b], bias=LV[:, b, 5:6])
                    cx.act(ig[k][:, :], pi[:, :], AF.Sigmoid, [pi.b, LV.b], [ig[k].b], bias=LV[:, b, 6:7])
                    cx.act(a_[k][:, :], r_[k][:, :], AF.Exp, [r_[k].b, M8.b], [a_[k].b], scale=M8[:, b:b + 1])
                    cx.act(a2[k][:, :], r_[k][:, :], AF.Exp, [r_[k].b, M16.b], [a2[k].b], scale=M16[:, b:b + 1])
                    cx.act(sq[k][:, :], a2[k][:, :], AF.Sqrt, [a2[k].b], [sq[k].b], bias=1.0, scale=-1.0)
                    cx.tt("pool", uu[k][:, :], ig[k][:, :], u_[:, :], ALU.mult, [ig[k].b, u_.b], [uu[k].b])
                    cx.tt("dve", uu[k][:, :], uu[k][:, :], sq[k][:, :], ALU.mult, [uu[k].b, sq[k].b], [uu[k].b])
                    hcur = hs[b][j % 2]
                    if j == 0:
                        init, ib = H0[:, 0:1], H0.b
                    else:
                        init, ib = hs[b][(j - 1) % 2][:, 511:512], hs[b][(j - 1) % 2].b
                    cx.S.op("dve", lambda e, o=hcur[:, :], d0=a_[k][:, :], d1=uu[k][:, :], i0=init: e.tensor_tensor_scan(
                        out=o, data0=d0, data1=d1, initial=i0, op0=ALU.mult, op1=ALU.add), [a_[k].b, uu[k].b, ib], [hcur.b])
                    cx.cp("act", lyS[k][:, :], py[:, :], [py.b], [lyS[k].b])
                    cx.act(x2[k][:, :], py[:, :], AF.Square, [py.b], [x2[k].b])
                    cx.ts("dve", inn[k][:, :], x2[k][:, :], 0.044715, 1.0, ALU.mult, ALU.add, [x2[k].b], [inn[k].b])
                    cx.tt("pool", inn[k][:, :], inn[k][:, :], lyS[k][:, :], ALU.mult, [inn[k].b, lyS[k].b], [inn[k].b])
                    cx.act(sgm[k][:, :], inn[k][:, :], AF.Sigmoid, [inn[k].b], [sgm[k].b], scale=1.5957691216057308)
                    cx.tt("pool", sgm[k][:, :], sgm[k][:, :], lyS[k][:, :], ALU.mult, [sgm[k].b, lyS[k].b], [sgm[k].b])
                    cx.tt("dve", yl[:, b, :], sgm[k][:, :], hcur[:, :], ALU.mult, [sgm[k].b, hcur.b], [yl.b])
                cx.store(y_dst(256, 256, j).rearrange("(f p) t -> p f t", p=128), yl[:, :, :], yl, yT_b, is_output=True)
            S.barrier()
        cx.stack = root

        with ExitStack() as st:
          if "3" in parts:
            cx.stack = st
            W3 = cx.sb([128, 8, 388], BF16, "W3")
            stg = cx.sb([128, 8, 512], F32, "stg", n=2)
            QT = cx.sb([128, SL], BF16, "QT")
            KT = cx.sb([128, SL], BF16, "KT")
            VA = cx.sb([128, NB, 132], BF16, "VA")
            cx.memset("pool", VA[:, :, :], 1.0, [VA.b])
            BFL = cx.sb([128, 2], F32, "BFL")
            cx.load(BFL[:, :], bfd.partition_broadcast(128), BFL)
            ZF = cx.sb([128, NB], F32, "ZF")
            EZ = cx.sb([128, NB], F32, "EZ")
            LF = cx.sb([128, NB], F32, "LF")
            CK = cx.sb([128, NB], F32, "CK")
            NCK = cx.sb([128, NB], F32, "NCK")
            TOT = cx.sb([128, NB], F32, "TOT")
            INC = cx.sb([128, NB], F32, "INC")
            e1 = cx.sb([128, 1], F32, "e1", n=2)
            l1 = cx.sb([128, 1], F32, "l1", n=2)
            hT = cx.sb([128, 8, 512], BF16, "hT", n=2)
            pS = cx.ps([128, 512], F32, "pS", n=2)
            pO = cx.ps([128, 512], F32, "pO", n=4)
            pM = cx.ps([128, 512], F32, "pM")
            pTr = cx.ps([128, 1024], BF16, "pTr")
            SP = cx.sb([128, 512], F32, "SP", n=3)
            PT = cx.sb([128, 512], BF16, "PT", n=3)
            CQ = cx.sb([128, 512], F32, "CQ", n=2)
            ckb = cx.sb([128, 128], F32, "ckb", n=2)
            rl = cx.sb([128, 1], F32, "rl", n=2)
            yb = cx.sb([128, 128], BF16, "yb", n=2)
            yTt = cx.sb([128, 512], BF16, "yTt", n=2)
            SCALE = 128.0 ** -0.5
            for hh in range(2):
                load_w_bf16(cx, W3, 0, w3d[hh], 8, 385, stg)
                cx.load(hT[0][:, :, :], hTv[:, :, 0:512], hT[0], rd=[hTd_b])
                for j in range(NT):
                    if j + 1 < NT:
                        cx.load(hT[(j + 1) % 2][:, :, :], hTv[:, :, (j + 1) * 512:(j + 2) * 512], hT[(j + 1) % 2], rd=[hTd_b])
                    h_ = hT[j % 2]
                    tsl = slice(j * 512, (j + 1) * 512)
                    for c in range(8):
                        cx.mm(pS[0][:, :], W3[:, c, 0:128], h_[:, c, :], c == 0, c == 7, [W3.b, h_.b], [pS[0].b])
                    cx.cp("dve", QT[:, tsl], pS[0][:, :], [pS[0].b], [QT.b])
                    for c in range(8):
                        cx.mm(pS[1][:, :], W3[:, c, 128:256], h_[:, c, :], c == 0, c == 7, [W3.b, h_.b], [pS[1].b])
                    cx.cp("act", KT[:, tsl], pS[1][:, :], [pS[1].b], [KT.b])
                    for sub in range(4):
                        if "v" in os.environ.get("K_SKIP", ""):
                            break
                        blk = j * 4 + sub
                        p = pO[sub % 2]
                        for c in range(8):
                            cx.mm(p[:, 0:129], h_[:, c, sub * 128:(sub + 1) * 128], W3[:, c, 256:385], c == 0, c == 7, [W3.b, h_.b], [p.b])
                        cx.cp("dve", VA[:, blk, 0:128], p[:, 0:128], [p.b], [VA.b])
                        cx.ts("dve", ZF[:, blk:blk + 1], p[:, 128:129], BFL[:, hh:hh + 1], None, ALU.add, None, [p.b, BFL.b], [ZF.b])
                cx.act(EZ[:, :], ZF[:, :], AF.Exp, [ZF.b], [EZ.b], scale=-1.0)
                cx.act(EZ[:, :], EZ[:, :], AF.Ln, [EZ.b], [EZ.b], bias=1.0)
                cx.ts("dve", LF[:, :], EZ[:, :], -1.0, None, ALU.mult, None, [EZ.b], [LF.b])
                if os.environ.get("K_B3STOP") == "1":
                    continue
                cx.mm(pM[:, 0:NB], TRI, LF[:, :], True, True, [CON.b, LF.b], [pM.b])
                cx.mm(pS[0][:, 0:NB], ONES, LF[:, :], True, True, [CON.b, LF.b], [pS[0].b])
                cx.cp("act", TOT[:, :], pS[0][:, 0:NB], [pS[0].b], [TOT.b])
                cx.S.op("dve", lambda e: e.tensor_tensor_scan(out=INC[:, :], data0=ONESB[:, 0:NB], data1=TOT[:, :], initial=0.0,
                                                              op0=ALU.mult, op1=ALU.add), [ONESB.b, TOT.b], [INC.b])
                cx.tt("dve", INC[:, :], INC[:, :], TOT[:, :], ALU.subtract, [INC.b, TOT.b], [INC.b])
                cx.tt("dve", CK[:, :], pM[:, 0:NB], INC[:, :], ALU.add, [pM.b, INC.b], [CK.b])
                cx.ts("dve", NCK[:, :], CK[:, :], -1.0, None, ALU.mult, None, [CK.b], [NCK.b])
                S.barrier()
                if os.environ.get("K_B3STOP") == "2":
                    continue
                pairs = [(j, kb) for j in range(NT) for kb in range(4 * j + 4)]

                def qk(i):
                    j, kb = pairs[i]
                    lo = max(0, kb - 4 * j) * 128
                    p = pS[i % 2]
                    cx.mm(p[:, lo:512], KT[:, kb * 128:(kb + 1) * 128], QT[:, j * 512 + lo:(j + 1) * 512], True, True, [KT.b, QT.b], [p.b])

                def cqrep(j):
                    cq = CQ[j % 2]
                    for sub in range(4):
                        cb_ = ckb[sub % 2]
                        cx.ts("dve", cb_[:, :], ONESB[:, 0:128], CK[:, 4 * j + sub:4 * j + sub + 1], None, ALU.mult, None, [ONESB.b, CK.b], [cb_.b])
                        cx.mm(pM[:, sub * 128:(sub + 1) * 128], cb_[:, :], IDF, True, True, [cb_.b, CON.b], [pM.b])
                    cx.cp("dve", cq[:, :], pM[:, :], [pM.b], [cq.b])
                cqrep(0)
                qk(0)
                for i, (j, kb) in enumerate(pairs):
                    if i + 1 < len(pairs):
                        if pairs[i + 1][0] != j:
                            cqrep(j + 1)
                        qk(i + 1)
                    sub_lo = max(0, kb - 4 * j)
                    lo = sub_lo * 128
                    p, sp, pt, cq = pS[i % 2], SP[i % 3], PT[i % 3], CQ[j % 2]
                    cx.stt("dve", sp[:, lo:512], p[:, lo:512], SCALE, cq[:, lo:512], ALU.mult, ALU.add, [p.b, cq.b], [sp.b])
                    if kb >= 4 * j:
                        cx.tt("pool", sp[:, lo:lo + 128], sp[:, lo:lo + 128], NEGM[:, :], ALU.add, [sp.b, NEGM.b], [sp.b])
                    cx.act(pt[:, lo:512], sp[:, lo:512], AF.Exp, [sp.b, NCK.b], [pt.b], bias=NCK[:, kb:kb + 1])
                    for sub in range(sub_lo, 4):
                        cx.mm(pO[sub][:, 0:129], pt[:, sub * 128:(sub + 1) * 128], VA[:, kb, 0:129], kb == 0, kb == 4 * j + sub, [pt.b, VA.b], [pO[sub].b])
                    if kb >= 4 * j:
                        sub = kb - 4 * j
                        o = pO[sub]
                        r1, y_ = rl[sub % 2], yb[sub % 2]
                        yt_ = yTt[j % 2]
                        cx.recip(r1[:, :], o[:, 128:129], [o.b], [r1.b])
                        cx.ts("dve", y_[:, :], o[:, 0:128], r1[:, 0:1], None, ALU.mult, None, [o.b, r1.b], [y_.b])
                        cx.tr(pTr[:, sub * 128:(sub + 1) * 128], y_[:, :], IDB[:, :], [y_.b, IDB.b], [pTr.b])
                        cx.cp("dve", yt_[:, sub * 128:(sub + 1) * 128], pTr[:, sub * 128:(sub + 1) * 128], [pTr.b], [yt_.b])
                        if sub == 3:
                            cx.store(y_dst(512 + hh * 128, 128, j), yt_[:, :], yt_, yT_b, is_output=True)
                S.barrier()
        cx.stack = root
    cx.stack = base
    S.barrier()
    S.lane_release(lmark)


def build_B(SL, parts="0123"):
    nc = bass.Bass("TRN2", target_bir_lowering=False)
    A = declare_B(nc, SL)
    A["cosT"] = nc.dram_tensor("cosT", [128, SL], F32, kind="ExternalInput").ap()
    A["sinT"] = nc.dram_tensor("sinT", [128, SL], F32, kind="ExternalInput").ap()
    cond = nc.dram_tensor("consts", [128, 512], F32, kind="ExternalInput").ap()
    with ExitStack() as root:
        cx = Cx(nc, root)
        K = make_consts(cx, cond)
        emit_B(cx, SL, A, K, parts)
        cx.S.emit()
        build_B.stats = cx.S.stats()
    return nc


OFF = {"rq": 0, "rk": 512, "rv": 1024, "rg": 2048, "lx": 3072, "ly": 4096, "fq": 5120, "fk": 6144, "fv": 7168,
       "ff": 8192, "gates": 8200}


def _consts():
    idx = np.arange(128)
    tri = (idx[None, :] >= idx[:, None]).astype(np.float32)
    ones = np.ones((128, 128), np.float32)
    sel = np.zeros((128, 128), np.float32)
    sel[63, :] = 1.0
    ident = np.eye(128, dtype=np.float32)
    return np.ascontiguousarray(np.concatenate([tri, ones, sel, ident], axis=1))


def _rope_tables(SL):
    half = 64
    inv = (10000.0 ** (-np.arange(half, dtype=np.float32) / np.float32(half))).astype(np.float32)
    ang = (np.arange(SL, dtype=np.float32)[:, None] * inv[None, :]).astype(np.float32)
    cos = np.cos(ang.astype(np.float64)).astype(np.float32).T
    sin = np.sin(ang.astype(np.float64)).astype(np.float32).T
    cosT = np.ascontiguousarray(np.concatenate([cos, cos], axis=0))
    sinT = np.ascontiguousarray(np.concatenate([-sin, sin], axis=0))
    return cosT, sinT


def _ret_consts(g):
    gam = 1.0 - 2.0 ** (-5.0 - g)
    idx = np.arange(128)
    ch = idx // 64
    wt = (gam ** np.abs(idx[None, :] - idx[:, None])) * (ch[:, None] <= ch[None, :]) * (128.0 ** -0.5)
    dq = np.tile((gam ** (idx + 1.0))[None, :], (128, 1))
    dk = (gam ** (127.0 - idx)) * (128.0 ** -0.5)
    rc = np.concatenate([wt, dq, dk[:, None], np.full((128, 1), gam ** 128.0)], axis=1)
    return np.ascontiguousarray(rc.astype(np.float32))


def prep_B(inp, l, b, g, SL, tables, xcur=None):
    w_in = inp["w_in"][l]
    sw = (np.arange(128) + 64) % 128
    wq = w_in[:, OFF["rq"] + g * 128:OFF["rq"] + (g + 1) * 128]
    wk = w_in[:, OFF["rk"] + g * 128:OFF["rk"] + (g + 1) * 128]
    w1 = np.concatenate([wq, wq[:, sw], wk, wk[:, sw],
                         w_in[:, OFF["rv"] + g * 256:OFF["rv"] + (g + 1) * 256],
                         w_in[:, OFF["rg"] + g * 256:OFF["rg"] + (g + 1) * 256]], axis=1)
    w2 = np.concatenate([w_in[:, OFF["lx"] + g * 256:OFF["lx"] + (g + 1) * 256],
                         w_in[:, OFF["ly"] + g * 256:OFF["ly"] + (g + 1) * 256]], axis=1)
    w3 = np.stack([np.concatenate([w_in[:, OFF["fq"] + h * 128:OFF["fq"] + (h + 1) * 128],
                                   w_in[:, OFF["fk"] + h * 128:OFF["fk"] + (h + 1) * 128],
                                   w_in[:, OFF["fv"] + h * 128:OFF["fv"] + (h + 1) * 128],
                                   w_in[:, OFF["ff"] + h:OFF["ff"] + h + 1]], axis=1) for h in (2 * g, 2 * g + 1)])
    ch = slice(g * 256, (g + 1) * 256)
    lv = np.stack([inp["conv_w"][l][0, ch], inp["conv_w"][l][1, ch], inp["conv_w"][l][2, ch], inp["conv_w"][l][3, ch],
                   inp["conv_b"][l][ch], inp["lru_b_a"][l][ch], inp["lru_b_x"][l][ch], inp["lru_lambda"][l][ch]], axis=-1)
    lv = lv.reshape(2, 128, 8).transpose(1, 0, 2)
    lw = np.stack([inp["lru_w_a"][l][2 * g], inp["lru_w_a"][l][2 * g + 1], inp["lru_w_x"][l][2 * g], inp["lru_w_x"][l][2 * g + 1]])
    cosT, sinT, con = tables
    f = np.ascontiguousarray
    return {"xb": f((inp["x"] if xcur is None else xcur)[b, :SL]), "cvec": f(inp["c"][b]), "adaw": f(inp["ada_w"][l][:, 0:2048]),
            "adab": f(inp["ada_b"][l][0:2048]), "gpre": f(inp["norm_pre_mix"][l]), "w1": f(w1), "w2": f(w2), "w3": f(w3),
            "cosT": cosT, "sinT": sinT, "rc": _ret_consts(g), "lruv": f(lv.astype(np.float32)), "lruw": f(lw),
            "bf": f(inp["fox_b_f"][l][2 * g:2 * g + 2]), "consts": con}


def post_norm_residual(cx, PZ, xin, GT, modb, out, junk, ssa, ssb, rs, tt_):
    cx.act(junk[:, :], PZ[0][:, :], AF.Square, [PZ[0].b], [junk.b, ssa.b], accum=ssa[:, 0:1])
    cx.act(junk[:, :], PZ[1][:, :], AF.Square, [PZ[1].b], [junk.b, ssb.b], accum=ssb[:, 0:1])
    cx.tt("dve", ssa[:, 0:1], ssa[:, 0:1], ssb[:, 0:1], ALU.add, [ssa.b, ssb.b], [ssa.b])
    cx.act(rs[:, 0:1], ssa[:, 0:1], AF.Sqrt, [ssa.b], [rs.b], bias=EPS, scale=1.0 / D)
    cx.recip(rs[:, 0:1], rs[:, 0:1], [rs.b], [rs.b])
    for n in range(2):
        cx.stt("dve", tt_[:, n * 512:(n + 1) * 512], PZ[n][:, :], rs[:, 0:1], GT[:, n * 512:(n + 1) * 512], ALU.mult, ALU.mult,
               [PZ[n].b, rs.b, modb], [tt_.b])
    cx.tt("pool", out[:, :], xin[:, :], tt_[:, :], ALU.add, [xin.b, tt_.b], [out.b])


def declare_C(nc, T, sfx="", out_internal=False):
    def din(name, shape, dt=F32):
        DECL.append((name + sfx, tuple(shape)))
        return nc.dram_tensor(name + sfx, list(shape), dt, kind="ExternalInput").ap()
    A = {}
    A["cvec"] = din("cvec", [D])
    A["adaw"] = din("adaw", [D, 6144])
    A["adab"] = din("adab", [6144])
    A["gains"] = din("gains", [4, D])
    A["wg"] = din("wg", [D, 3072])
    A["wo"] = din("wo", [3072, D])
    A["wout"] = din("wout", [D, D])
    A["w1"] = din("w1", [D, 4096])
    A["w2"] = din("w2", [4096, D])
    A["xo"] = nc.dram_tensor("xo" + sfx, [T, D], F32, kind=("Internal" if out_internal else "ExternalOutput")).ap()
    A["mTd"] = nc.dram_tensor("mTd" + sfx, [D, T], BF16, kind="Internal").ap()
    A["x1d"] = nc.dram_tensor("x1d" + sfx, [T, D], F32, kind="Internal").ap()
    A["h2Td"] = nc.dram_tensor("h2Td" + sfx, [D, T], BF16, kind="Internal").ap()
    A["f1Td"] = nc.dram_tensor("f1Td" + sfx, [4096, T], BF16, kind="Internal").ap()
    return A


def emit_C(cx, T, A, K, parts="1234"):
    nc, S, base = cx.nc, cx.S, cx.stack
    cvec, adaw, adab, gains = A["cvec"], A["adaw"], A["adab"], A["gains"]
    wgd, wod, woutd, w1d, w2d = A["wg"], A["wo"], A["wout"], A["w1"], A["w2"]
    xo, mTd, x1d, h2Td, f1Td = A["xo"], A["mTd"], A["x1d"], A["h2Td"], A["f1Td"]
    x_tile, y_tiles = A["x_tile"], A["y_tiles"]
    x_rd, y_rd = list(A.get("x_rd", ())), list(A.get("y_rd", ()))
    CON, IDB = K["CON"], K["IDB"]
    mTd_b, x1d_b, h2Td_b, f1Td_b = Buf("mTd"), Buf("x1d"), Buf("h2Td"), Buf("f1Td")
    xo_b = A.setdefault("xo_b", Buf("xo"))
    mTv = mTd.rearrange("(c p) t -> p c t", p=128)
    h2Tv = h2Td.rearrange("(c p) t -> p c t", p=128)
    f1Tv = f1Td.rearrange("(c p) t -> p c t", p=128)
    NB = T // 128
    NT = T // 512
    lmark = S.lane_mark()
    with ExitStack() as root:
        cx.stack = root
        MOD = cx.sb([128, 6144], F32, "MOD")
        with ExitStack() as st:
            cx.stack = st
            stg = cx.sb([128, 8, 512], F32, "stg", n=2)
            pb = cx.ps([128, 512], F32, "pmod", n=2)
            GB = cx.sb([128, D], F32, "GB", n=2)
            compute_mod(cx, cvec, adaw, adab, 0, 6144, MOD, CON, stg, pb)
            for gi, (col, plus1) in enumerate(((1024, True), (2048, False), (4096, True), (5120, False))):
                gb = GB[gi % 2]
                cx.load(gb[:, :], gains[gi].partition_broadcast(128), gb)
                if plus1:
                    cx.stt("dve", MOD[:, col:col + 1024], MOD[:, col:col + 1024], 1.0, gb[:, :], ALU.add, ALU.mult, [MOD.b, gb.b], [MOD.b])
                else:
                    cx.tt("dve", MOD[:, col:col + 1024], MOD[:, col:col + 1024], gb[:, :], ALU.mult, [MOD.b, gb.b], [MOD.b])
            S.barrier()
        cx.stack = root
        SH_m, G2_m, GT_m = MOD[:, 0:1024], MOD[:, 1024:2048], MOD[:, 2048:3072]
        SH_f, G2_f, GT_f = MOD[:, 3072:4096], MOD[:, 4096:5120], MOD[:, 5120:6144]

        with ExitStack() as st:
          if "1" in parts:
            cx.stack = st
            WG = cx.sb([128, 8, 3072], BF16, "WG")
            WO = cx.sb([128, 24, 1024], BF16, "WO")
            with ExitStack() as st2:
                cx.stack = st2
                stg = cx.sb([128, 8, 512], F32, "stg", n=2)
                load_w_bf16(cx, WG, 0, wgd, 8, 3072, stg)
                load_w_bf16(cx, WO, 0, wod, 24, 1024, stg)
                S.barrier()
            cx.stack = st
            xt = cx.sb([128, D], F32, "xt", n=2)
            ss = cx.sb([128, 1], F32, "ss", n=2)
            rs = cx.sb([128, 1], F32, "rs", n=2)
            t1 = cx.sb([128, D], F32, "t1", n=2)
            hb = cx.sb([128, D], BF16, "hb", n=2)
            hTt = cx.sb([128, 8, 128], BF16, "hTt", n=2)
            yTt = cx.sb([128, 24, 128], BF16, "yTt", n=2)
            gS = cx.sb([128, 512], F32, "gS", n=2)
            mg = cx.sb([128, D], F32, "mg", n=2)
            tmp = cx.sb([128, 512], F32, "tmp", n=2)
            mgb = cx.sb([128, D], BF16, "mgb", n=2)
            mTt = cx.sb([128, 8, 512], BF16, "mTt", n=2)
            pG = cx.ps([128, 512], F32, "pG", n=2)
            pM = cx.ps([128, 512], F32, "pM", n=2)
            pTh = cx.ps([128, 8, 128], BF16, "pTh")
            pTm = cx.ps([128, 8, 128], BF16, "pTm")

            def ld(i):
                cx.load(xt[i % 2][:, :], x_tile(i), xt[i % 2], rd=x_rd)
                for c0, ncn, src in y_tiles(i):
                    cx.load(yTt[i % 2][:, c0:c0 + ncn, :], src, yTt[i % 2], rd=y_rd)
            ld(0)
            k = 0
            for i in range(NB):
                if i + 1 < NB:
                    ld(i + 1)
                x_, h_, ht, yt_, m_, mb_ = xt[i % 2], hb[i % 2], hTt[i % 2], yTt[i % 2], mg[i % 2], mgb[i % 2]
                norm_mod_tile(cx, x_, G2_m, SH_m, MOD.b, h_, ss[i % 2], rs[i % 2], t1[i % 2], h_)
                for c in range(8):
                    cx.tr(pTh[:, c, :], h_[:, c * 128:(c + 1) * 128], IDB[:, :], [h_.b, IDB.b], [pTh.b])
                cx.cp("act", ht[:, :, :], pTh[:, :, :], [pTh.b], [ht.b])
                for br in range(3):
                    for n in range(2):
                        pg, pm, g_, tm = pG[k % 2], pM[k % 2], gS[k % 2], tmp[k % 2]
                        k += 1
                        for c in range(8):
                            cx.mm(pg[:, :], ht[:, c, :], WG[:, c, br * 1024 + n * 512:br * 1024 + (n + 1) * 512], c == 0, c == 7, [ht.b, WG.b], [pg.b])
                        cx.act(g_[:, :], pg[:, :], AF.Sigmoid, [pg.b], [g_.b])
                        for c in range(8):
                            cx.mm(pm[:, :], yt_[:, br * 8 + c, :], WO[:, br * 8 + c, n * 512:(n + 1) * 512], c == 0, c == 7, [yt_.b, WO.b], [pm.b])
                        msl = m_[:, n * 512:(n + 1) * 512]
                        if br == 0:
                            cx.tt("dve", msl, g_[:, :], pm[:, :], ALU.mult, [g_.b, pm.b], [m_.b])
                        else:
                            cx.tt("dve", tm[:, :], g_[:, :], pm[:, :], ALU.mult, [g_.b, pm.b], [tm.b])
                            cx.tt("pool", msl, msl, tm[:, :], ALU.add, [m_.b, tm.b], [m_.b])
                cx.cp("pool", mb_[:, :], m_[:, :], [m_.b], [mb_.b])
                for c in range(8):
                    cx.tr(pTm[:, c, :], mb_[:, c * 128:(c + 1) * 128], IDB[:, :], [mb_.b, IDB.b], [pTm.b])
                mt = mTt[(i // 4) % 2]
                cx.cp("act", mt[:, :, (i % 4) * 128:(i % 4 + 1) * 128], pTm[:, :, :], [pTm.b], [mt.b])
                if i % 4 == 3:
                    j = i // 4
                    cx.store(mTv[:, :, j * 512:(j + 1) * 512], mt[:, :, :], mt, mTd_b)
            S.barrier()
        cx.stack = root

        with ExitStack() as st:
          if "2" in parts:
            cx.stack = st
            WOUT = cx.sb([128, 8, 1024], BF16, "WOUT")
            with ExitStack() as st2:
                cx.stack = st2
                stg = cx.sb([128, 8, 512], F32, "stg", n=2)
                load_w_bf16(cx, WOUT, 0, woutd, 8, 1024, stg)
                S.barrier()
            cx.stack = st
            xt = cx.sb([128, D], F32, "xt", n=2)
            mTt = cx.sb([128, 8, 512], BF16, "mTt", n=2)
            pZ = [cx.ps([128, 512], F32, "pZ", n=2) for _ in range(2)]
            pTh = cx.ps([128, 8, 128], BF16, "pTh")
            junk = cx.sb([128, 512], BF16, "junk")
            ssa = cx.sb([128, 1], F32, "ssa", n=2)
            ssb = cx.sb([128, 1], F32, "ssb", n=2)
            rs = cx.sb([128, 1], F32, "rs", n=2)
            tt_ = cx.sb([128, D], F32, "tt", n=2)
            x1t = cx.sb([128, D], F32, "x1t", n=2)
            ss2 = cx.sb([128, 1], F32, "ss2", n=2)
            rs2 = cx.sb([128, 1], F32, "rs2", n=2)
            t1 = cx.sb([128, D], F32, "t1", n=2)
            hb = cx.sb([128, D], BF16, "hb", n=2)
            h2Tt = cx.sb([128, 8, 512], BF16, "h2Tt", n=2)

            def ld(i):
                cx.load(xt[i % 2][:, :], x_tile(i), xt[i % 2], rd=x_rd)
                if i % 4 == 0:
                    j = i // 4
                    cx.load(mTt[j % 2][:, :, :], mTv[:, :, j * 512:(j + 1) * 512], mTt[j % 2], rd=[mTd_b])
            ld(0)
            for i in range(NB):
                if i + 1 < NB:
                    ld(i + 1)
                x_, mt, z, x1 = xt[i % 2], mTt[(i // 4) % 2], pZ[i % 2], x1t[i % 2]
                sub = i % 4
                for n in range(2):
                    for c in range(8):
                        cx.mm(z[n][:, :], mt[:, c, sub * 128:(sub + 1) * 128], WOUT[:, c, n * 512:(n + 1) * 512], c == 0, c == 7, [mt.b, WOUT.b], [z[n].b])
                post_norm_residual(cx, z, x_, GT_m, MOD.b, x1, junk, ssa[i % 2], ssb[i % 2], rs[i % 2], tt_[i % 2])
                cx.store(x1d[i * 128:(i + 1) * 128, :], x1[:, :], x1, x1d_b)
                h_ = hb[i % 2]
                norm_mod_tile(cx, x1, G2_f, SH_f, MOD.b, h_, ss2[i % 2], rs2[i % 2], t1[i % 2], h_)
                for c in range(8):
                    cx.tr(pTh[:, c, :], h_[:, c * 128:(c + 1) * 128], IDB[:, :], [h_.b, IDB.b], [pTh.b])
                ht = h2Tt[(i // 4) % 2]
                cx.cp("act", ht[:, :, sub * 128:(sub + 1) * 128], pTh[:, :, :], [pTh.b], [ht.b])
                if sub == 3:
                    j = i // 4
                    cx.store(h2Tv[:, :, j * 512:(j + 1) * 512], ht[:, :, :], ht, h2Td_b)
            S.barrier()
        cx.stack = root

        with ExitStack() as st:
          if "3" in parts:
            cx.stack = st
            W1 = cx.sb([128, 8, 4096], BF16, "W1")
            with ExitStack() as st2:
                cx.stack = st2
                stg = cx.sb([128, 8, 512], F32, "stg", n=2)
                load_w_bf16(cx, W1, 0, w1d, 8, 4096, stg)
                S.barrier()
            cx.stack = st
            h2Tt = cx.sb([128, 8, 512], BF16, "h2Tt", n=2)
            f1Tt = cx.sb([128, 32, 512], BF16, "f1Tt", n=2)
            rS = cx.sb([128, 512], F32, "rS", n=3)
            pF = cx.ps([128, 512], F32, "pF", n=4)
            cx.load(h2Tt[0][:, :, :], h2Tv[:, :, 0:512], h2Tt[0], rd=[h2Td_b])
            k = 0
            for j in range(NT):
                if j + 1 < NT:
                    cx.load(h2Tt[(j + 1) % 2][:, :, :], h2Tv[:, :, (j + 1) * 512:(j + 2) * 512], h2Tt[(j + 1) % 2], rd=[h2Td_b])
                ht, ft = h2Tt[j % 2], f1Tt[j % 2]
                for fc in range(32):
                    p, r = pF[k % 4], rS[k % 3]
                    k += 1
                    for c in range(8):
                        cx.mm(p[:, :], W1[:, c, fc * 128:(fc + 1) * 128], ht[:, c, :], c == 0, c == 7, [W1.b, ht.b], [p.b])
                    cx.act(r[:, :], p[:, :], AF.Relu, [p.b], [r.b])
                    cx.tt("pool" if fc % 2 else "dve", ft[:, fc, :], r[:, :], r[:, :], ALU.mult, [r.b], [ft.b])
                cx.store(f1Tv[:, :, j * 512:(j + 1) * 512], ft[:, :, :], ft, f1Td_b)
            S.barrier()
        cx.stack = root

        with ExitStack() as st:
          if "4" in parts:
            cx.stack = st
            W2 = cx.sb([128, 32, 1024], BF16, "W2")
            with ExitStack() as st2:
                cx.stack = st2
                stg = cx.sb([128, 8, 512], F32, "stg", n=2)
                load_w_bf16(cx, W2, 0, w2d, 32, 1024, stg)
                S.barrier()
            cx.stack = st
            f1Tt = cx.sb([128, 32, 512], BF16, "f1Tt", n=2)
            x1t = cx.sb([128, D], F32, "x1t", n=2)
            pZ = [cx.ps([128, 512], F32, "pZ", n=2) for _ in range(2)]
            junk = cx.sb([128, 512], BF16, "junk")
            ssa = cx.sb([128, 1], F32, "ssa", n=2)
            ssb = cx.sb([128, 1], F32, "ssb", n=2)
            rs = cx.sb([128, 1], F32, "rs", n=2)
            tt_ = cx.sb([128, D], F32, "tt", n=2)
            ot = cx.sb([128, D], F32, "ot", n=2)

            def ld(i):
                cx.load(x1t[i % 2][:, :], x1d[i * 128:(i + 1) * 128, :], x1t[i % 2], rd=[x1d_b])
                if i % 4 == 0:
                    j = i // 4
                    cx.load(f1Tt[j % 2][:, :, :], f1Tv[:, :, j * 512:(j + 1) * 512], f1Tt[j % 2], rd=[f1Td_b])
            ld(0)
            for i in range(NB):
                if i + 1 < NB:
                    ld(i + 1)
                x1, ft, z, o_ = x1t[i % 2], f1Tt[(i // 4) % 2], pZ[i % 2], ot[i % 2]
                sub = i % 4
                for n in range(2):
                    for c in range(32):
                        cx.mm(z[n][:, :], ft[:, c, sub * 128:(sub + 1) * 128], W2[:, c, n * 512:(n + 1) * 512], c == 0, c == 31, [ft.b, W2.b], [z[n].b])
                post_norm_residual(cx, z, x1, GT_f, MOD.b, o_, junk, ssa[i % 2], ssb[i % 2], rs[i % 2], tt_[i % 2])
                cx.store(xo[i * 128:(i + 1) * 128, :], o_[:, :], o_, xo_b, is_output=True)
            S.barrier()
        cx.stack = root
    cx.stack = base
    S.barrier()
    S.lane_release(lmark)


def build_C(T, parts="1234"):
    nc = bass.Bass("TRN2", target_bir_lowering=False)
    A = declare_C(nc, T)
    xc = nc.dram_tensor("xc", [T, D], F32, kind="ExternalInput").ap()
    yTc = nc.dram_tensor("yTc", [3072, T], BF16, kind="ExternalInput").ap()
    yTv = yTc.rearrange("(c p) t -> p c t", p=128)
    A["x_tile"] = lambda i: xc[i * 128:(i + 1) * 128, :]
    A["y_tiles"] = lambda i: [(0, 24, yTv[:, :, i * 128:(i + 1) * 128])]
    cond = nc.dram_tensor("consts", [128, 512], F32, kind="ExternalInput").ap()
    with ExitStack() as root:
        cx = Cx(nc, root)
        K = make_consts(cx, cond)
        emit_C(cx, T, A, K, parts)
        cx.S.emit()
        build_C.stats = cx.S.stats()
    return nc


def prep_C(inp, l, b, xcur, yT_all, t0, T, con):
    f = np.ascontiguousarray
    w_in = inp["w_in"][l]
    gains = np.stack([inp["norm_pre_mix"][l], inp["norm_post_mix"][l], inp["norm_pre_mlp"][l], inp["norm_post_mlp"][l]])
    wo = np.concatenate([inp["ret_w_o"][l], inp["lru_w_o"][l], inp["fox_w_o"][l]], axis=0)
    return {"xc": None if xcur is None else f(xcur[b, t0:t0 + T]), "yTc": None if yT_all is None else f(yT_all[b][:, t0:t0 + T]), "cvec": f(inp["c"][b]), "adaw": f(inp["ada_w"][l]),
            "adab": f(inp["ada_b"][l]), "gains": f(gains), "wg": f(w_in[:, OFF["gates"]:OFF["gates"] + 3072]), "wo": f(wo),
            "wout": f(inp["w_out"][l]), "w1": f(inp["mlp_w1"][l]), "w2": f(inp["mlp_w2"][l]), "consts": con}


_PROG = {}


CH_BYTES = 1 << 20


def build_F(SQ, T, L=2):
    nc = bass.Bass("TRN2", target_bir_lowering=False)
    NG = SQ // T
    groups = [list(range(g * NG, (g + 1) * NG)) for g in range(8 // NG)]
    CW = min(T, CH_BYTES // (256 * 2))
    NCC = SQ // CW
    KPT = T // CW
    XR = min(T, CH_BYTES // (D * 4))
    NXC = T // XR
    xb0 = nc.dram_tensor("xb", [SQ, D], F32, kind="ExternalInput").ap()
    cosd = nc.dram_tensor("cosT", [128, SQ], F32, kind="ExternalInput").ap()
    sind = nc.dram_tensor("sinT", [128, SQ], F32, kind="ExternalInput").ap()
    cond = nc.dram_tensor("consts", [128, 512], F32, kind="ExternalInput").ap()
    XG = [None] + [nc.dram_tensor("XG%d" % l, [NXC, NG, XR, D], F32, kind="Internal").ap() for l in range(1, L)]
    YL = [nc.dram_tensor("YL%d" % l, [NCC, 3, 256, CW], BF16, kind="Internal").ap() for l in range(L)]
    YG = [nc.dram_tensor("YG%d" % l, [NCC, 3, NG * 256, CW], BF16, kind="Internal").ap() for l in range(L)]
    YQ = [nc.dram_tensor("YQ%d" % l, [3 * NG * 256, T], BF16, kind="Internal").ap() for l in range(L)]
    XQ = nc.dram_tensor("XQ", [T, D], F32, kind="Internal").ap()
    AB = [declare_B(nc, SQ, "_b%d" % l, x_ap=xb0, y_internal=True) for l in range(L)]
    AC = [declare_C(nc, T, "_c%d" % l, out_internal=(l < L - 1)) for l in range(L)]
    with ExitStack() as root:
        cx = Cx(nc, root)
        S = cx.S
        S.need_rank = True
        K = make_consts(cx, cond)
        XG_b = None
        for l in range(L):
            A, C = AB[l], AC[l]
            A["cosT"], A["sinT"] = cosd, sind
            if l > 0:
                A["x_rd"] = [XG_b]

                def xb_tile(i, xg=XG[l]):
                    t0 = i * 128
                    return xg[(t0 % T) // XR, t0 // T, (t0 % XR):(t0 % XR) + 128, :]
                A["xb_tile"] = xb_tile

            def y_dst(r0, nr, j, yl=YL[l]):
                t0 = j * 512
                return yl[t0 // CW, r0 // 256, (r0 % 256):(r0 % 256) + nr, (t0 % CW):(t0 % CW) + 512]
            A["y_dst"] = y_dst
            emit_B(cx, SQ, A, K, parts=("" if "B" in os.environ.get("K_FSKIP", "") else "0123"))
            S.new_epoch()
            YG_b = Buf("YG%d" % l)
            for cc in range(NCC):
                for br in range(3):
                    S.coll(lambda e, i_=YL[l][cc, br], o_=YG[l][cc, br]: e.collective_compute(
                        "AllGather", ALU.bypass, replica_groups=groups, ins=[i_.opt()], outs=[o_.opt()]),
                        YG_b, reads=[A["yT_b"]], writes=[YG_b])
            YQ_b = Buf("YQ%d" % l)
            yq3 = YQ[l].rearrange("(i r) t -> i r t", i=3)
            for k in range(KPT):
                for br in range(3):
                    S.dma(yq3[br:br + 1, :, k * CW:(k + 1) * CW],
                          (lambda S_, e, k=k, br=br, yg=YG[l]: yg[bass.ds(S_.q_rank * KPT + k, 1), br, :, :]),
                          YQ_b, reads=[YG_b], writes=[YQ_b])
            yqv = YQ[l].rearrange("(c p) t -> p c t", p=128)
            C["y_tiles"] = lambda i, yqv=yqv: [(0, 24, yqv[:, :, i * 128:(i + 1) * 128])]
            C["y_rd"] = [YQ_b]
            if l == 0:
                XQ_b = Buf("XQ")
                S.dma(XQ, (lambda S_, e: xb0[bass.ds(S_.q_rank * T, T), :]), XQ_b, writes=[XQ_b])
                C["x_tile"] = lambda i: XQ[i * 128:(i + 1) * 128, :]
                C["x_rd"] = [XQ_b]
            else:
                X1 = AC[l - 1]["xo"]
                C["x_tile"] = lambda i, X1=X1: X1[i * 128:(i + 1) * 128, :]
                C["x_rd"] = [AC[l - 1]["xo_b"]]
            emit_C(cx, T, C, K)
            S.new_epoch()
            if l < L - 1:
                XG_b = Buf("XG%d" % (l + 1))
                for k in range(NXC):
                    S.coll(lambda e, i_=C["xo"][k * XR:(k + 1) * XR, :], o_=XG[l + 1][k]: e.collective_compute(
                        "AllGather", ALU.bypass, replica_groups=groups, ins=[i_.opt()], outs=[o_.opt()]),
                        XG_b, reads=[C["xo_b"]], writes=[XG_b])
        S.emit()
        build_F.stats = S.stats()
    return nc


def _prog(kind, *n):
    key = (kind,) + tuple(n)
    if key not in _PROG:
        _PROG[key] = {"B": build_B, "C": build_C, "F": build_F}[kind](*n)
    return _PROG[key]


def kernel(**inp):
    inp = {k: np.asarray(v) for k, v in inp.items()}
    Bn, SQ, _ = inp["x"].shape
    L = inp["w_in"].shape[0]
    NG = 8 // Bn
    T = SQ // NG
    con = _consts()
    tables = (*_rope_tables(SQ), con)
    cores = list(range(8))
    nc = _prog("F", SQ, T, L)
    shared = ("xb", "cosT", "sinT", "consts")
    maps = []
    for c in cores:
        b, g = c // NG, c % NG
        m = {}
        for l in range(L):
            pb = prep_B(inp, l, b, g, SQ, tables)
            for k, v in pb.items():
                if k in shared:
                    m[k] = v
                else:
                    m[k + "_b%d" % l] = v
            pc = prep_C(inp, l, b, None, None, 0, T, con)
            for k, v in pc.items():
                if k not in ("xc", "yTc", "consts"):
                    m[k + "_c%d" % l] = v
        maps.append(m)
    res = run_bass_kernel_spmd(nc, maps, core_ids=cores)
    out = np.empty((Bn, SQ, D), np.float32)
    for c in cores:
        out[c // NG, (c % NG) * T:(c % NG + 1) * T] = np.asarray(res.results[c]["xo_c%d" % (L - 1)])
    return out
```

```python
import os
import numpy as np
from contextlib import ExitStack
import concourse.bass as bass
import concourse.mybir as mybir
from concourse.bass_utils import run_bass_kernel_spmd

F32 = mybir.dt.float32
BF16 = mybir.dt.bfloat16
AF = mybir.ActivationFunctionType
ALU = mybir.AluOpType
AX = mybir.AxisListType
BIG = 1 << 40


class Buf:
    __slots__ = ("name", "lw", "rd", "lane", "bulk")

    def __init__(self, name="", bulk=False):
        self.name = name
        self.lw = None
        self.rd = {}
        self.lane = None
        self.bulk = bulk


class Sched:
    ENG = ("pe", "act", "dve", "pool", "sp")

    def __init__(self, nc, stack):
        self.nc = nc
        self.stack = stack
        self.q = {k: [] for k in self.ENG}
        self.sem = {k: stack.enter_context(nc.semaphore("s_" + k)) for k in self.ENG}
        self.cnt = {k: 0 for k in self.ENG}
        self.waited = {k: {} for k in self.ENG}
        self.lanes = []
        self.nlane = 0
        self.out_lanes = []
        self.need_rank = False
        self.free_lanes = []
        self.live_lanes = []
        self.nepoch = 1
        self.q_rank = None

    def _lane(self, buf):
        if buf.lane is None:
            if self.free_lanes and not buf.bulk:
                buf.lane = self.free_lanes.pop()
            else:
                sem = self.stack.enter_context(self.nc.semaphore("l%d" % self.nlane))
                self.nlane += 1
                buf.lane = [sem, 0, buf.bulk]
                self.lanes.append(buf.lane)
            self.live_lanes.append(buf.lane)
        return buf.lane

    def lane_mark(self):
        return len(self.live_lanes)

    def lane_release(self, mark):
        self.free_lanes.extend(self.live_lanes[mark:])
        del self.live_lanes[mark:]

    def new_epoch(self):
        for k in self.ENG:
            self.sem[k] = self.stack.enter_context(self.nc.semaphore("s%d_%s" % (self.nepoch, k)))
            self.cnt[k] = 0
        self.nepoch += 1

    def _deps(self, eng, reads, writes):
        d = {}

        def add(t):
            if t is None:
                return
            s, v = t
            if d.get(s, 0) < v:
                d[s] = v
        own = self.sem[eng]
        for b in reads:
            add(b.lw)
        for b in writes:
            if b.lw is not None and b.lw[0] is not own:
                add(b.lw)
            for s, v in b.rd.items():
                if s is not own:
                    add((s, v))
        w = self.waited[eng]
        waits = []
        for s, v in d.items():
            if eng == "pe" and s is own:
                continue
            if w.get(s, 0) < v:
                w[s] = v
                waits.append((s, v))
        return waits

    def _mark(self, tok, reads, writes):
        s, v = tok
        for b in reads:
            if b.rd.get(s, 0) < v:
                b.rd[s] = v
        for b in writes:
            b.lw = tok
            b.rd = {}

    def op(self, eng, fn, reads=(), writes=()):
        waits = self._deps(eng, reads, writes)
        self.cnt[eng] += 1
        tok = (self.sem[eng], self.cnt[eng])
        self.q[eng].append((waits, fn, self.sem[eng], 1))
        self._mark(tok, reads, writes)
        return tok

    def dma(self, out_ap, in_ap, lane_buf, reads=(), writes=(), q="sp", is_output=False, **kw):
        waits = self._deps(q, reads, writes)
        lane = self._lane(lane_buf)
        lane[1] += 16
        tok = (lane[0], BIG if lane[2] else lane[1])
        self.q[q].append((waits, lambda e: e.dma_start(out=out_ap, in_=(in_ap(self, e) if callable(in_ap) else in_ap), **kw), lane[0], 16))
        self._mark(tok, reads, writes)
        if is_output and lane not in self.out_lanes:
            self.out_lanes.append(lane)
        return tok

    def coll(self, fn, lane_buf, reads=(), writes=()):
        waits = self._deps("pool", reads, writes)
        lane = self._lane(lane_buf)
        lane[1] += 1
        tok = (lane[0], lane[1])
        self.q["pool"].append((waits, fn, lane[0], 1))
        self._mark(tok, reads, writes)
        return tok

    def barrier(self):
        targets = [(self.sem[k], self.cnt[k]) for k in ("pe", "act", "dve", "pool") if self.cnt[k] > 0]
        targets += [(l[0], l[1]) for l in self.lanes if l[1] > 0 and not l[2]]
        targets += [(l[0], BIG) for l in self.lanes if l[1] > 0 and l[2]]
        for eng in self.ENG:
            w = self.waited[eng]
            waits = []
            for s, v in targets:
                if w.get(s, 0) < v:
                    w[s] = v
                    waits.append((s, v))
            if waits:
                self.q[eng].append((waits, None, None, 0))

    def emit(self):
        nc = self.nc
        final = {id(l[0]): l[1] for l in self.lanes}
        lanes_by_sem = {id(l[0]): l for l in self.lanes}

        def run(eng_name, e):
            if eng_name == "sp" and self.need_rank:
                self.q_rank = e.snap(e.partition_id() % 4)
            for waits, fn, sem, inc in self.q[eng_name]:
                for s, v in waits:
                    if v >= BIG:
                        v = final[id(s)]
                    e.wait_ge(s, v)
                if fn is not None:
                    fn(e).then_inc(sem, inc)
            if eng_name == "sp":
                for l in self.lanes:
                    if l[1] > 0:
                        e.wait_ge(l[0], l[1])
                for k in ("pe", "act", "dve", "pool"):
                    if self.cnt[k] > 0:
                        e.wait_ge(self.sem[k], self.cnt[k])

        with nc.Block() as block:
            @block.sync
            def _(e):
                run("sp", e)

            @block.tensor
            def _(e):
                run("pe", e)

            @block.scalar
            def _(e):
                run("act", e)

            @block.vector
            def _(e):
                run("dve", e)

            @block.gpsimd
            def _(e):
                run("pool", e)

    def stats(self):
        return {k: len(v) for k, v in self.q.items()}


EPS = 1e-6
D = 1024
DECL = []


class Tl:
    __slots__ = ("t", "b")

    def __init__(self, t, name):
        self.t = t
        self.b = Buf(name)

    def __getitem__(self, k):
        return self.t[k]


class PsView:
    __slots__ = ("tl", "off", "b")

    def __init__(self, tl, off, name):
        self.tl, self.off, self.b = tl, off, Buf(name)

    def cols(self, a, b):
        return self.tl[:, self.off + a:self.off + b]


class Cx:
    def __init__(self, nc, stack):
        self.nc = nc
        self.S = Sched(nc, stack)
        self.root = stack
        self.stack = stack
        self.n = 0

    def sb(self, shape, dt, name=None, n=1):
        r = []
        for _ in range(n):
            self.n += 1
            nm = "%s_%d" % (name or "sb", self.n)
            r.append(Tl(self.stack.enter_context(self.nc.sbuf_tensor(nm, list(shape), dt)), nm))
        return r[0] if n == 1 else r

    def ps(self, shape, dt, name=None, n=1):
        r = []
        for _ in range(n):
            self.n += 1
            nm = "%s_%d" % (name or "ps", self.n)
            r.append(Tl(self.stack.enter_context(self.nc.psum_tensor(nm, list(shape), dt)), nm))
        return r[0] if n == 1 else r

    def mm(self, out, lhsT, rhs, start, stop, rd, wr, **kw):
        self.S.op("pe", lambda e: e.matmul(out, lhsT=lhsT, rhs=rhs, start=start, stop=stop, **kw), rd, wr)

    def tr(self, out, in_, ident, rd, wr):
        self.S.op("pe", lambda e: e.transpose(out=out, in_=in_, identity=ident), rd, wr)

    def act(self, out, in_, func, rd, wr, bias=None, scale=None, accum=None):
        kw = {}
        if bias is not None:
            kw["bias"] = bias
        if scale is not None:
            kw["scale"] = scale
        if accum is not None:
            kw["accum_out"] = accum
        self.S.op("act", lambda e: e.activation(out=out, in_=in_, func=func, **kw), rd, wr)

    def tt(self, eng, out, a, b, op, rd, wr):
        self.S.op(eng, lambda e: e.tensor_tensor(out=out, in0=a, in1=b, op=op), rd, wr)

    def ts(self, eng, out, a, s1, s2, op0, op1, rd, wr):
        if s2 is None:
            self.S.op(eng, lambda e: e.tensor_single_scalar(out=out, in_=a, scalar=s1, op=op0), rd, wr)
        else:
            self.S.op(eng, lambda e: e.tensor_scalar(out=out, in0=a, scalar1=s1, scalar2=s2, op0=op0, op1=op1), rd, wr)

    def stt(self, eng, out, a, s, b, op0, op1, rd, wr):
        self.S.op(eng, lambda e: e.scalar_tensor_tensor(out=out, in0=a, scalar=s, in1=b, op0=op0, op1=op1), rd, wr)

    def cp(self, eng, out, in_, rd, wr):
        if eng == "act":
            self.S.op("act", lambda e: e.copy(out=out, in_=in_), rd, wr)
        else:
            self.S.op(eng, lambda e: e.tensor_copy(out=out, in_=in_), rd, wr)

    def recip(self, out, in_, rd, wr):
        self.S.op("dve", lambda e: e.reciprocal(out=out, in_=in_), rd, wr)

    def memset(self, eng, ap, v, wr):
        self.S.op(eng, lambda e: e.memset(ap, v), (), wr)

    def load(self, out, in_, tl, q="sp", rd=()):
        self.S.dma(out, in_, tl.b, reads=rd, writes=[tl.b], q=q)

    def store(self, out, in_, tl, dbuf=None, q="pool", is_output=False):
        self.S.dma(out, in_, tl.b, reads=[tl.b], writes=[dbuf] if dbuf is not None else (), q=q, is_output=is_output)


def load_w_bf16(cx, dst, dst_c0, dram, kchunks, ncols, stg):
    src = dram.rearrange("(c p) n -> p c n", p=128)
    i = 0
    for c0 in range(0, kchunks, 8):
        c1 = min(kchunks, c0 + 8)
        for n0 in range(0, ncols, 512):
            n1 = min(ncols, n0 + 512)
            s = stg[i % len(stg)]
            i += 1
            cx.load(s[:, 0:c1 - c0, 0:n1 - n0], src[:, c0:c1, n0:n1], s)
            eng = "pool" if i % 2 == 0 else "dve"
            cx.cp(eng, dst[:, dst_c0 + c0:dst_c0 + c1, n0:n1], s[:, 0:c1 - c0, 0:n1 - n0], [s.b], [dst.b])


def compute_mod(cx, cvec, adaw, adab, col0, ncols, MOD, CON, stg, pbank):
    crep = cx.sb([128, D], F32, "crep")
    sB = cx.sb([128, 8, 128], F32, "sB")
    brep = cx.sb([128, 512], F32, "brep", n=2)
    cx.load(crep[:, :], cvec.partition_broadcast(128), crep)
    cx.act(crep[:, :], crep[:, :], AF.Silu, [crep.b], [crep.b])
    for c in range(8):
        p = pbank[c % 2]
        cx.tr(p[:, 0:128], crep[:, c * 128:(c + 1) * 128], CON[:, 384:512], [crep.b, CON.b], [p.b])
        cx.cp("dve", sB[:, c, :], p[:, 0:128], [p.b], [sB.b])
    aw = adaw.rearrange("(c p) n -> p c n", p=128)
    for i, n0 in enumerate(range(0, ncols, 512)):
        s = stg[i % len(stg)]
        br = brep[i % 2]
        p = pbank[i % 2]
        cx.load(s[:, :, :], aw[:, :, col0 + n0:col0 + n0 + 512], s)
        cx.load(br[:, :], adab[col0 + n0:col0 + n0 + 512].partition_broadcast(128), br)
        for c in range(8):
            cx.mm(p[:, :], sB[:, c, :], s[:, c, :], c == 0, c == 7, [sB.b, s.b], [p.b])
        cx.tt("dve", MOD[:, n0:n0 + 512], p[:, :], br[:, :], ALU.add, [p.b, br.b], [MOD.b])


def norm_mod_tile(cx, xt, G2, SH, modb, hb, ss, rs, t1, junk):
    cx.act(junk[:, :], xt[:, :], AF.Square, [xt.b], [junk.b, ss.b], accum=ss[:, 0:1])
    cx.act(rs[:, 0:1], ss[:, 0:1], AF.Sqrt, [ss.b], [rs.b], bias=EPS, scale=1.0 / D)
    cx.recip(rs[:, 0:1], rs[:, 0:1], [rs.b], [rs.b])
    cx.stt("dve", t1[:, :], xt[:, :], rs[:, 0:1], G2, ALU.mult, ALU.mult, [xt.b, rs.b, modb], [t1.b])
    cx.tt("pool", hb[:, :], t1[:, :], SH, ALU.add, [t1.b, modb], [hb.b])


def declare_B(nc, SL, sfx="", x_ap=None, y_internal=False):
    def din(name, shape, dt=F32):
        DECL.append((name + sfx, tuple(shape)))
        return nc.dram_tensor(name + sfx, list(shape), dt, kind="ExternalInput").ap()
    A = {}
    A["xb"] = din("xb", [SL, D]) if x_ap is None else x_ap
    A["cvec"] = din("cvec", [D])
    A["adaw"] = din("adaw", [D, 2048])
    A["adab"] = din("adab", [2048])
    A["gpre"] = din("gpre", [D])
    A["w1"] = din("w1", [D, 1024])
    A["w2"] = din("w2", [D, 512])
    A["w3"] = din("w3", [2, D, 385])
    A["rc"] = din("rc", [128, 258])
    A["lruv"] = din("lruv", [128, 2, 8])
    A["lruw"] = din("lruw", [4, 128, 128])
    A["bf"] = din("bf", [2])
    A["yT"] = nc.dram_tensor("yT" + sfx, [768, SL], BF16, kind=("Internal" if y_internal else "ExternalOutput")).ap()
    A["hTd"] = nc.dram_tensor("hTd" + sfx, [D, SL], BF16, kind="Internal").ap()
    return A


def make_consts(cx, cond):
    CON = cx.sb([128, 512], F32, "CON")
    IDB = cx.sb([128, 128], BF16, "IDB")
    MASKB = cx.sb([128, 128], BF16, "MASKB")
    ONESB = cx.sb([128, 512], F32, "ONESB")
    NEGM = cx.sb([128, 128], F32, "NEGM")
    cx.load(CON[:, :], cond, CON)
    cx.cp("dve", IDB[:, :], CON[:, 384:512], [CON.b], [IDB.b])
    cx.cp("dve", MASKB[:, :], CON[:, 0:128], [CON.b], [MASKB.b])
    cx.memset("pool", ONESB[:, :], 1.0, [ONESB.b])
    cx.ts("dve", NEGM[:, :], CON[:, 0:128], 30000.0, -30000.0, ALU.mult, ALU.add, [CON.b], [NEGM.b])
    return {"CON": CON, "IDB": IDB, "MASKB": MASKB, "ONESB": ONESB, "NEGM": NEGM}


def emit_B(cx, SL, A, K, parts="0123"):
    nc, S, base = cx.nc, cx.S, cx.stack
    xb, cvec, adaw, adab, gpre = A["xb"], A["cvec"], A["adaw"], A["adab"], A["gpre"]
    w1d, w2d, w3d, cosd, sind = A["w1"], A["w2"], A["w3"], A["cosT"], A["sinT"]
    rcd, lvd, lwd, bfd, yT, hTd = A["rc"], A["lruv"], A["lruw"], A["bf"], A["yT"], A["hTd"]
    x_rd = list(A.get("x_rd", ()))
    xb_tile = A.get("xb_tile") or (lambda i: xb[i * 128:(i + 1) * 128, :])
    y_dst = A.get("y_dst") or (lambda r0, nr, j: yT[r0:r0 + nr, j * 512:(j + 1) * 512])
    CON, IDB, MASKB, ONESB, NEGM = K["CON"], K["IDB"], K["MASKB"], K["ONESB"], K["NEGM"]
    TRI, ONES, SEL, IDF = CON[:, 0:128], CON[:, 128:256], CON[:, 256:384], CON[:, 384:512]
    hTd_b = Buf("hTd")
    yT_b = A.setdefault("yT_b", Buf("yTd"))
    hTv = hTd.rearrange("(c p) t -> p c t", p=128)
    NT = SL // 512
    NB = SL // 128
    lmark = S.lane_mark()
    with ExitStack() as root:
        cx.stack = root
        with ExitStack() as st:
            cx.stack = st
            MOD = cx.sb([128, 2048], F32, "MOD")
            GB = cx.sb([128, D], F32, "GB")
            stg = cx.sb([128, 8, 512], F32, "stg", n=2)
            pb = cx.ps([128, 512], F32, "pmod", n=2)
            compute_mod(cx, cvec, adaw, adab, 0, 2048, MOD, CON, stg, pb)
            cx.load(GB[:, :], gpre.partition_broadcast(128), GB)
            cx.stt("dve", MOD[:, 1024:2048], MOD[:, 1024:2048], 1.0, GB[:, :], ALU.add, ALU.mult, [MOD.b, GB.b], [MOD.b])
            xt = cx.sb([128, D], F32, "xt", n=3)
            junk = cx.sb([128, D], BF16, "junk")
            ss = cx.sb([128, 1], F32, "ss", n=3)
            rs = cx.sb([128, 1], F32, "rs", n=3)
            t1 = cx.sb([128, D], F32, "t1", n=2)
            hb = cx.sb([128, D], BF16, "hb", n=2)
            pT = cx.ps([128, 8, 128], BF16, "pT", n=2)
            hTt = cx.sb([128, 8, 512], BF16, "hTt", n=2)
            cx.load(xt[0][:, :], xb_tile(0), xt[0], rd=x_rd)
            for i in range(NB):
                if i + 1 < NB:
                    cx.load(xt[(i + 1) % 3][:, :], xb_tile(i + 1), xt[(i + 1) % 3], rd=x_rd)
                x_, h_, p_, g_ = xt[i % 3], hb[i % 2], pT[i % 2], hTt[(i // 4) % 2]
                norm_mod_tile(cx, x_, MOD[:, 1024:2048], MOD[:, 0:1024], MOD.b, h_, ss[i % 3], rs[i % 3], t1[i % 2], junk)
                for c in range(8):
                    cx.tr(p_[:, c, :], h_[:, c * 128:(c + 1) * 128], IDB[:, :], [h_.b, IDB.b], [p_.b])
                cx.cp("act", g_[:, :, (i % 4) * 128:(i % 4 + 1) * 128], p_[:, :, :], [p_.b], [g_.b])
                if i % 4 == 3:
                    j = i // 4
                    cx.store(hTv[:, :, j * 512:(j + 1) * 512], g_[:, :, :], g_, hTd_b)
            S.barrier()
        cx.stack = root

        with ExitStack() as st:
          if "1" in parts:
            cx.stack = st
            W1 = cx.sb([128, 8, 1024], BF16, "W1")
            stg = cx.sb([128, 8, 512], F32, "stg", n=2)
            load_w_bf16(cx, W1, 0, w1d, 8, 1024, stg)
            RC = cx.sb([128, 258], F32, "RC")
            cx.load(RC[:, :], rcd, RC)
            WT, DQ, DKC, G128 = RC[:, 0:128], RC[:, 128:256], RC[:, 256:257], RC[:, 257:258]
            St = cx.sb([128, 256], F32, "St")
            Stb = cx.sb([128, 256], BF16, "Stb", n=2)
            cx.memset("dve", St[:, :], 0.0, [St.b])
            cx.memset("dve", Stb[0][:, :], 0.0, [Stb[0].b])
            hT = cx.sb([128, 8, 512], BF16, "hT", n=2)
            cs = cx.sb([128, 512], F32, "cs", n=2)
            sn = cx.sb([128, 512], F32, "sn", n=2)
            pA = cx.ps([128, 512], F32, "pA")
            pB = cx.ps([128, 512], F32, "pB")
            pVG = cx.ps([128, 512], F32, "pVG", n=2)
            pSO = cx.ps([128, 512], F32, "pSO")
            pU = cx.ps([128, 512], F32, "pU")
            pTr = cx.ps([128, 1024], BF16, "pTr")
            pTrK, pTrY = Buf("pTrK"), Buf("pTrY")
            pST, pO = Buf("pST"), Buf("pO")
            tA = cx.sb([128, 512], F32, "tA", n=2)
            tB = cx.sb([128, 512], F32, "tB", n=2)
            qT = cx.sb([128, 512], BF16, "qT", n=2)
            kT = cx.sb([128, 512], BF16, "kT", n=2)
            V = cx.sb([128, 256], BF16, "V", n=4)
            SG = cx.sb([128, 256], F32, "SG", n=4)
            SW = cx.sb([128, 128], BF16, "SW", n=2)
            qd = cx.sb([128, 128], BF16, "qd", n=2)
            kd = cx.sb([128, 128], BF16, "kd", n=2)
            stt_ = cx.sb([128, 6], F32, "bst", n=2)
            mv = cx.sb([128, 2], F32, "mv", n=2)
            rs = cx.sb([128, 1], F32, "rs", n=2)
            nb = cx.sb([128, 1], F32, "nb", n=2)
            ret = cx.sb([128, 256], F32, "ret", n=2)
            yb = cx.sb([128, 256], BF16, "yb", n=2)
            yTt = cx.sb([128, 2, 512], BF16, "yTt", n=2)

            def ld(j):
                cx.load(hT[j % 2][:, :, :], hTv[:, :, j * 512:(j + 1) * 512], hT[j % 2], rd=[hTd_b])
                cx.load(cs[j % 2][:, :], cosd[:, j * 512:(j + 1) * 512], cs[j % 2])
                cx.load(sn[j % 2][:, :], sind[:, j * 512:(j + 1) * 512], sn[j % 2])
            ld(0)
            blk = 0
            for j in range(NT):
                if j + 1 < NT:
                    ld(j + 1)
                h_, c_, s_ = hT[j % 2], cs[j % 2], sn[j % 2]
                for which, dst in ((0, qT[j % 2]), (1, kT[j % 2])):
                    for c in range(8):
                        cx.mm(pA[:, :], W1[:, c, which * 256:which * 256 + 128], h_[:, c, :], c == 0, c == 7, [W1.b, h_.b], [pA.b])
                    for c in range(8):
                        cx.mm(pB[:, :], W1[:, c, which * 256 + 128:which * 256 + 256], h_[:, c, :], c == 0, c == 7, [W1.b, h_.b], [pB.b])
                    a_, b_ = tA[which], tB[which]
                    cx.tt("dve", a_[:, :], pA[:, :], c_[:, :], ALU.mult, [pA.b, c_.b], [a_.b])
                    cx.tt("dve", b_[:, :], pB[:, :], s_[:, :], ALU.mult, [pB.b, s_.b], [b_.b])
                    cx.tt("pool", dst[:, :], a_[:, :], b_[:, :], ALU.add, [a_.b, b_.b], [dst.b])
                q_, k_ = qT[j % 2], kT[j % 2]
                yt_ = yTt[j % 2]
                for sub in range(4):
                    sl = slice(sub * 128, (sub + 1) * 128)
                    pv = pVG[sub % 2]
                    v_, sg_ = V[sub], SG[sub]
                    for c in range(8):
                        cx.mm(pv[:, :], h_[:, c, sl], W1[:, c, 512:1024], c == 0, c == 7, [W1.b, h_.b], [pv.b])
                    cx.cp("act", v_[:, :], pv[:, 0:256], [pv.b], [v_.b])
                    cx.act(sg_[:, :], pv[:, 256:512], AF.Silu, [pv.b], [sg_.b])
                    sw_, qd_, kd_ = SW[blk % 2], qd[blk % 2], kd[blk % 2]
                    cx.mm(pSO[:, 0:128], k_[:, sl], q_[:, sl], True, True, [k_.b, q_.b], [pST])
                    cx.tt("dve", sw_[:, :], pSO[:, 0:128], WT, ALU.mult, [pST, RC.b], [sw_.b])
                    cx.tt("pool", qd_[:, :], q_[:, sl], DQ, ALU.mult, [q_.b, RC.b], [qd_.b])
                    cx.mm(pSO[:, 128:384], sw_[:, :], v_[:, :], True, False, [sw_.b, v_.b], [pO])
                    cx.mm(pSO[:, 128:384], qd_[:, :], Stb[blk % 2][:, :], False, True, [qd_.b, Stb[blk % 2].b], [pO])
                    cx.tr(pTr[:, 0:128], k_[:, sl], IDB[:, :], [k_.b, IDB.b], [pTrK])
                    cx.act(kd_[:, :], pTr[:, 0:128], AF.Copy, [pTrK, RC.b], [kd_.b], scale=DKC)
                    cx.mm(pU[:, 0:256], kd_[:, :], v_[:, :], True, True, [kd_.b, v_.b], [pU.b])
                    cx.stt("dve", St[:, :], St[:, :], G128, pU[:, 0:256], ALU.mult, ALU.add, [St.b, pU.b, RC.b], [St.b])
                    cx.cp("pool", Stb[(blk + 1) % 2][:, :], St[:, :], [St.b], [Stb[(blk + 1) % 2].b])
                    b6, m2, r1, n1 = stt_[blk % 2], mv[blk % 2], rs[blk % 2], nb[blk % 2]
                    cx.S.op("dve", lambda e, o=b6[:, :], i=pSO[:, 128:384]: e.bn_stats(out=o, in_=i), [pO], [b6.b])
                    cx.S.op("dve", lambda e, o=m2[:, :], i=b6[:, :]: e.bn_aggr(out=o, in_=i), [b6.b], [m2.b])
                    cx.act(r1[:, 0:1], m2[:, 1:2], AF.Sqrt, [m2.b], [r1.b], bias=EPS)
                    cx.recip(r1[:, 0:1], r1[:, 0:1], [r1.b], [r1.b])
                    cx.stt("dve", n1[:, 0:1], m2[:, 0:1], -1.0, r1[:, 0:1], ALU.mult, ALU.mult, [m2.b, r1.b], [n1.b])
                    rt, y_ = ret[blk % 2], yb[blk % 2]
                    cx.act(rt[:, :], pSO[:, 128:384], AF.Identity, [pO, r1.b, n1.b], [rt.b], bias=n1[:, 0:1], scale=r1[:, 0:1])
                    cx.tt("pool", y_[:, :], rt[:, :], sg_[:, :], ALU.mult, [rt.b, sg_.b], [y_.b])
                    for f in range(2):
                        cx.tr(pTr[:, 128 + f * 128:256 + f * 128], y_[:, f * 128:(f + 1) * 128], IDB[:, :], [y_.b, IDB.b], [pTrY])
                    cx.cp("act", yt_[:, :, sl], pTr[:, 128:384].rearrange("p (f t) -> p f t", f=2), [pTrY], [yt_.b])
                    blk += 1
                cx.store(y_dst(0, 256, j).rearrange("(f p) t -> p f t", p=128), yt_[:, :, :], yt_, yT_b, is_output=True)
            S.barrier()
        cx.stack = root

        with ExitStack() as st:
          if "2" in parts:
            cx.stack = st
            W2 = cx.sb([128, 8, 512], BF16, "W2")
            stg = cx.sb([128, 8, 512], F32, "stg", n=2)
            load_w_bf16(cx, W2, 0, w2d, 8, 512, stg)
            WAX = cx.sb([128, 4, 128], BF16, "WAX")
            waxs = cx.sb([128, 4, 128], F32, "waxs")
            cx.load(waxs[:, :, :], lwd.rearrange("n i o -> i n o"), waxs)
            cx.cp("dve", WAX[:, :, :], waxs[:, :, :], [waxs.b], [WAX.b])
            LV = cx.sb([128, 2, 8], F32, "LV")
            cx.load(LV[:, :, :], lvd, LV)
            EL = cx.sb([128, 2], F32, "EL")
            SPL = cx.sb([128, 2], F32, "SPL")
            M8 = cx.sb([128, 2], F32, "M8")
            M16 = cx.sb([128, 2], F32, "M16")
            cx.act(EL[:, :], LV[:, :, 7], AF.Exp, [LV.b], [EL.b], scale=-1.0)
            cx.act(SPL[:, :], EL[:, :], AF.Ln, [EL.b], [SPL.b], bias=1.0)
            cx.ts("dve", M8[:, :], SPL[:, :], -8.0, None, ALU.mult, None, [SPL.b], [M8.b])
            cx.ts("dve", M16[:, :], SPL[:, :], -16.0, None, ALU.mult, None, [SPL.b], [M16.b])
            H0 = cx.sb([128, 1], F32, "H0")
            cx.memset("dve", H0[:, :], 0.0, [H0.b])
            hT = cx.sb([128, 8, 512], BF16, "hT", n=2)
            LX = [cx.sb([128, 515], F32, "LX", n=2) for _ in range(2)]
            for b in range(2):
                cx.memset("pool", LX[b][0][:, 0:3], 0.0, [LX[b][0].b])
            pX = cx.ps([128, 512], F32, "pX", n=2)
            pR = cx.ps([128, 512], F32, "pR", n=2)
            pI = cx.ps([128, 512], F32, "pI", n=2)
            pY = cx.ps([128, 512], F32, "pY", n=2)
            u = cx.sb([128, 512], F32, "u", n=2)
            ub = cx.sb([128, 512], BF16, "ub", n=2)
            r_ = cx.sb([128, 512], F32, "r", n=2)
            ig = cx.sb([128, 512], F32, "ig", n=2)
            a_ = cx.sb([128, 512], F32, "a", n=2)
            a2 = cx.sb([128, 512], F32, "a2", n=2)
            sq = cx.sb([128, 512], F32, "sq", n=2)
            uu = cx.sb([128, 512], F32, "uu", n=2)
            hs = [cx.sb([128, 512], F32, "hs", n=2) for _ in range(2)]
            lyS = cx.sb([128, 512], F32, "lyS", n=2)
            x2 = cx.sb([128, 512], F32, "x2", n=2)
            inn = cx.sb([128, 512], F32, "inn", n=2)
            sgm = cx.sb([128, 512], F32, "sgm", n=2)
            yL = cx.sb([128, 2, 512], BF16, "yL", n=2)
            cx.load(hT[0][:, :, :], hTv[:, :, 0:512], hT[0], rd=[hTd_b])
            it = 0
            for j in range(NT):
                if j + 1 < NT:
                    cx.load(hT[(j + 1) % 2][:, :, :], hTv[:, :, (j + 1) * 512:(j + 2) * 512], hT[(j + 1) % 2], rd=[hTd_b])
                h_ = hT[j % 2]
                yl = yL[j % 2]
                for b in range(2):
                    k = it % 2
                    it += 1
                    lx, lxn = LX[b][j % 2], LX[b][(j + 1) % 2]
                    px, pr, pi, py = pX[k], pR[k], pI[k], pY[k]
                    for c in range(8):
                        cx.mm(px[:, :], W2[:, c, b * 128:(b + 1) * 128], h_[:, c, :], c == 0, c == 7, [W2.b, h_.b], [px.b])
                    for c in range(8):
                        cx.mm(py[:, :], W2[:, c, 256 + b * 128:256 + (b + 1) * 128], h_[:, c, :], c == 0, c == 7, [W2.b, h_.b], [py.b])
                    cx.cp("act", lx[:, 3:515], px[:, :], [px.b], [lx.b])
                    cx.cp("pool", lxn[:, 0:3], lx[:, 512:515], [lx.b], [lxn.b])
                    u_ = u[k]
                    cx.ts("dve", u_[:, :], lx[:, 0:512], LV[:, b, 0:1], LV[:, b, 4:5], ALU.mult, ALU.add, [lx.b, LV.b], [u_.b])
                    for i in range(1, 4):
                        cx.stt("dve", u_[:, :], lx[:, i:i + 512], LV[:, b, i:i + 1], u_[:, :], ALU.mult, ALU.add, [lx.b, LV.b, u_.b], [u_.b])
                    cx.cp("pool", ub[k][:, :], u_[:, :], [u_.b], [ub[k].b])
                    cx.mm(pr[:, :], WAX[:, b, :], ub[k][:, :], True, True, [WAX.b, ub[k].b], [pr.b])
                    cx.mm(pi[:, :], WAX[:, 2 + b, :], ub[k][:, :], True, True, [WAX.b, ub[k].b], [pi.b])
                    cx.act(r_[k][:, :], pr[:, :], AF.Sigmoid, [pr.b, LV.b], [r_[k].b], bias=LV[:, b, 5:6])
                    cx.act(ig[k][:, :], pi[:, :], AF.Sigmoid, [pi.b, LV.b], [ig[k].b], bias=LV[:, b, 6:7])
                    cx.act(a_[k][:, :], r_[k][:, :], AF.Exp, [r_[k].b, M8.b], [a_[k].b], scale=M8[:, b:b + 1])
                    cx.act(a2[k][:, :], r_[k][:, :], AF.Exp, [r_[k].b, M16.b], [a2[k].b], scale=M16[:, b:b + 1])
                    cx.act(sq[k][:, :], a2[k][:, :], AF.Sqrt, [a2[k].b], [sq[k].b], bias=1.0, scale=-1.0)
                    cx.tt("pool", uu[k][:, :], ig[k][:, :], u_[:, :], ALU.mult, [ig[k].b, u_.b], [uu[k].b])
                    cx.tt("dve", uu[k][:, :], uu[k][:, :], sq[k][:, :], ALU.mult, [uu[k].b, sq[k].b], [uu[k].b])
                    hcur = hs[b][j % 2]
                    if j == 0:
                        init, ib = H0[:, 0:1], H0.b
                    else:
                        init, ib = hs[b][(j - 1) % 2][:, 511:512], hs[b][(j - 1) % 2].b
                    cx.S.op("dve", lambda e, o=hcur[:, :], d0=a_[k][:, :], d1=uu[k][:, :], i0=init: e.tensor_tensor_scan(
                        out=o, data0=d0, data1=d1, initial=i0, op0=ALU.mult, op1=ALU.add), [a_[k].b, uu[k].b, ib], [hcur.b])
                    cx.cp("act", lyS[k][:, :], py[:, :], [py.b], [lyS[k].b])
                    cx.act(x2[k][:, :], py[:, :], AF.Square, [py.b], [x2[k].b])
                    cx.ts("dve", inn[k][:, :], x2[k][:, :], 0.044715, 1.0, ALU.mult, ALU.add, [x2[k].b], [inn[k].b])
                    cx.tt("pool", inn[k][:, :], inn[k][:, :], lyS[k][:, :], ALU.mult, [inn[k].b, lyS[k].b], [inn[k].b])
                    cx.act(sgm[k][:, :], inn[k][:, :], AF.Sigmoid, [inn[k].b], [sgm[k].b], scale=1.5957691216057308)
                    cx.tt("pool", sgm[k][:, :], sgm[k][:, :], lyS[k][:, :], ALU.mult, [sgm[k].b, lyS[k].b], [sgm[k].b])
                    cx.tt("dve", yl[:, b, :], sgm[k][:, :], hcur[:, :], ALU.mult, [sgm[k].b, hcur.b], [yl.b])
                cx.store(y_dst(256, 256, j).rearrange("(f p) t -> p f t", p=128), yl[:, :, :], yl, yT_b, is_output=True)
            S.barrier()
        cx.stack = root

        with ExitStack() as st:
          if "3" in parts:
            cx.stack = st
            W3 = cx.sb([128, 8, 388], BF16, "W3")
            stg = cx.sb([128, 8, 512], F32, "stg", n=2)
            QT = cx.sb([128, SL], BF16, "QT")
            KT = cx.sb([128, SL], BF16, "KT")
            VA = cx.sb([128, NB, 132], BF16, "VA")
            cx.memset("pool", VA[:, :, :], 1.0, [VA.b])
            BFL = cx.sb([128, 2], F32, "BFL")
            cx.load(BFL[:, :], bfd.partition_broadcast(128), BFL)
            ZF = cx.sb([128, NB], F32, "ZF")
            EZ = cx.sb([128, NB], F32, "EZ")
            LF = cx.sb([128, NB], F32, "LF")
            CK = cx.sb([128, NB], F32, "CK")
            NCK = cx.sb([128, NB], F32, "NCK")
            TOT = cx.sb([128, NB], F32, "TOT")
            INC = cx.sb([128, NB], F32, "INC")
            e1 = cx.sb([128, 1], F32, "e1", n=2)
            l1 = cx.sb([128, 1], F32, "l1", n=2)
            hT = cx.sb([128, 8, 512], BF16, "hT", n=2)
            NPS = 4
            LA = 3
            pS = cx.ps([128, 512], F32, "pS", n=NPS)
            pOt = cx.ps([128, 512], F32, "pO", n=2)
            pO = [PsView(pOt[s_ // 2], (s_ % 2) * 256, "pO%d" % s_) for s_ in range(4)]
            pM = cx.ps([128, 512], F32, "pM")
            pTr = cx.ps([128, 1024], BF16, "pTr")
            SP = cx.sb([128, 512], F32, "SP", n=4)
            PT = cx.sb([128, 512], BF16, "PT", n=4)
            CQ = cx.sb([128, 512], F32, "CQ", n=2)
            ckb = cx.sb([128, 128], F32, "ckb", n=2)
            rl = cx.sb([128, 1], F32, "rl", n=2)
            yb = cx.sb([128, 128], BF16, "yb", n=2)
            yTt = cx.sb([128, 512], BF16, "yTt", n=2)
            SCALE = 128.0 ** -0.5
            for hh in range(2):
                load_w_bf16(cx, W3, 0, w3d[hh], 8, 385, stg)
                cx.load(hT[0][:, :, :], hTv[:, :, 0:512], hT[0], rd=[hTd_b])
                for j in range(NT):
                    if j + 1 < NT:
                        cx.load(hT[(j + 1) % 2][:, :, :], hTv[:, :, (j + 1) * 512:(j + 2) * 512], hT[(j + 1) % 2], rd=[hTd_b])
                    h_ = hT[j % 2]
                    tsl = slice(j * 512, (j + 1) * 512)
                    for c in range(8):
                        cx.mm(pS[0][:, :], W3[:, c, 0:128], h_[:, c, :], c == 0, c == 7, [W3.b, h_.b], [pS[0].b])
                    cx.cp("dve", QT[:, tsl], pS[0][:, :], [pS[0].b], [QT.b])
                    for c in range(8):
                        cx.mm(pS[1][:, :], W3[:, c, 128:256], h_[:, c, :], c == 0, c == 7, [W3.b, h_.b], [pS[1].b])
                    cx.cp("act", KT[:, tsl], pS[1][:, :], [pS[1].b], [KT.b])
                    for sub in range(4):
                        if "v" in os.environ.get("K_SKIP", ""):
                            break
                        blk = j * 4 + sub
                        p = pO[(sub % 2) * 2]
                        for c in range(8):
                            cx.mm(p.cols(0, 129), h_[:, c, sub * 128:(sub + 1) * 128], W3[:, c, 256:385], c == 0, c == 7, [W3.b, h_.b], [p.b])
                        cx.cp("dve", VA[:, blk, 0:128], p.cols(0, 128), [p.b], [VA.b])
                        cx.ts("dve", ZF[:, blk:blk + 1], p.cols(128, 129), BFL[:, hh:hh + 1], None, ALU.add, None, [p.b, BFL.b], [ZF.b])
                cx.act(EZ[:, :], ZF[:, :], AF.Exp, [ZF.b], [EZ.b], scale=-1.0)
                cx.act(EZ[:, :], EZ[:, :], AF.Ln, [EZ.b], [EZ.b], bias=1.0)
                cx.ts("dve", LF[:, :], EZ[:, :], -1.0, None, ALU.mult, None, [EZ.b], [LF.b])
                if os.environ.get("K_B3STOP") == "1":
                    continue
                cx.mm(pM[:, 0:NB], TRI, LF[:, :], True, True, [CON.b, LF.b], [pM.b])
                cx.mm(pS[0][:, 0:NB], ONES, LF[:, :], True, True, [CON.b, LF.b], [pS[0].b])
                cx.cp("act", TOT[:, :], pS[0][:, 0:NB], [pS[0].b], [TOT.b])
                cx.S.op("dve", lambda e: e.tensor_tensor_scan(out=INC[:, :], data0=ONESB[:, 0:NB], data1=TOT[:, :], initial=0.0,
                                                              op0=ALU.mult, op1=ALU.add), [ONESB.b, TOT.b], [INC.b])
                cx.tt("dve", INC[:, :], INC[:, :], TOT[:, :], ALU.subtract, [INC.b, TOT.b], [INC.b])
                cx.tt("dve", CK[:, :], pM[:, 0:NB], INC[:, :], ALU.add, [pM.b, INC.b], [CK.b])
                cx.ts("dve", NCK[:, :], CK[:, :], -1.0, None, ALU.mult, None, [CK.b], [NCK.b])
                S.barrier()
                if os.environ.get("K_B3STOP") == "2":
                    continue
                pairs = [(j, kb) for j in range(NT) for kb in range(4 * j + 4)]

                def qk(i):
                    j, kb = pairs[i]
                    lo = max(0, kb - 4 * j) * 128
                    p = pS[i % NPS]
                    cx.mm(p[:, lo:512], KT[:, kb * 128:(kb + 1) * 128], QT[:, j * 512 + lo:(j + 1) * 512], True, True, [KT.b, QT.b], [p.b])

                def cqrep(j):
                    cq = CQ[j % 2]
                    for sub in range(4):
                        cb_ = ckb[sub % 2]
                        cx.ts("dve", cb_[:, :], ONESB[:, 0:128], CK[:, 4 * j + sub:4 * j + sub + 1], None, ALU.mult, None, [ONESB.b, CK.b], [cb_.b])
                        cx.mm(pM[:, sub * 128:(sub + 1) * 128], cb_[:, :], IDF, True, True, [cb_.b, CON.b], [pM.b])
                    cx.cp("dve", cq[:, :], pM[:, :], [pM.b], [cq.b])
                cqrep(0)
                for i0 in range(LA):
                    qk(i0)
                for i, (j, kb) in enumerate(pairs):
                    if i + LA < len(pairs):
                        if pairs[i + LA][0] != pairs[i + LA - 1][0]:
                            cqrep(pairs[i + LA][0])
                        qk(i + LA)
                    sub_lo = max(0, kb - 4 * j)
                    lo = sub_lo * 128
                    p, sp, pt, cq = pS[i % NPS], SP[i % 4], PT[i % 4], CQ[j % 2]
                    cx.stt("dve", sp[:, lo:512], p[:, lo:512], SCALE, cq[:, lo:512], ALU.mult, ALU.add, [p.b, cq.b], [sp.b])
                    if kb >= 4 * j:
                        cx.tt("pool", sp[:, lo:lo + 128], sp[:, lo:lo + 128], NEGM[:, :], ALU.add, [sp.b, NEGM.b], [sp.b])
                    cx.act(pt[:, lo:512], sp[:, lo:512], AF.Exp, [sp.b, NCK.b], [pt.b], bias=NCK[:, kb:kb + 1])
                    for sub in range(sub_lo, 4):
                        cx.mm(pO[sub].cols(0, 129), pt[:, sub * 128:(sub + 1) * 128], VA[:, kb, 0:129], kb == 0, kb == 4 * j + sub, [pt.b, VA.b], [pO[sub].b])
                    if kb >= 4 * j:
                        sub = kb - 4 * j
                        o = pO[sub]
                        r1, y_ = rl[sub % 2], yb[sub % 2]
                        yt_ = yTt[j % 2]
                        cx.recip(r1[:, :], o.cols(128, 129), [o.b], [r1.b])
                        cx.ts("dve", y_[:, :], o.cols(0, 128), r1[:, 0:1], None, ALU.mult, None, [o.b, r1.b], [y_.b])
                        cx.tr(pTr[:, sub * 128:(sub + 1) * 128], y_[:, :], IDB[:, :], [y_.b, IDB.b], [pTr.b])
                        cx.cp("dve", yt_[:, sub * 128:(sub + 1) * 128], pTr[:, sub * 128:(sub + 1) * 128], [pTr.b], [yt_.b])
                        if sub == 3:
                            cx.store(y_dst(512 + hh * 128, 128, j), yt_[:, :], yt_, yT_b, is_output=True)
                S.barrier()
        cx.stack = root
    cx.stack = base
    S.barrier()
    S.lane_release(lmark)


def build_B(SL, parts="0123"):
    nc = bass.Bass("TRN2", target_bir_lowering=False)
    A = declare_B(nc, SL)
    A["cosT"] = nc.dram_tensor("cosT", [128, SL], F32, kind="ExternalInput").ap()
    A["sinT"] = nc.dram_tensor("sinT", [128, SL], F32, kind="ExternalInput").ap()
    cond = nc.dram_tensor("consts", [128, 512], F32, kind="ExternalInput").ap()
    with ExitStack() as root:
        cx = Cx(nc, root)
        K = make_consts(cx, cond)
        emit_B(cx, SL, A, K, parts)
        cx.S.emit()
        build_B.stats = cx.S.stats()
    return nc


OFF = {"rq": 0, "rk": 512, "rv": 1024, "rg": 2048, "lx": 3072, "ly": 4096, "fq": 5120, "fk": 6144, "fv": 7168,
       "ff": 8192, "gates": 8200}


def _consts():
    idx = np.arange(128)
    tri = (idx[None, :] >= idx[:, None]).astype(np.float32)
    ones = np.ones((128, 128), np.float32)
    sel = np.zeros((128, 128), np.float32)
    sel[63, :] = 1.0
    ident = np.eye(128, dtype=np.float32)
    return np.ascontiguousarray(np.concatenate([tri, ones, sel, ident], axis=1))


def _rope_tables(SL):
    half = 64
    inv = (10000.0 ** (-np.arange(half, dtype=np.float32) / np.float32(half))).astype(np.float32)
    ang = (np.arange(SL, dtype=np.float32)[:, None] * inv[None, :]).astype(np.float32)
    cos = np.cos(ang.astype(np.float64)).astype(np.float32).T
    sin = np.sin(ang.astype(np.float64)).astype(np.float32).T
    cosT = np.ascontiguousarray(np.concatenate([cos, cos], axis=0))
    sinT = np.ascontiguousarray(np.concatenate([-sin, sin], axis=0))
    return cosT, sinT


def _ret_consts(g):
    gam = 1.0 - 2.0 ** (-5.0 - g)
    idx = np.arange(128)
    ch = idx // 64
    wt = (gam ** np.abs(idx[None, :] - idx[:, None])) * (ch[:, None] <= ch[None, :]) * (128.0 ** -0.5)
    dq = np.tile((gam ** (idx + 1.0))[None, :], (128, 1))
    dk = (gam ** (127.0 - idx)) * (128.0 ** -0.5)
    rc = np.concatenate([wt, dq, dk[:, None], np.full((128, 1), gam ** 128.0)], axis=1)
    return np.ascontiguousarray(rc.astype(np.float32))


def prep_B(inp, l, b, g, SL, tables, xcur=None):
    w_in = inp["w_in"][l]
    sw = (np.arange(128) + 64) % 128
    wq = w_in[:, OFF["rq"] + g * 128:OFF["rq"] + (g + 1) * 128]
    wk = w_in[:, OFF["rk"] + g * 128:OFF["rk"] + (g + 1) * 128]
    w1 = np.concatenate([wq, wq[:, sw], wk, wk[:, sw],
                         w_in[:, OFF["rv"] + g * 256:OFF["rv"] + (g + 1) * 256],
                         w_in[:, OFF["rg"] + g * 256:OFF["rg"] + (g + 1) * 256]], axis=1)
    w2 = np.concatenate([w_in[:, OFF["lx"] + g * 256:OFF["lx"] + (g + 1) * 256],
                         w_in[:, OFF["ly"] + g * 256:OFF["ly"] + (g + 1) * 256]], axis=1)
    w3 = np.stack([np.concatenate([w_in[:, OFF["fq"] + h * 128:OFF["fq"] + (h + 1) * 128],
                                   w_in[:, OFF["fk"] + h * 128:OFF["fk"] + (h + 1) * 128],
                                   w_in[:, OFF["fv"] + h * 128:OFF["fv"] + (h + 1) * 128],
                                   w_in[:, OFF["ff"] + h:OFF["ff"] + h + 1]], axis=1) for h in (2 * g, 2 * g + 1)])
    ch = slice(g * 256, (g + 1) * 256)
    lv = np.stack([inp["conv_w"][l][0, ch], inp["conv_w"][l][1, ch], inp["conv_w"][l][2, ch], inp["conv_w"][l][3, ch],
                   inp["conv_b"][l][ch], inp["lru_b_a"][l][ch], inp["lru_b_x"][l][ch], inp["lru_lambda"][l][ch]], axis=-1)
    lv = lv.reshape(2, 128, 8).transpose(1, 0, 2)
    lw = np.stack([inp["lru_w_a"][l][2 * g], inp["lru_w_a"][l][2 * g + 1], inp["lru_w_x"][l][2 * g], inp["lru_w_x"][l][2 * g + 1]])
    cosT, sinT, con = tables
    f = np.ascontiguousarray
    return {"xb": f((inp["x"] if xcur is None else xcur)[b, :SL]), "cvec": f(inp["c"][b]), "adaw": f(inp["ada_w"][l][:, 0:2048]),
            "adab": f(inp["ada_b"][l][0:2048]), "gpre": f(inp["norm_pre_mix"][l]), "w1": f(w1), "w2": f(w2), "w3": f(w3),
            "cosT": cosT, "sinT": sinT, "rc": _ret_consts(g), "lruv": f(lv.astype(np.float32)), "lruw": f(lw),
            "bf": f(inp["fox_b_f"][l][2 * g:2 * g + 2]), "consts": con}


def post_norm_residual(cx, PZ, xin, GT, modb, out, junk, ssa, ssb, rs, tt_):
    cx.act(junk[:, :], PZ[0][:, :], AF.Square, [PZ[0].b], [junk.b, ssa.b], accum=ssa[:, 0:1])
    cx.act(junk[:, :], PZ[1][:, :], AF.Square, [PZ[1].b], [junk.b, ssb.b], accum=ssb[:, 0:1])
    cx.tt("dve", ssa[:, 0:1], ssa[:, 0:1], ssb[:, 0:1], ALU.add, [ssa.b, ssb.b], [ssa.b])
    cx.act(rs[:, 0:1], ssa[:, 0:1], AF.Sqrt, [ssa.b], [rs.b], bias=EPS, scale=1.0 / D)
    cx.recip(rs[:, 0:1], rs[:, 0:1], [rs.b], [rs.b])
    for n in range(2):
        cx.stt("dve", tt_[:, n * 512:(n + 1) * 512], PZ[n][:, :], rs[:, 0:1], GT[:, n * 512:(n + 1) * 512], ALU.mult, ALU.mult,
               [PZ[n].b, rs.b, modb], [tt_.b])
    cx.tt("pool", out[:, :], xin[:, :], tt_[:, :], ALU.add, [xin.b, tt_.b], [out.b])


def declare_C(nc, T, sfx="", out_internal=False):
    def din(name, shape, dt=F32):
        DECL.append((name + sfx, tuple(shape)))
        return nc.dram_tensor(name + sfx, list(shape), dt, kind="ExternalInput").ap()
    A = {}
    A["cvec"] = din("cvec", [D])
    A["adaw"] = din("adaw", [D, 6144])
    A["adab"] = din("adab", [6144])
    A["gains"] = din("gains", [4, D])
    A["wg"] = din("wg", [D, 3072])
    A["wo"] = din("wo", [3072, D])
    A["wout"] = din("wout", [D, D])
    A["w1"] = din("w1", [D, 4096])
    A["w2"] = din("w2", [4096, D])
    A["xo"] = nc.dram_tensor("xo" + sfx, [T, D], F32, kind=("Internal" if out_internal else "ExternalOutput")).ap()
    A["mTd"] = nc.dram_tensor("mTd" + sfx, [D, T], BF16, kind="Internal").ap()
    A["x1d"] = nc.dram_tensor("x1d" + sfx, [T, D], F32, kind="Internal").ap()
    A["h2Td"] = nc.dram_tensor("h2Td" + sfx, [D, T], BF16, kind="Internal").ap()
    A["f1Td"] = nc.dram_tensor("f1Td" + sfx, [4096, T], BF16, kind="Internal").ap()
    return A


def emit_C(cx, T, A, K, parts="1234"):
    nc, S, base = cx.nc, cx.S, cx.stack
    cvec, adaw, adab, gains = A["cvec"], A["adaw"], A["adab"], A["gains"]
    wgd, wod, woutd, w1d, w2d = A["wg"], A["wo"], A["wout"], A["w1"], A["w2"]
    xo, mTd, x1d, h2Td, f1Td = A["xo"], A["mTd"], A["x1d"], A["h2Td"], A["f1Td"]
    x_tile, y_tiles = A["x_tile"], A["y_tiles"]
    x_rd, y_rd = list(A.get("x_rd", ())), list(A.get("y_rd", ()))
    CON, IDB = K["CON"], K["IDB"]
    mTd_b, x1d_b, h2Td_b, f1Td_b = Buf("mTd"), Buf("x1d"), Buf("h2Td"), Buf("f1Td")
    xo_b = A.setdefault("xo_b", Buf("xo"))
    mTv = mTd.rearrange("(c p) t -> p c t", p=128)
    h2Tv = h2Td.rearrange("(c p) t -> p c t", p=128)
    f1Tv = f1Td.rearrange("(c p) t -> p c t", p=128)
    NB = T // 128
    NT = T // 512
    lmark = S.lane_mark()
    with ExitStack() as root:
        cx.stack = root
        MOD = cx.sb([128, 6144], F32, "MOD")
        with ExitStack() as st:
            cx.stack = st
            stg = cx.sb([128, 8, 512], F32, "stg", n=2)
            pb = cx.ps([128, 512], F32, "pmod", n=2)
            GB = cx.sb([128, D], F32, "GB", n=2)
            compute_mod(cx, cvec, adaw, adab, 0, 6144, MOD, CON, stg, pb)
            for gi, (col, plus1) in enumerate(((1024, True), (2048, False), (4096, True), (5120, False))):
                gb = GB[gi % 2]
                cx.load(gb[:, :], gains[gi].partition_broadcast(128), gb)
                if plus1:
                    cx.stt("dve", MOD[:, col:col + 1024], MOD[:, col:col + 1024], 1.0, gb[:, :], ALU.add, ALU.mult, [MOD.b, gb.b], [MOD.b])
                else:
                    cx.tt("dve", MOD[:, col:col + 1024], MOD[:, col:col + 1024], gb[:, :], ALU.mult, [MOD.b, gb.b], [MOD.b])
            S.barrier()
        cx.stack = root
        SH_m, G2_m, GT_m = MOD[:, 0:1024], MOD[:, 1024:2048], MOD[:, 2048:3072]
        SH_f, G2_f, GT_f = MOD[:, 3072:4096], MOD[:, 4096:5120], MOD[:, 5120:6144]

        with ExitStack() as st:
          if "1" in parts:
            cx.stack = st
            WG = cx.sb([128, 8, 3072], BF16, "WG")
            WO = cx.sb([128, 24, 1024], BF16, "WO")
            with ExitStack() as st2:
                cx.stack = st2
                stg = cx.sb([128, 8, 512], F32, "stg", n=2)
                load_w_bf16(cx, WG, 0, wgd, 8, 3072, stg)
                load_w_bf16(cx, WO, 0, wod, 24, 1024, stg)
                S.barrier()
            cx.stack = st
            xt = cx.sb([128, D], F32, "xt", n=2)
            ss = cx.sb([128, 1], F32, "ss", n=2)
            rs = cx.sb([128, 1], F32, "rs", n=2)
            t1 = cx.sb([128, D], F32, "t1", n=2)
            hb = cx.sb([128, D], BF16, "hb", n=2)
            hTt = cx.sb([128, 8, 128], BF16, "hTt", n=2)
            yTt = cx.sb([128, 24, 128], BF16, "yTt", n=2)
            gS = cx.sb([128, 512], F32, "gS", n=2)
            mg = cx.sb([128, D], F32, "mg", n=2)
            tmp = cx.sb([128, 512], F32, "tmp", n=2)
            mgb = cx.sb([128, D], BF16, "mgb", n=2)
            mTt = cx.sb([128, 8, 512], BF16, "mTt", n=2)
            pG = cx.ps([128, 512], F32, "pG", n=2)
            pM = cx.ps([128, 512], F32, "pM", n=2)
            pTh = cx.ps([128, 8, 128], BF16, "pTh")
            pTm = cx.ps([128, 8, 128], BF16, "pTm")

            def ld(i):
                cx.load(xt[i % 2][:, :], x_tile(i), xt[i % 2], rd=x_rd)
                for c0, ncn, src in y_tiles(i):
                    cx.load(yTt[i % 2][:, c0:c0 + ncn, :], src, yTt[i % 2], rd=y_rd)
            ld(0)
            k = 0
            for i in range(NB):
                if i + 1 < NB:
                    ld(i + 1)
                x_, h_, ht, yt_, m_, mb_ = xt[i % 2], hb[i % 2], hTt[i % 2], yTt[i % 2], mg[i % 2], mgb[i % 2]
                norm_mod_tile(cx, x_, G2_m, SH_m, MOD.b, h_, ss[i % 2], rs[i % 2], t1[i % 2], h_)
                for c in range(8):
                    cx.tr(pTh[:, c, :], h_[:, c * 128:(c + 1) * 128], IDB[:, :], [h_.b, IDB.b], [pTh.b])
                cx.cp("act", ht[:, :, :], pTh[:, :, :], [pTh.b], [ht.b])
                for br in range(3):
                    for n in range(2):
                        pg, pm, g_, tm = pG[k % 2], pM[k % 2], gS[k % 2], tmp[k % 2]
                        k += 1
                        for c in range(8):
                            cx.mm(pg[:, :], ht[:, c, :], WG[:, c, br * 1024 + n * 512:br * 1024 + (n + 1) * 512], c == 0, c == 7, [ht.b, WG.b], [pg.b])
                        cx.act(g_[:, :], pg[:, :], AF.Sigmoid, [pg.b], [g_.b])
                        for c in range(8):
                            cx.mm(pm[:, :], yt_[:, br * 8 + c, :], WO[:, br * 8 + c, n * 512:(n + 1) * 512], c == 0, c == 7, [yt_.b, WO.b], [pm.b])
                        msl = m_[:, n * 512:(n + 1) * 512]
                        if br == 0:
                            cx.tt("dve", msl, g_[:, :], pm[:, :], ALU.mult, [g_.b, pm.b], [m_.b])
                        else:
                            cx.tt("dve", tm[:, :], g_[:, :], pm[:, :], ALU.mult, [g_.b, pm.b], [tm.b])
                            cx.tt("pool", msl, msl, tm[:, :], ALU.add, [m_.b, tm.b], [m_.b])
                cx.cp("pool", mb_[:, :], m_[:, :], [m_.b], [mb_.b])
                for c in range(8):
                    cx.tr(pTm[:, c, :], mb_[:, c * 128:(c + 1) * 128], IDB[:, :], [mb_.b, IDB.b], [pTm.b])
                mt = mTt[(i // 4) % 2]
                cx.cp("act", mt[:, :, (i % 4) * 128:(i % 4 + 1) * 128], pTm[:, :, :], [pTm.b], [mt.b])
                if i % 4 == 3:
                    j = i // 4
                    cx.store(mTv[:, :, j * 512:(j + 1) * 512], mt[:, :, :], mt, mTd_b)
            S.barrier()
        cx.stack = root

        with ExitStack() as st:
          if "2" in parts:
            cx.stack = st
            WOUT = cx.sb([128, 8, 1024], BF16, "WOUT")
            with ExitStack() as st2:
                cx.stack = st2
                stg = cx.sb([128, 8, 512], F32, "stg", n=2)
                load_w_bf16(cx, WOUT, 0, woutd, 8, 1024, stg)
                S.barrier()
            cx.stack = st
            xt = cx.sb([128, D], F32, "xt", n=2)
            mTt = cx.sb([128, 8, 512], BF16, "mTt", n=2)
            pZ = [cx.ps([128, 512], F32, "pZ", n=2) for _ in range(2)]
            pTh = cx.ps([128, 8, 128], BF16, "pTh")
            junk = cx.sb([128, 512], BF16, "junk")
            ssa = cx.sb([128, 1], F32, "ssa", n=2)
            ssb = cx.sb([128, 1], F32, "ssb", n=2)
            rs = cx.sb([128, 1], F32, "rs", n=2)
            tt_ = cx.sb([128, D], F32, "tt", n=2)
            x1t = cx.sb([128, D], F32, "x1t", n=2)
            ss2 = cx.sb([128, 1], F32, "ss2", n=2)
            rs2 = cx.sb([128, 1], F32, "rs2", n=2)
            t1 = cx.sb([128, D], F32, "t1", n=2)
            hb = cx.sb([128, D], BF16, "hb", n=2)
            h2Tt = cx.sb([128, 8, 512], BF16, "h2Tt", n=2)

            def ld(i):
                cx.load(xt[i % 2][:, :], x_tile(i), xt[i % 2], rd=x_rd)
                if i % 4 == 0:
                    j = i // 4
                    cx.load(mTt[j % 2][:, :, :], mTv[:, :, j * 512:(j + 1) * 512], mTt[j % 2], rd=[mTd_b])
            ld(0)
            for i in range(NB):
                if i + 1 < NB:
                    ld(i + 1)
                x_, mt, z, x1 = xt[i % 2], mTt[(i // 4) % 2], pZ[i % 2], x1t[i % 2]
                sub = i % 4
                for n in range(2):
                    for c in range(8):
                        cx.mm(z[n][:, :], mt[:, c, sub * 128:(sub + 1) * 128], WOUT[:, c, n * 512:(n + 1) * 512], c == 0, c == 7, [mt.b, WOUT.b], [z[n].b])
                post_norm_residual(cx, z, x_, GT_m, MOD.b, x1, junk, ssa[i % 2], ssb[i % 2], rs[i % 2], tt_[i % 2])
                cx.store(x1d[i * 128:(i + 1) * 128, :], x1[:, :], x1, x1d_b)
                h_ = hb[i % 2]
                norm_mod_tile(cx, x1, G2_f, SH_f, MOD.b, h_, ss2[i % 2], rs2[i % 2], t1[i % 2], h_)
                for c in range(8):
                    cx.tr(pTh[:, c, :], h_[:, c * 128:(c + 1) * 128], IDB[:, :], [h_.b, IDB.b], [pTh.b])
                ht = h2Tt[(i // 4) % 2]
                cx.cp("act", ht[:, :, sub * 128:(sub + 1) * 128], pTh[:, :, :], [pTh.b], [ht.b])
                if sub == 3:
                    j = i // 4
                    cx.store(h2Tv[:, :, j * 512:(j + 1) * 512], ht[:, :, :], ht, h2Td_b)
            S.barrier()
        cx.stack = root

        with ExitStack() as st:
          if "3" in parts:
            cx.stack = st
            W1 = cx.sb([128, 8, 4096], BF16, "W1")
            with ExitStack() as st2:
                cx.stack = st2
                stg = cx.sb([128, 8, 512], F32, "stg", n=2)
                load_w_bf16(cx, W1, 0, w1d, 8, 4096, stg)
                S.barrier()
            cx.stack = st
            h2Tt = cx.sb([128, 8, 512], BF16, "h2Tt", n=2)
            f1Tt = cx.sb([128, 32, 512], BF16, "f1Tt", n=2)
            rS = cx.sb([128, 512], F32, "rS", n=3)
            pF = cx.ps([128, 512], F32, "pF", n=4)
            cx.load(h2Tt[0][:, :, :], h2Tv[:, :, 0:512], h2Tt[0], rd=[h2Td_b])
            k = 0
            for j in range(NT):
                if j + 1 < NT:
                    cx.load(h2Tt[(j + 1) % 2][:, :, :], h2Tv[:, :, (j + 1) * 512:(j + 2) * 512], h2Tt[(j + 1) % 2], rd=[h2Td_b])
                ht, ft = h2Tt[j % 2], f1Tt[j % 2]
                for fc in range(32):
                    p, r = pF[k % 4], rS[k % 3]
                    k += 1
                    for c in range(8):
                        cx.mm(p[:, :], W1[:, c, fc * 128:(fc + 1) * 128], ht[:, c, :], c == 0, c == 7, [W1.b, ht.b], [p.b])
                    cx.act(r[:, :], p[:, :], AF.Relu, [p.b], [r.b])
                    cx.tt("pool" if fc % 2 else "dve", ft[:, fc, :], r[:, :], r[:, :], ALU.mult, [r.b], [ft.b])
                cx.store(f1Tv[:, :, j * 512:(j + 1) * 512], ft[:, :, :], ft, f1Td_b)
            S.barrier()
        cx.stack = root

        with ExitStack() as st:
          if "4" in parts:
            cx.stack = st
            W2 = cx.sb([128, 32, 1024], BF16, "W2")
            with ExitStack() as st2:
                cx.stack = st2
                stg = cx.sb([128, 8, 512], F32, "stg", n=2)
                load_w_bf16(cx, W2, 0, w2d, 32, 1024, stg)
                S.barrier()
            cx.stack = st
            f1Tt = cx.sb([128, 32, 512], BF16, "f1Tt", n=2)
            x1t = cx.sb([128, D], F32, "x1t", n=2)
            pZ = [cx.ps([128, 512], F32, "pZ", n=2) for _ in range(2)]
            junk = cx.sb([128, 512], BF16, "junk")
            ssa = cx.sb([128, 1], F32, "ssa", n=2)
            ssb = cx.sb([128, 1], F32, "ssb", n=2)
            rs = cx.sb([128, 1], F32, "rs", n=2)
            tt_ = cx.sb([128, D], F32, "tt", n=2)
            ot = cx.sb([128, D], F32, "ot", n=2)

            def ld(i):
                cx.load(x1t[i % 2][:, :], x1d[i * 128:(i + 1) * 128, :], x1t[i % 2], rd=[x1d_b])
                if i % 4 == 0:
                    j = i // 4
                    cx.load(f1Tt[j % 2][:, :, :], f1Tv[:, :, j * 512:(j + 1) * 512], f1Tt[j % 2], rd=[f1Td_b])
            ld(0)
            for i in range(NB):
                if i + 1 < NB:
                    ld(i + 1)
                x1, ft, z, o_ = x1t[i % 2], f1Tt[(i // 4) % 2], pZ[i % 2], ot[i % 2]
                sub = i % 4
                for n in range(2):
                    for c in range(32):
                        cx.mm(z[n][:, :], ft[:, c, sub * 128:(sub + 1) * 128], W2[:, c, n * 512:(n + 1) * 512], c == 0, c == 31, [ft.b, W2.b], [z[n].b])
                post_norm_residual(cx, z, x1, GT_f, MOD.b, o_, junk, ssa[i % 2], ssb[i % 2], rs[i % 2], tt_[i % 2])
                cx.store(xo[i * 128:(i + 1) * 128, :], o_[:, :], o_, xo_b, is_output=True)
            S.barrier()
        cx.stack = root
    cx.stack = base
    S.barrier()
    S.lane_release(lmark)


def build_C(T, parts="1234"):
    nc = bass.Bass("TRN2", target_bir_lowering=False)
    A = declare_C(nc, T)
    xc = nc.dram_tensor("xc", [T, D], F32, kind="ExternalInput").ap()
    yTc = nc.dram_tensor("yTc", [3072, T], BF16, kind="ExternalInput").ap()
    yTv = yTc.rearrange("(c p) t -> p c t", p=128)
    A["x_tile"] = lambda i: xc[i * 128:(i + 1) * 128, :]
    A["y_tiles"] = lambda i: [(0, 24, yTv[:, :, i * 128:(i + 1) * 128])]
    cond = nc.dram_tensor("consts", [128, 512], F32, kind="ExternalInput").ap()
    with ExitStack() as root:
        cx = Cx(nc, root)
        K = make_consts(cx, cond)
        emit_C(cx, T, A, K, parts)
        cx.S.emit()
        build_C.stats = cx.S.stats()
    return nc


def prep_C(inp, l, b, xcur, yT_all, t0, T, con):
    f = np.ascontiguousarray
    w_in = inp["w_in"][l]
    gains = np.stack([inp["norm_pre_mix"][l], inp["norm_post_mix"][l], inp["norm_pre_mlp"][l], inp["norm_post_mlp"][l]])
    wo = np.concatenate([inp["ret_w_o"][l], inp["lru_w_o"][l], inp["fox_w_o"][l]], axis=0)
    return {"xc": None if xcur is None else f(xcur[b, t0:t0 + T]), "yTc": None if yT_all is None else f(yT_all[b][:, t0:t0 + T]), "cvec": f(inp["c"][b]), "adaw": f(inp["ada_w"][l]),
            "adab": f(inp["ada_b"][l]), "gains": f(gains), "wg": f(w_in[:, OFF["gates"]:OFF["gates"] + 3072]), "wo": f(wo),
            "wout": f(inp["w_out"][l]), "w1": f(inp["mlp_w1"][l]), "w2": f(inp["mlp_w2"][l]), "consts": con}


_PROG = {}


CH_BYTES = 1 << 20


def build_F(SQ, T, L=2):
    nc = bass.Bass("TRN2", target_bir_lowering=False)
    NG = SQ // T
    groups = [list(range(g * NG, (g + 1) * NG)) for g in range(8 // NG)]
    CW = min(T, CH_BYTES // (256 * 2))
    NCC = SQ // CW
    KPT = T // CW
    XR = min(T, CH_BYTES // (D * 4))
    NXC = T // XR
    xb0 = nc.dram_tensor("xb", [SQ, D], F32, kind="ExternalInput").ap()
    cosd = nc.dram_tensor("cosT", [128, SQ], F32, kind="ExternalInput").ap()
    sind = nc.dram_tensor("sinT", [128, SQ], F32, kind="ExternalInput").ap()
    cond = nc.dram_tensor("consts", [128, 512], F32, kind="ExternalInput").ap()
    XG = [None] + [nc.dram_tensor("XG%d" % l, [NXC, NG, XR, D], F32, kind="Internal").ap() for l in range(1, L)]
    YL = [nc.dram_tensor("YL%d" % l, [NCC, 3, 256, CW], BF16, kind="Internal").ap() for l in range(L)]
    YG = [nc.dram_tensor("YG%d" % l, [NCC, 3, NG * 256, CW], BF16, kind="Internal").ap() for l in range(L)]
    YQ = [nc.dram_tensor("YQ%d" % l, [3 * NG * 256, T], BF16, kind="Internal").ap() for l in range(L)]
    XQ = nc.dram_tensor("XQ", [T, D], F32, kind="Internal").ap()
    AB = [declare_B(nc, SQ, "_b%d" % l, x_ap=xb0, y_internal=True) for l in range(L)]
    AC = [declare_C(nc, T, "_c%d" % l, out_internal=(l < L - 1)) for l in range(L)]
    with ExitStack() as root:
        cx = Cx(nc, root)
        S = cx.S
        S.need_rank = True
        K = make_consts(cx, cond)
        XG_b = None
        for l in range(L):
            A, C = AB[l], AC[l]
            A["cosT"], A["sinT"] = cosd, sind
            if l > 0:
                A["x_rd"] = [XG_b]

                def xb_tile(i, xg=XG[l]):
                    t0 = i * 128
                    return xg[(t0 % T) // XR, t0 // T, (t0 % XR):(t0 % XR) + 128, :]
                A["xb_tile"] = xb_tile

            def y_dst(r0, nr, j, yl=YL[l]):
                t0 = j * 512
                return yl[t0 // CW, r0 // 256, (r0 % 256):(r0 % 256) + nr, (t0 % CW):(t0 % CW) + 512]
            A["y_dst"] = y_dst
            emit_B(cx, SQ, A, K, parts=("" if "B" in os.environ.get("K_FSKIP", "") else "0123"))
            S.new_epoch()
            YG_b = Buf("YG%d" % l)
            for cc in range(NCC):
                for br in range(3):
                    S.coll(lambda e, i_=YL[l][cc, br], o_=YG[l][cc, br]: e.collective_compute(
                        "AllGather", ALU.bypass, replica_groups=groups, ins=[i_.opt()], outs=[o_.opt()]),
                        YG_b, reads=[A["yT_b"]], writes=[YG_b])
            YQ_b = Buf("YQ%d" % l)
            yq3 = YQ[l].rearrange("(i r) t -> i r t", i=3)
            for k in range(KPT):
                for br in range(3):
                    S.dma(yq3[br:br + 1, :, k * CW:(k + 1) * CW],
                          (lambda S_, e, k=k, br=br, yg=YG[l]: yg[bass.ds(S_.q_rank * KPT + k, 1), br, :, :]),
                          YQ_b, reads=[YG_b], writes=[YQ_b])
            yqv = YQ[l].rearrange("(c p) t -> p c t", p=128)
            C["y_tiles"] = lambda i, yqv=yqv: [(0, 24, yqv[:, :, i * 128:(i + 1) * 128])]
            C["y_rd"] = [YQ_b]
            if l == 0:
                XQ_b = Buf("XQ")
                S.dma(XQ, (lambda S_, e: xb0[bass.ds(S_.q_rank * T, T), :]), XQ_b, writes=[XQ_b])
                C["x_tile"] = lambda i: XQ[i * 128:(i + 1) * 128, :]
                C["x_rd"] = [XQ_b]
            else:
                X1 = AC[l - 1]["xo"]
                C["x_tile"] = lambda i, X1=X1: X1[i * 128:(i + 1) * 128, :]
                C["x_rd"] = [AC[l - 1]["xo_b"]]
            emit_C(cx, T, C, K)
            S.new_epoch()
            if l < L - 1:
                XG_b = Buf("XG%d" % (l + 1))
                for k in range(NXC):
                    S.coll(lambda e, i_=C["xo"][k * XR:(k + 1) * XR, :], o_=XG[l + 1][k]: e.collective_compute(
                        "AllGather", ALU.bypass, replica_groups=groups, ins=[i_.opt()], outs=[o_.opt()]),
                        XG_b, reads=[C["xo_b"]], writes=[XG_b])
        S.emit()
        build_F.stats = S.stats()
    return nc


def _prog(kind, *n):
    key = (kind,) + tuple(n)
    if key not in _PROG:
        _PROG[key] = {"B": build_B, "C": build_C, "F": build_F}[kind](*n)
    return _PROG[key]


def kernel(**inp):
    inp = {k: np.asarray(v) for k, v in inp.items()}
    Bn, SQ, _ = inp["x"].shape
    L = inp["w_in"].shape[0]
    NG = 8 // Bn
    T = SQ // NG
    con = _consts()
    tables = (*_rope_tables(SQ), con)
    cores = list(range(8))
    nc = _prog("F", SQ, T, L)
    shared = ("xb", "cosT", "sinT", "consts")
    maps = []
    for c in cores:
        b, g = c // NG, c % NG
        m = {}
        for l in range(L):
            pb = prep_B(inp, l, b, g, SQ, tables)
            for k, v in pb.items():
                if k in shared:
                    m[k] = v
                else:
                    m[k + "_b%d" % l] = v
            pc = prep_C(inp, l, b, None, None, 0, T, con)
            for k, v in pc.items():
                if k not in ("xc", "yTc", "consts"):
                    m[k + "_c%d" % l] = v
        maps.append(m)
    res = run_bass_kernel_spmd(nc, maps, core_ids=cores)
    out = np.empty((Bn, SQ, D), np.float32)
    for c in cores:
        out[c // NG, (c % NG) * T:(c % NG + 1) * T] = np.asarray(res.results[c]["xo_c%d" % (L - 1)])
    return out
```

```python
import os
import numpy as np
from contextlib import ExitStack
import concourse.bass as bass
import concourse.mybir as mybir
from concourse.bass_utils import run_bass_kernel_spmd

F32 = mybir.dt.float32
BF16 = mybir.dt.bfloat16
AF = mybir.ActivationFunctionType
ALU = mybir.AluOpType
AX = mybir.AxisListType
BIG = 1 << 40


class Buf:
    __slots__ = ("name", "lw", "rd", "lane", "bulk")

    def __init__(self, name="", bulk=False):
        self.name = name
        self.lw = None
        self.rd = {}
        self.lane = None
        self.bulk = bulk


class Sched:
    ENG = ("pe", "act", "dve", "pool", "sp")

    def __init__(self, nc, stack):
        self.nc = nc
        self.stack = stack
        self.q = {k: [] for k in self.ENG}
        self.sem = {k: stack.enter_context(nc.semaphore("s_" + k)) for k in self.ENG}
        self.cnt = {k: 0 for k in self.ENG}
        self.waited = {k: {} for k in self.ENG}
        self.lanes = []
        self.nlane = 0
        self.out_lanes = []
        self.need_rank = False
        self.free_lanes = []
        self.live_lanes = []
        self.nepoch = 1
        self.q_rank = None

    def _lane(self, buf):
        if buf.lane is None:
            if self.free_lanes and not buf.bulk:
                buf.lane = self.free_lanes.pop()
            else:
                sem = self.stack.enter_context(self.nc.semaphore("l%d" % self.nlane))
                self.nlane += 1
                buf.lane = [sem, 0, buf.bulk]
                self.lanes.append(buf.lane)
            self.live_lanes.append(buf.lane)
        return buf.lane

    def lane_mark(self):
        return len(self.live_lanes)

    def lane_release(self, mark):
        self.free_lanes.extend(self.live_lanes[mark:])
        del self.live_lanes[mark:]

    def new_epoch(self):
        for k in self.ENG:
            self.sem[k] = self.stack.enter_context(self.nc.semaphore("s%d_%s" % (self.nepoch, k)))
            self.cnt[k] = 0
        self.nepoch += 1

    def _deps(self, eng, reads, writes):
        d = {}

        def add(t):
            if t is None:
                return
            s, v = t
            if d.get(s, 0) < v:
                d[s] = v
        own = self.sem[eng]
        for b in reads:
            add(b.lw)
        for b in writes:
            if b.lw is not None and b.lw[0] is not own:
                add(b.lw)
            for s, v in b.rd.items():
                if s is not own:
                    add((s, v))
        w = self.waited[eng]
        waits = []
        for s, v in d.items():
            if eng == "pe" and s is own:
                continue
            if w.get(s, 0) < v:
                w[s] = v
                waits.append((s, v))
        return waits

    def _mark(self, tok, reads, writes):
        s, v = tok
        for b in reads:
            if b.rd.get(s, 0) < v:
                b.rd[s] = v
        for b in writes:
            b.lw = tok
            b.rd = {}

    def op(self, eng, fn, reads=(), writes=()):
        waits = self._deps(eng, reads, writes)
        self.cnt[eng] += 1
        tok = (self.sem[eng], self.cnt[eng])
        self.q[eng].append((waits, fn, self.sem[eng], 1))
        self._mark(tok, reads, writes)
        return tok

    def dma(self, out_ap, in_ap, lane_buf, reads=(), writes=(), q="sp", is_output=False, **kw):
        waits = self._deps(q, reads, writes)
        lane = self._lane(lane_buf)
        lane[1] += 16
        tok = (lane[0], BIG if lane[2] else lane[1])
        self.q[q].append((waits, lambda e: e.dma_start(out=out_ap, in_=(in_ap(self, e) if callable(in_ap) else in_ap), **kw), lane[0], 16))
        self._mark(tok, reads, writes)
        if is_output and lane not in self.out_lanes:
            self.out_lanes.append(lane)
        return tok

    def coll(self, fn, lane_buf, reads=(), writes=()):
        waits = self._deps("pool", reads, writes)
        lane = self._lane(lane_buf)
        lane[1] += 1
        tok = (lane[0], lane[1])
        self.q["pool"].append((waits, fn, lane[0], 1))
        self._mark(tok, reads, writes)
        return tok

    def barrier(self):
        targets = [(self.sem[k], self.cnt[k]) for k in ("pe", "act", "dve", "pool") if self.cnt[k] > 0]
        targets += [(l[0], l[1]) for l in self.lanes if l[1] > 0 and not l[2]]
        targets += [(l[0], BIG) for l in self.lanes if l[1] > 0 and l[2]]
        for eng in self.ENG:
            w = self.waited[eng]
            waits = []
            for s, v in targets:
                if w.get(s, 0) < v:
                    w[s] = v
                    waits.append((s, v))
            if waits:
                self.q[eng].append((waits, None, None, 0))

    def emit(self):
        nc = self.nc
        final = {id(l[0]): l[1] for l in self.lanes}
        lanes_by_sem = {id(l[0]): l for l in self.lanes}

        def run(eng_name, e):
            if eng_name == "sp" and self.need_rank:
                self.q_rank = e.snap(e.partition_id() % 4)
            for waits, fn, sem, inc in self.q[eng_name]:
                for s, v in waits:
                    if v >= BIG:
                        v = final[id(s)]
                    e.wait_ge(s, v)
                if fn is not None:
                    fn(e).then_inc(sem, inc)
            if eng_name == "sp":
                for l in self.lanes:
                    if l[1] > 0:
                        e.wait_ge(l[0], l[1])
                for k in ("pe", "act", "dve", "pool"):
                    if self.cnt[k] > 0:
                        e.wait_ge(self.sem[k], self.cnt[k])

        with nc.Block() as block:
            @block.sync
            def _(e):
                run("sp", e)

            @block.tensor
            def _(e):
                run("pe", e)

            @block.scalar
            def _(e):
                run("act", e)

            @block.vector
            def _(e):
                run("dve", e)

            @block.gpsimd
            def _(e):
                run("pool", e)

    def stats(self):
        return {k: len(v) for k, v in self.q.items()}


EPS = 1e-6
D = 1024
DECL = []


class Tl:
    __slots__ = ("t", "b")

    def __init__(self, t, name):
        self.t = t
        self.b = Buf(name)

    def __getitem__(self, k):
        return self.t[k]


class PsView:
    __slots__ = ("tl", "off", "b")

    def __init__(self, tl, off, name):
        self.tl, self.off, self.b = tl, off, Buf(name)

    def cols(self, a, b):
        return self.tl[:, self.off + a:self.off + b]


class Cx:
    def __init__(self, nc, stack):
        self.nc = nc
        self.S = Sched(nc, stack)
        self.root = stack
        self.stack = stack
        self.n = 0

    def sb(self, shape, dt, name=None, n=1):
        r = []
        for _ in range(n):
            self.n += 1
            nm = "%s_%d" % (name or "sb", self.n)
            r.append(Tl(self.stack.enter_context(self.nc.sbuf_tensor(nm, list(shape), dt)), nm))
        return r[0] if n == 1 else r

    def ps(self, shape, dt, name=None, n=1):
        r = []
        for _ in range(n):
            self.n += 1
            nm = "%s_%d" % (name or "ps", self.n)
            r.append(Tl(self.stack.enter_context(self.nc.psum_tensor(nm, list(shape), dt)), nm))
        return r[0] if n == 1 else r

    def mm(self, out, lhsT, rhs, start, stop, rd, wr, **kw):
        self.S.op("pe", lambda e: e.matmul(out, lhsT=lhsT, rhs=rhs, start=start, stop=stop, **kw), rd, wr)

    def tr(self, out, in_, ident, rd, wr):
        self.S.op("pe", lambda e: e.transpose(out=out, in_=in_, identity=ident), rd, wr)

    def act(self, out, in_, func, rd, wr, bias=None, scale=None, accum=None):
        kw = {}
        if bias is not None:
            kw["bias"] = bias
        if scale is not None:
            kw["scale"] = scale
        if accum is not None:
            kw["accum_out"] = accum
        self.S.op("act", lambda e: e.activation(out=out, in_=in_, func=func, **kw), rd, wr)

    def tt(self, eng, out, a, b, op, rd, wr):
        self.S.op(eng, lambda e: e.tensor_tensor(out=out, in0=a, in1=b, op=op), rd, wr)

    def ts(self, eng, out, a, s1, s2, op0, op1, rd, wr):
        if s2 is None:
            self.S.op(eng, lambda e: e.tensor_single_scalar(out=out, in_=a, scalar=s1, op=op0), rd, wr)
        else:
            self.S.op(eng, lambda e: e.tensor_scalar(out=out, in0=a, scalar1=s1, scalar2=s2, op0=op0, op1=op1), rd, wr)

    def stt(self, eng, out, a, s, b, op0, op1, rd, wr):
        self.S.op(eng, lambda e: e.scalar_tensor_tensor(out=out, in0=a, scalar=s, in1=b, op0=op0, op1=op1), rd, wr)

    def cp(self, eng, out, in_, rd, wr):
        if eng == "act":
            self.S.op("act", lambda e: e.copy(out=out, in_=in_), rd, wr)
        else:
            self.S.op(eng, lambda e: e.tensor_copy(out=out, in_=in_), rd, wr)

    def recip(self, out, in_, rd, wr):
        self.S.op("dve", lambda e: e.reciprocal(out=out, in_=in_), rd, wr)

    def memset(self, eng, ap, v, wr):
        self.S.op(eng, lambda e: e.memset(ap, v), (), wr)

    def load(self, out, in_, tl, q="sp", rd=()):
        self.S.dma(out, in_, tl.b, reads=rd, writes=[tl.b], q=q)

    def store(self, out, in_, tl, dbuf=None, q="pool", is_output=False):
        self.S.dma(out, in_, tl.b, reads=[tl.b], writes=[dbuf] if dbuf is not None else (), q=q, is_output=is_output)


def load_w_bf16(cx, dst, dst_c0, dram, kchunks, ncols, stg):
    src = dram.rearrange("(c p) n -> p c n", p=128)
    i = 0
    for c0 in range(0, kchunks, 8):
        c1 = min(kchunks, c0 + 8)
        for n0 in range(0, ncols, 512):
            n1 = min(ncols, n0 + 512)
            s = stg[i % len(stg)]
            i += 1
            cx.load(s[:, 0:c1 - c0, 0:n1 - n0], src[:, c0:c1, n0:n1], s)
            eng = "pool" if i % 2 == 0 else "dve"
            cx.cp(eng, dst[:, dst_c0 + c0:dst_c0 + c1, n0:n1], s[:, 0:c1 - c0, 0:n1 - n0], [s.b], [dst.b])


def compute_mod(cx, cvec, adaw, adab, col0, ncols, MOD, CON, stg, pbank):
    crep = cx.sb([128, D], F32, "crep")
    sB = cx.sb([128, 8, 128], F32, "sB")
    brep = cx.sb([128, 512], F32, "brep", n=2)
    cx.load(crep[:, :], cvec.partition_broadcast(128), crep)
    cx.act(crep[:, :], crep[:, :], AF.Silu, [crep.b], [crep.b])
    for c in range(8):
        p = pbank[c % 2]
        cx.tr(p[:, 0:128], crep[:, c * 128:(c + 1) * 128], CON[:, 384:512], [crep.b, CON.b], [p.b])
        cx.cp("dve", sB[:, c, :], p[:, 0:128], [p.b], [sB.b])
    aw = adaw.rearrange("(c p) n -> p c n", p=128)
    for i, n0 in enumerate(range(0, ncols, 512)):
        s = stg[i % len(stg)]
        br = brep[i % 2]
        p = pbank[i % 2]
        cx.load(s[:, :, :], aw[:, :, col0 + n0:col0 + n0 + 512], s)
        cx.load(br[:, :], adab[col0 + n0:col0 + n0 + 512].partition_broadcast(128), br)
        for c in range(8):
            cx.mm(p[:, :], sB[:, c, :], s[:, c, :], c == 0, c == 7, [sB.b, s.b], [p.b])
        cx.tt("dve", MOD[:, n0:n0 + 512], p[:, :], br[:, :], ALU.add, [p.b, br.b], [MOD.b])


def norm_mod_tile(cx, xt, G2, SH, modb, hb, ss, rs, t1, junk):
    cx.act(junk[:, :], xt[:, :], AF.Square, [xt.b], [junk.b, ss.b], accum=ss[:, 0:1])
    cx.act(rs[:, 0:1], ss[:, 0:1], AF.Sqrt, [ss.b], [rs.b], bias=EPS, scale=1.0 / D)
    cx.recip(rs[:, 0:1], rs[:, 0:1], [rs.b], [rs.b])
    cx.stt("dve", t1[:, :], xt[:, :], rs[:, 0:1], G2, ALU.mult, ALU.mult, [xt.b, rs.b, modb], [t1.b])
    cx.tt("pool", hb[:, :], t1[:, :], SH, ALU.add, [t1.b, modb], [hb.b])


def declare_B(nc, SL, sfx="", x_ap=None, y_internal=False):
    def din(name, shape, dt=F32):
        DECL.append((name + sfx, tuple(shape)))
        return nc.dram_tensor(name + sfx, list(shape), dt, kind="ExternalInput").ap()
    A = {}
    A["xb"] = din("xb", [SL, D]) if x_ap is None else x_ap
    A["cvec"] = din("cvec", [D])
    A["adaw"] = din("adaw", [D, 2048])
    A["adab"] = din("adab", [2048])
    A["gpre"] = din("gpre", [D])
    A["w1"] = din("w1", [D, 1024])
    A["w2"] = din("w2", [D, 512])
    A["w3"] = din("w3", [2, D, 385])
    A["rc"] = din("rc", [128, 258])
    A["lruv"] = din("lruv", [128, 2, 8])
    A["lruw"] = din("lruw", [4, 128, 128])
    A["bf"] = din("bf", [2])
    A["yT"] = nc.dram_tensor("yT" + sfx, [768, SL], BF16, kind=("Internal" if y_internal else "ExternalOutput")).ap()
    A["hTd"] = nc.dram_tensor("hTd" + sfx, [D, SL], BF16, kind="Internal").ap()
    return A


def make_consts(cx, cond):
    CON = cx.sb([128, 512], F32, "CON")
    IDB = cx.sb([128, 128], BF16, "IDB")
    MASKB = cx.sb([128, 128], BF16, "MASKB")
    ONESB = cx.sb([128, 512], F32, "ONESB")
    NEGM = cx.sb([128, 128], F32, "NEGM")
    cx.load(CON[:, :], cond, CON)
    cx.cp("dve", IDB[:, :], CON[:, 384:512], [CON.b], [IDB.b])
    cx.cp("dve", MASKB[:, :], CON[:, 0:128], [CON.b], [MASKB.b])
    cx.memset("pool", ONESB[:, :], 1.0, [ONESB.b])
    cx.ts("dve", NEGM[:, :], CON[:, 0:128], 30000.0, -30000.0, ALU.mult, ALU.add, [CON.b], [NEGM.b])
    return {"CON": CON, "IDB": IDB, "MASKB": MASKB, "ONESB": ONESB, "NEGM": NEGM}


def emit_B(cx, SL, A, K, parts="0123"):
    nc, S, base = cx.nc, cx.S, cx.stack
    xb, cvec, adaw, adab, gpre = A["xb"], A["cvec"], A["adaw"], A["adab"], A["gpre"]
    w1d, w2d, w3d, cosd, sind = A["w1"], A["w2"], A["w3"], A["cosT"], A["sinT"]
    rcd, lvd, lwd, bfd, yT, hTd = A["rc"], A["lruv"], A["lruw"], A["bf"], A["yT"], A["hTd"]
    x_rd = list(A.get("x_rd", ()))
    xb_tile = A.get("xb_tile") or (lambda i: xb[i * 128:(i + 1) * 128, :])
    y_dst = A.get("y_dst") or (lambda r0, nr, j: yT[r0:r0 + nr, j * 512:(j + 1) * 512])
    CON, IDB, MASKB, ONESB, NEGM = K["CON"], K["IDB"], K["MASKB"], K["ONESB"], K["NEGM"]
    TRI, ONES, SEL, IDF = CON[:, 0:128], CON[:, 128:256], CON[:, 256:384], CON[:, 384:512]
    hTd_b = Buf("hTd")
    yT_b = A.setdefault("yT_b", Buf("yTd"))
    hTv = hTd.rearrange("(c p) t -> p c t", p=128)
    NT = SL // 512
    NB = SL // 128
    lmark = S.lane_mark()
    with ExitStack() as root:
        cx.stack = root
        with ExitStack() as st:
            cx.stack = st
            MOD = cx.sb([128, 2048], F32, "MOD")
            GB = cx.sb([128, D], F32, "GB")
            stg = cx.sb([128, 8, 512], F32, "stg", n=2)
            pb = cx.ps([128, 512], F32, "pmod", n=2)
            compute_mod(cx, cvec, adaw, adab, 0, 2048, MOD, CON, stg, pb)
            cx.load(GB[:, :], gpre.partition_broadcast(128), GB)
            cx.stt("dve", MOD[:, 1024:2048], MOD[:, 1024:2048], 1.0, GB[:, :], ALU.add, ALU.mult, [MOD.b, GB.b], [MOD.b])
            xt = cx.sb([128, D], F32, "xt", n=3)
            junk = cx.sb([128, D], BF16, "junk")
            ss = cx.sb([128, 1], F32, "ss", n=3)
            rs = cx.sb([128, 1], F32, "rs", n=3)
            t1 = cx.sb([128, D], F32, "t1", n=2)
            hb = cx.sb([128, D], BF16, "hb", n=2)
            pT = cx.ps([128, 8, 128], BF16, "pT", n=2)
            hTt = cx.sb([128, 8, 512], BF16, "hTt", n=2)
            cx.load(xt[0][:, :], xb_tile(0), xt[0], rd=x_rd)
            for i in range(NB):
                if i + 1 < NB:
                    cx.load(xt[(i + 1) % 3][:, :], xb_tile(i + 1), xt[(i + 1) % 3], rd=x_rd)
                x_, h_, p_, g_ = xt[i % 3], hb[i % 2], pT[i % 2], hTt[(i // 4) % 2]
                norm_mod_tile(cx, x_, MOD[:, 1024:2048], MOD[:, 0:1024], MOD.b, h_, ss[i % 3], rs[i % 3], t1[i % 2], junk)
                for c in range(8):
                    cx.tr(p_[:, c, :], h_[:, c * 128:(c + 1) * 128], IDB[:, :], [h_.b, IDB.b], [p_.b])
                cx.cp("act", g_[:, :, (i % 4) * 128:(i % 4 + 1) * 128], p_[:, :, :], [p_.b], [g_.b])
                if i % 4 == 3:
                    j = i // 4
                    cx.store(hTv[:, :, j * 512:(j + 1) * 512], g_[:, :, :], g_, hTd_b)
            S.barrier()
        cx.stack = root

        with ExitStack() as st:
          if "1" in parts:
            cx.stack = st
            W1 = cx.sb([128, 8, 1024], BF16, "W1")
            stg = cx.sb([128, 8, 512], F32, "stg", n=2)
            load_w_bf16(cx, W1, 0, w1d, 8, 1024, stg)
            RC = cx.sb([128, 258], F32, "RC")
            cx.load(RC[:, :], rcd, RC)
            WT, DQ, DKC, G128 = RC[:, 0:128], RC[:, 128:256], RC[:, 256:257], RC[:, 257:258]
            St = cx.sb([128, 256], F32, "St")
            Stb = cx.sb([128, 256], BF16, "Stb", n=2)
            cx.memset("dve", St[:, :], 0.0, [St.b])
            cx.memset("dve", Stb[0][:, :], 0.0, [Stb[0].b])
            hT = cx.sb([128, 8, 512], BF16, "hT", n=2)
            cs = cx.sb([128, 512], F32, "cs", n=2)
            sn = cx.sb([128, 512], F32, "sn", n=2)
            pA = cx.ps([128, 512], F32, "pA")
            pB = cx.ps([128, 512], F32, "pB")
            pVG = cx.ps([128, 512], F32, "pVG", n=2)
            pSOs = cx.ps([128, 512], F32, "pSO", n=2)
            pU = cx.ps([128, 512], F32, "pU")
            pTr = cx.ps([128, 1024], BF16, "pTr")
            pTrK, pTrY = Buf("pTrK"), Buf("pTrY")
            pSTs, pOs = [Buf("pST0"), Buf("pST1")], [Buf("pO0"), Buf("pO1")]
            tA = cx.sb([128, 512], F32, "tA", n=2)
            tB = cx.sb([128, 512], F32, "tB", n=2)
            qT = cx.sb([128, 512], BF16, "qT", n=2)
            kT = cx.sb([128, 512], BF16, "kT", n=2)
            V = cx.sb([128, 256], BF16, "V", n=4)
            SG = cx.sb([128, 256], F32, "SG", n=4)
            SW = cx.sb([128, 128], BF16, "SW", n=2)
            qd = cx.sb([128, 128], BF16, "qd", n=2)
            kd = cx.sb([128, 128], BF16, "kd", n=2)
            stt_ = cx.sb([128, 6], F32, "bst", n=2)
            mv = cx.sb([128, 2], F32, "mv", n=2)
            rs = cx.sb([128, 1], F32, "rs", n=2)
            nb = cx.sb([128, 1], F32, "nb", n=2)
            ret = cx.sb([128, 256], F32, "ret", n=2)
            yb = cx.sb([128, 256], BF16, "yb", n=2)
            yTt = cx.sb([128, 2, 512], BF16, "yTt", n=2)

            def ld(j):
                cx.load(hT[j % 2][:, :, :], hTv[:, :, j * 512:(j + 1) * 512], hT[j % 2], rd=[hTd_b])
                cx.load(cs[j % 2][:, :], cosd[:, j * 512:(j + 1) * 512], cs[j % 2])
                cx.load(sn[j % 2][:, :], sind[:, j * 512:(j + 1) * 512], sn[j % 2])
            ld(0)
            blk = 0
            for j in range(NT):
                if j + 1 < NT:
                    ld(j + 1)
                h_, c_, s_ = hT[j % 2], cs[j % 2], sn[j % 2]
                for which, dst in ((0, qT[j % 2]), (1, kT[j % 2])):
                    for c in range(8):
                        cx.mm(pA[:, :], W1[:, c, which * 256:which * 256 + 128], h_[:, c, :], c == 0, c == 7, [W1.b, h_.b], [pA.b])
                    for c in range(8):
                        cx.mm(pB[:, :], W1[:, c, which * 256 + 128:which * 256 + 256], h_[:, c, :], c == 0, c == 7, [W1.b, h_.b], [pB.b])
                    a_, b_ = tA[which], tB[which]
                    cx.tt("dve", a_[:, :], pA[:, :], c_[:, :], ALU.mult, [pA.b, c_.b], [a_.b])
                    cx.tt("dve", b_[:, :], pB[:, :], s_[:, :], ALU.mult, [pB.b, s_.b], [b_.b])
                    cx.tt("pool", dst[:, :], a_[:, :], b_[:, :], ALU.add, [a_.b, b_.b], [dst.b])
                q_, k_ = qT[j % 2], kT[j % 2]
                yt_ = yTt[j % 2]
                for sub in range(4):
                    sl = slice(sub * 128, (sub + 1) * 128)
                    pv = pVG[sub % 2]
                    v_, sg_ = V[sub], SG[sub]
                    pSO, pST, pO = pSOs[blk % 2], pSTs[blk % 2], pOs[blk % 2]
                    for c in range(8):
                        cx.mm(pv[:, :], h_[:, c, sl], W1[:, c, 512:1024], c == 0, c == 7, [W1.b, h_.b], [pv.b])
                    cx.cp("act", v_[:, :], pv[:, 0:256], [pv.b], [v_.b])
                    cx.act(sg_[:, :], pv[:, 256:512], AF.Silu, [pv.b], [sg_.b])
                    sw_, qd_, kd_ = SW[blk % 2], qd[blk % 2], kd[blk % 2]
                    cx.mm(pSO[:, 0:128], k_[:, sl], q_[:, sl], True, True, [k_.b, q_.b], [pST])
                    cx.tt("dve", sw_[:, :], pSO[:, 0:128], WT, ALU.mult, [pST, RC.b], [sw_.b])
                    cx.tt("pool", qd_[:, :], q_[:, sl], DQ, ALU.mult, [q_.b, RC.b], [qd_.b])
                    cx.mm(pSO[:, 128:384], sw_[:, :], v_[:, :], True, False, [sw_.b, v_.b], [pO])
                    cx.mm(pSO[:, 128:384], qd_[:, :], Stb[blk % 2][:, :], False, True, [qd_.b, Stb[blk % 2].b], [pO])
                    cx.tr(pTr[:, 0:128], k_[:, sl], IDB[:, :], [k_.b, IDB.b], [pTrK])
                    cx.act(kd_[:, :], pTr[:, 0:128], AF.Copy, [pTrK, RC.b], [kd_.b], scale=DKC)
                    cx.mm(pU[:, 0:256], kd_[:, :], v_[:, :], True, True, [kd_.b, v_.b], [pU.b])
                    cx.stt("dve", St[:, :], St[:, :], G128, pU[:, 0:256], ALU.mult, ALU.add, [St.b, pU.b, RC.b], [St.b])
                    cx.cp("pool", Stb[(blk + 1) % 2][:, :], St[:, :], [St.b], [Stb[(blk + 1) % 2].b])
                    b6, m2, r1, n1 = stt_[blk % 2], mv[blk % 2], rs[blk % 2], nb[blk % 2]
                    cx.S.op("dve", lambda e, o=b6[:, :], i=pSO[:, 128:384]: e.bn_stats(out=o, in_=i), [pO], [b6.b])
                    cx.S.op("dve", lambda e, o=m2[:, :], i=b6[:, :]: e.bn_aggr(out=o, in_=i), [b6.b], [m2.b])
                    cx.act(r1[:, 0:1], m2[:, 1:2], AF.Sqrt, [m2.b], [r1.b], bias=EPS)
                    cx.recip(r1[:, 0:1], r1[:, 0:1], [r1.b], [r1.b])
                    cx.stt("dve", n1[:, 0:1], m2[:, 0:1], -1.0, r1[:, 0:1], ALU.mult, ALU.mult, [m2.b, r1.b], [n1.b])
                    rt, y_ = ret[blk % 2], yb[blk % 2]
                    cx.act(rt[:, :], pSO[:, 128:384], AF.Identity, [pO, r1.b, n1.b], [rt.b], bias=n1[:, 0:1], scale=r1[:, 0:1])
                    cx.tt("pool", y_[:, :], rt[:, :], sg_[:, :], ALU.mult, [rt.b, sg_.b], [y_.b])
                    for f in range(2):
                        cx.tr(pTr[:, 128 + f * 128:256 + f * 128], y_[:, f * 128:(f + 1) * 128], IDB[:, :], [y_.b, IDB.b], [pTrY])
                    cx.cp("act", yt_[:, :, sl], pTr[:, 128:384].rearrange("p (f t) -> p f t", f=2), [pTrY], [yt_.b])
                    blk += 1
                cx.store(y_dst(0, 256, j).rearrange("(f p) t -> p f t", p=128), yt_[:, :, :], yt_, yT_b, is_output=True)
            S.barrier()
        cx.stack = root

        with ExitStack() as st:
          if "2" in parts:
            cx.stack = st
            W2 = cx.sb([128, 8, 512], BF16, "W2")
            stg = cx.sb([128, 8, 512], F32, "stg", n=2)
            load_w_bf16(cx, W2, 0, w2d, 8, 512, stg)
            WAX = cx.sb([128, 4, 128], BF16, "WAX")
            waxs = cx.sb([128, 4, 128], F32, "waxs")
            cx.load(waxs[:, :, :], lwd.rearrange("n i o -> i n o"), waxs)
            cx.cp("dve", WAX[:, :, :], waxs[:, :, :], [waxs.b], [WAX.b])
            LV = cx.sb([128, 2, 8], F32, "LV")
            cx.load(LV[:, :, :], lvd, LV)
            EL = cx.sb([128, 2], F32, "EL")
            SPL = cx.sb([128, 2], F32, "SPL")
            M8 = cx.sb([128, 2], F32, "M8")
            M16 = cx.sb([128, 2], F32, "M16")
            cx.act(EL[:, :], LV[:, :, 7], AF.Exp, [LV.b], [EL.b], scale=-1.0)
            cx.act(SPL[:, :], EL[:, :], AF.Ln, [EL.b], [SPL.b], bias=1.0)
            cx.ts("dve", M8[:, :], SPL[:, :], -8.0, None, ALU.mult, None, [SPL.b], [M8.b])
            cx.ts("dve", M16[:, :], SPL[:, :], -16.0, None, ALU.mult, None, [SPL.b], [M16.b])
            H0 = cx.sb([128, 1], F32, "H0")
            cx.memset("dve", H0[:, :], 0.0, [H0.b])
            hT = cx.sb([128, 8, 512], BF16, "hT", n=2)
            LX = [cx.sb([128, 515], F32, "LX", n=2) for _ in range(2)]
            for b in range(2):
                cx.memset("pool", LX[b][0][:, 0:3], 0.0, [LX[b][0].b])
            pX = cx.ps([128, 512], F32, "pX", n=2)
            pR = cx.ps([128, 512], F32, "pR", n=2)
            pI = cx.ps([128, 512], F32, "pI", n=2)
            pY = cx.ps([128, 512], F32, "pY", n=2)
            u = cx.sb([128, 512], F32, "u", n=2)
            ub = cx.sb([128, 512], BF16, "ub", n=2)
            r_ = cx.sb([128, 512], F32, "r", n=2)
            ig = cx.sb([128, 512], F32, "ig", n=2)
            a_ = cx.sb([128, 512], F32, "a", n=2)
            a2 = cx.sb([128, 512], F32, "a2", n=2)
            sq = cx.sb([128, 512], F32, "sq", n=2)
            uu = cx.sb([128, 512], F32, "uu", n=2)
            hs = [cx.sb([128, 512], F32, "hs", n=2) for _ in range(2)]
            lyS = cx.sb([128, 512], F32, "lyS", n=2)
            x2 = cx.sb([128, 512], F32, "x2", n=2)
            inn = cx.sb([128, 512], F32, "inn", n=2)
            sgm = cx.sb([128, 512], F32, "sgm", n=2)
            yL = cx.sb([128, 2, 512], BF16, "yL", n=2)
            cx.load(hT[0][:, :, :], hTv[:, :, 0:512], hT[0], rd=[hTd_b])
            it = 0
            for j in range(NT):
                if j + 1 < NT:
                    cx.load(hT[(j + 1) % 2][:, :, :], hTv[:, :, (j + 1) * 512:(j + 2) * 512], hT[(j + 1) % 2], rd=[hTd_b])
                h_ = hT[j % 2]
                yl = yL[j % 2]
                for b in range(2):
                    k = it % 2
                    it += 1
                    lx, lxn = LX[b][j % 2], LX[b][(j + 1) % 2]
                    px, pr, pi, py = pX[k], pR[k], pI[k], pY[k]
                    for c in range(8):
                        cx.mm(px[:, :], W2[:, c, b * 128:(b + 1) * 128], h_[:, c, :], c == 0, c == 7, [W2.b, h_.b], [px.b])
                    for c in range(8):
                        cx.mm(py[:, :], W2[:, c, 256 + b * 128:256 + (b + 1) * 128], h_[:, c, :], c == 0, c == 7, [W2.b, h_.b], [py.b])
                    cx.cp("act", lx[:, 3:515], px[:, :], [px.b], [lx.b])
                    cx.cp("pool", lxn[:, 0:3], lx[:, 512:515], [lx.b], [lxn.b])
                    u_ = u[k]
                    cx.ts("dve", u_[:, :], lx[:, 0:512], LV[:, b, 0:1], LV[:, b, 4:5], ALU.mult, ALU.add, [lx.b, LV.b], [u_.b])
                    for i in range(1, 4):
                        cx.stt("dve", u_[:, :], lx[:, i:i + 512], LV[:, b, i:i + 1], u_[:, :], ALU.mult, ALU.add, [lx.b, LV.b, u_.b], [u_.b])
                    cx.cp("pool", ub[k][:, :], u_[:, :], [u_.b], [ub[k].b])
                    cx.mm(pr[:, :], WAX[:, b, :], ub[k][:, :], True, True, [WAX.b, ub[k].b], [pr.b])
                    cx.mm(pi[:, :], WAX[:, 2 + b, :], ub[k][:, :], True, True, [WAX.b, ub[k].b], [pi.b])
                    cx.act(r_[k][:, :], pr[:, :], AF.Sigmoid, [pr.b, LV.b], [r_[k].b], bias=LV[:, b, 5:6])
                    cx.act(ig[k][:, :], pi[:, :], AF.Sigmoid, [pi.b, LV.b], [ig[k].b], bias=LV[:, b, 6:7])
                    cx.act(a_[k][:, :], r_[k][:, :], AF.Exp, [r_[k].b, M8.b], [a_[k].b], scale=M8[:, b:b + 1])
                    cx.act(a2[k][:, :], r_[k][:, :], AF.Exp, [r_[k].b, M16.b], [a2[k].b], scale=M16[:, b:b + 1])
                    cx.act(sq[k][:, :], a2[k][:, :], AF.Sqrt, [a2[k].b], [sq[k].b], bias=1.0, scale=-1.0)
                    cx.tt("pool", uu[k][:, :], ig[k][:, :], u_[:, :], ALU.mult, [ig[k].b, u_.b], [uu[k].b])
                    cx.tt("dve", uu[k][:, :], uu[k][:, :], sq[k][:, :], ALU.mult, [uu[k].b, sq[k].b], [uu[k].b])
                    hcur = hs[b][j % 2]
                    if j == 0:
                        init, ib = H0[:, 0:1], H0.b
                    else:
                        init, ib = hs[b][(j - 1) % 2][:, 511:512], hs[b][(j - 1) % 2].b
                    cx.S.op("dve", lambda e, o=hcur[:, :], d0=a_[k][:, :], d1=uu[k][:, :], i0=init: e.tensor_tensor_scan(
                        out=o, data0=d0, data1=d1, initial=i0, op0=ALU.mult, op1=ALU.add), [a_[k].b, uu[k].b, ib], [hcur.b])
                    cx.cp("act", lyS[k][:, :], py[:, :], [py.b], [lyS[k].b])
                    cx.act(x2[k][:, :], py[:, :], AF.Square, [py.b], [x2[k].b])
                    cx.ts("dve", inn[k][:, :], x2[k][:, :], 0.044715, 1.0, ALU.mult, ALU.add, [x2[k].b], [inn[k].b])
                    cx.tt("pool", inn[k][:, :], inn[k][:, :], lyS[k][:, :], ALU.mult, [inn[k].b, lyS[k].b], [inn[k].b])
                    cx.act(sgm[k][:, :], inn[k][:, :], AF.Sigmoid, [inn[k].b], [sgm[k].b], scale=1.5957691216057308)
                    cx.tt("pool", sgm[k][:, :], sgm[k][:, :], lyS[k][:, :], ALU.mult, [sgm[k].b, lyS[k].b], [sgm[k].b])
                    cx.tt("dve", yl[:, b, :], sgm[k][:, :], hcur[:, :], ALU.mult, [sgm[k].b, hcur.b], [yl.b])
                cx.store(y_dst(256, 256, j).rearrange("(f p) t -> p f t", p=128), yl[:, :, :], yl, yT_b, is_output=True)
            S.barrier()
        cx.stack = root

        with ExitStack() as st:
          if "3" in parts:
            cx.stack = st
            W3 = cx.sb([128, 8, 388], BF16, "W3")
            stg = cx.sb([128, 8, 512], F32, "stg", n=2)
            QT = cx.sb([128, SL], BF16, "QT")
            KT = cx.sb([128, SL], BF16, "KT")
            VA = cx.sb([128, NB, 132], BF16, "VA")
            cx.memset("pool", VA[:, :, :], 1.0, [VA.b])
            BFL = cx.sb([128, 2], F32, "BFL")
            cx.load(BFL[:, :], bfd.partition_broadcast(128), BFL)
            ZF = cx.sb([128, NB], F32, "ZF")
            EZ = cx.sb([128, NB], F32, "EZ")
            LF = cx.sb([128, NB], F32, "LF")
            CK = cx.sb([128, NB], F32, "CK")
            NCK = cx.sb([128, NB], F32, "NCK")
            TOT = cx.sb([128, NB], F32, "TOT")
            INC = cx.sb([128, NB], F32, "INC")
            e1 = cx.sb([128, 1], F32, "e1", n=2)
            l1 = cx.sb([128, 1], F32, "l1", n=2)
            hT = cx.sb([128, 8, 512], BF16, "hT", n=2)
            NPS = 4
            LA = 3
            pS = cx.ps([128, 512], F32, "pS", n=NPS)
            pOt = cx.ps([128, 512], F32, "pO", n=2)
            pO = [PsView(pOt[s_ // 2], (s_ % 2) * 256, "pO%d" % s_) for s_ in range(4)]
            pM = cx.ps([128, 512], F32, "pM")
            pTr = cx.ps([128, 1024], BF16, "pTr")
            SP = cx.sb([128, 512], F32, "SP", n=4)
            PT = cx.sb([128, 512], BF16, "PT", n=4)
            CQ = cx.sb([128, 512], F32, "CQ", n=2)
            ckb = cx.sb([128, 128], F32, "ckb", n=2)
            rl = cx.sb([128, 1], F32, "rl", n=2)
            yb = cx.sb([128, 128], BF16, "yb", n=2)
            yTt = cx.sb([128, 512], BF16, "yTt", n=2)
            SCALE = 128.0 ** -0.5
            for hh in range(2):
                load_w_bf16(cx, W3, 0, w3d[hh], 8, 385, stg)
                cx.load(hT[0][:, :, :], hTv[:, :, 0:512], hT[0], rd=[hTd_b])
                for j in range(NT):
                    if j + 1 < NT:
                        cx.load(hT[(j + 1) % 2][:, :, :], hTv[:, :, (j + 1) * 512:(j + 2) * 512], hT[(j + 1) % 2], rd=[hTd_b])
                    h_ = hT[j % 2]
                    tsl = slice(j * 512, (j + 1) * 512)
                    for c in range(8):
                        cx.mm(pS[0][:, :], W3[:, c, 0:128], h_[:, c, :], c == 0, c == 7, [W3.b, h_.b], [pS[0].b])
                    cx.cp("dve", QT[:, tsl], pS[0][:, :], [pS[0].b], [QT.b])
                    for c in range(8):
                        cx.mm(pS[1][:, :], W3[:, c, 128:256], h_[:, c, :], c == 0, c == 7, [W3.b, h_.b], [pS[1].b])
                    cx.cp("act", KT[:, tsl], pS[1][:, :], [pS[1].b], [KT.b])
                    for sub in range(4):
                        if "v" in os.environ.get("K_SKIP", ""):
                            break
                        blk = j * 4 + sub
                        p = pO[(sub % 2) * 2]
                        for c in range(8):
                            cx.mm(p.cols(0, 129), h_[:, c, sub * 128:(sub + 1) * 128], W3[:, c, 256:385], c == 0, c == 7, [W3.b, h_.b], [p.b])
                        cx.cp("dve", VA[:, blk, 0:128], p.cols(0, 128), [p.b], [VA.b])
                        cx.ts("dve", ZF[:, blk:blk + 1], p.cols(128, 129), BFL[:, hh:hh + 1], None, ALU.add, None, [p.b, BFL.b], [ZF.b])
                cx.act(EZ[:, :], ZF[:, :], AF.Exp, [ZF.b], [EZ.b], scale=-1.0)
                cx.act(EZ[:, :], EZ[:, :], AF.Ln, [EZ.b], [EZ.b], bias=1.0)
                cx.ts("dve", LF[:, :], EZ[:, :], -1.0, None, ALU.mult, None, [EZ.b], [LF.b])
                if os.environ.get("K_B3STOP") == "1":
                    continue
                cx.mm(pM[:, 0:NB], TRI, LF[:, :], True, True, [CON.b, LF.b], [pM.b])
                cx.mm(pS[0][:, 0:NB], ONES, LF[:, :], True, True, [CON.b, LF.b], [pS[0].b])
                cx.cp("act", TOT[:, :], pS[0][:, 0:NB], [pS[0].b], [TOT.b])
                cx.S.op("dve", lambda e: e.tensor_tensor_scan(out=INC[:, :], data0=ONESB[:, 0:NB], data1=TOT[:, :], initial=0.0,
                                                              op0=ALU.mult, op1=ALU.add), [ONESB.b, TOT.b], [INC.b])
                cx.tt("dve", INC[:, :], INC[:, :], TOT[:, :], ALU.subtract, [INC.b, TOT.b], [INC.b])
                cx.tt("dve", CK[:, :], pM[:, 0:NB], INC[:, :], ALU.add, [pM.b, INC.b], [CK.b])
                cx.ts("dve", NCK[:, :], CK[:, :], -1.0, None, ALU.mult, None, [CK.b], [NCK.b])
                S.barrier()
                if os.environ.get("K_B3STOP") == "2":
                    continue
                pairs = [(j, kb) for j in range(NT) for kb in range(4 * j + 4)]

                def qk(i):
                    j, kb = pairs[i]
                    lo = max(0, kb - 4 * j) * 128
                    p = pS[i % NPS]
                    cx.mm(p[:, lo:512], KT[:, kb * 128:(kb + 1) * 128], QT[:, j * 512 + lo:(j + 1) * 512], True, True, [KT.b, QT.b], [p.b])

                def cqrep(j):
                    cq = CQ[j % 2]
                    for sub in range(4):
                        cb_ = ckb[sub % 2]
                        cx.ts("dve", cb_[:, :], ONESB[:, 0:128], CK[:, 4 * j + sub:4 * j + sub + 1], None, ALU.mult, None, [ONESB.b, CK.b], [cb_.b])
                        cx.mm(pM[:, sub * 128:(sub + 1) * 128], cb_[:, :], IDF, True, True, [cb_.b, CON.b], [pM.b])
                    cx.cp("dve", cq[:, :], pM[:, :], [pM.b], [cq.b])
                cqrep(0)
                for i0 in range(LA):
                    qk(i0)
                for i, (j, kb) in enumerate(pairs):
                    if i + LA < len(pairs):
                        if pairs[i + LA][0] != pairs[i + LA - 1][0]:
                            cqrep(pairs[i + LA][0])
                        qk(i + LA)
                    sub_lo = max(0, kb - 4 * j)
                    lo = sub_lo * 128
                    p, sp, pt, cq = pS[i % NPS], SP[i % 4], PT[i % 4], CQ[j % 2]
                    cx.stt("dve", sp[:, lo:512], p[:, lo:512], SCALE, cq[:, lo:512], ALU.mult, ALU.add, [p.b, cq.b], [sp.b])
                    if kb >= 4 * j:
                        cx.tt("pool", sp[:, lo:lo + 128], sp[:, lo:lo + 128], NEGM[:, :], ALU.add, [sp.b, NEGM.b], [sp.b])
                    cx.act(pt[:, lo:512], sp[:, lo:512], AF.Exp, [sp.b, NCK.b], [pt.b], bias=NCK[:, kb:kb + 1])
                    for sub in range(sub_lo, 4):
                        cx.mm(pO[sub].cols(0, 129), pt[:, sub * 128:(sub + 1) * 128], VA[:, kb, 0:129], kb == 0, kb == 4 * j + sub, [pt.b, VA.b], [pO[sub].b])
                    if kb >= 4 * j:
                        sub = kb - 4 * j
                        o = pO[sub]
                        r1, y_ = rl[sub % 2], yb[sub % 2]
                        yt_ = yTt[j % 2]
                        cx.recip(r1[:, :], o.cols(128, 129), [o.b], [r1.b])
                        cx.ts("dve", y_[:, :], o.cols(0, 128), r1[:, 0:1], None, ALU.mult, None, [o.b, r1.b], [y_.b])
                        cx.tr(pTr[:, sub * 128:(sub + 1) * 128], y_[:, :], IDB[:, :], [y_.b, IDB.b], [pTr.b])
                        cx.cp("dve", yt_[:, sub * 128:(sub + 1) * 128], pTr[:, sub * 128:(sub + 1) * 128], [pTr.b], [yt_.b])
                        if sub == 3:
                            cx.store(y_dst(512 + hh * 128, 128, j), yt_[:, :], yt_, yT_b, is_output=True)
                S.barrier()
        cx.stack = root
    cx.stack = base
    S.barrier()
    S.lane_release(lmark)


def build_B(SL, parts="0123"):
    nc = bass.Bass("TRN2", target_bir_lowering=False)
    A = declare_B(nc, SL)
    A["cosT"] = nc.dram_tensor("cosT", [128, SL], F32, kind="ExternalInput").ap()
    A["sinT"] = nc.dram_tensor("sinT", [128, SL], F32, kind="ExternalInput").ap()
    cond = nc.dram_tensor("consts", [128, 512], F32, kind="ExternalInput").ap()
    with ExitStack() as root:
        cx = Cx(nc, root)
        K = make_consts(cx, cond)
        emit_B(cx, SL, A, K, parts)
        cx.S.emit()
        build_B.stats = cx.S.stats()
    return nc


OFF = {"rq": 0, "rk": 512, "rv": 1024, "rg": 2048, "lx": 3072, "ly": 4096, "fq": 5120, "fk": 6144, "fv": 7168,
       "ff": 8192, "gates": 8200}


def _consts():
    idx = np.arange(128)
    tri = (idx[None, :] >= idx[:, None]).astype(np.float32)
    ones = np.ones((128, 128), np.float32)
    sel = np.zeros((128, 128), np.float32)
    sel[63, :] = 1.0
    ident = np.eye(128, dtype=np.float32)
    return np.ascontiguousarray(np.concatenate([tri, ones, sel, ident], axis=1))


def _rope_tables(SL):
    half = 64
    inv = (10000.0 ** (-np.arange(half, dtype=np.float32) / np.float32(half))).astype(np.float32)
    ang = (np.arange(SL, dtype=np.float32)[:, None] * inv[None, :]).astype(np.float32)
    cos = np.cos(ang.astype(np.float64)).astype(np.float32).T
    sin = np.sin(ang.astype(np.float64)).astype(np.float32).T
    cosT = np.ascontiguousarray(np.concatenate([cos, cos], axis=0))
    sinT = np.ascontiguousarray(np.concatenate([-sin, sin], axis=0))
    return cosT, sinT


def _ret_consts(g):
    gam = 1.0 - 2.0 ** (-5.0 - g)
    idx = np.arange(128)
    ch = idx // 64
    wt = (gam ** np.abs(idx[None, :] - idx[:, None])) * (ch[:, None] <= ch[None, :]) * (128.0 ** -0.5)
    dq = np.tile((gam ** (idx + 1.0))[None, :], (128, 1))
    dk = (gam ** (127.0 - idx)) * (128.0 ** -0.5)
    rc = np.concatenate([wt, dq, dk[:, None], np.full((128, 1), gam ** 128.0)], axis=1)
    return np.ascontiguousarray(rc.astype(np.float32))


def prep_B(inp, l, b, g, SL, tables, xcur=None):
    w_in = inp["w_in"][l]
    sw = (np.arange(128) + 64) % 128
    wq = w_in[:, OFF["rq"] + g * 128:OFF["rq"] + (g + 1) * 128]
    wk = w_in[:, OFF["rk"] + g * 128:OFF["rk"] + (g + 1) * 128]
    w1 = np.concatenate([wq, wq[:, sw], wk, wk[:, sw],
                         w_in[:, OFF["rv"] + g * 256:OFF["rv"] + (g + 1) * 256],
                         w_in[:, OFF["rg"] + g * 256:OFF["rg"] + (g + 1) * 256]], axis=1)
    w2 = np.concatenate([w_in[:, OFF["lx"] + g * 256:OFF["lx"] + (g + 1) * 256],
                         w_in[:, OFF["ly"] + g * 256:OFF["ly"] + (g + 1) * 256]], axis=1)
    w3 = np.stack([np.concatenate([w_in[:, OFF["fq"] + h * 128:OFF["fq"] + (h + 1) * 128],
                                   w_in[:, OFF["fk"] + h * 128:OFF["fk"] + (h + 1) * 128],
                                   w_in[:, OFF["fv"] + h * 128:OFF["fv"] + (h + 1) * 128],
                                   w_in[:, OFF["ff"] + h:OFF["ff"] + h + 1]], axis=1) for h in (2 * g, 2 * g + 1)])
    ch = slice(g * 256, (g + 1) * 256)
    lv = np.stack([inp["conv_w"][l][0, ch], inp["conv_w"][l][1, ch], inp["conv_w"][l][2, ch], inp["conv_w"][l][3, ch],
                   inp["conv_b"][l][ch], inp["lru_b_a"][l][ch], inp["lru_b_x"][l][ch], inp["lru_lambda"][l][ch]], axis=-1)
    lv = lv.reshape(2, 128, 8).transpose(1, 0, 2)
    lw = np.stack([inp["lru_w_a"][l][2 * g], inp["lru_w_a"][l][2 * g + 1], inp["lru_w_x"][l][2 * g], inp["lru_w_x"][l][2 * g + 1]])
    cosT, sinT, con = tables
    f = np.ascontiguousarray
    return {"xb": f((inp["x"] if xcur is None else xcur)[b, :SL]), "cvec": f(inp["c"][b]), "adaw": f(inp["ada_w"][l][:, 0:2048]),
            "adab": f(inp["ada_b"][l][0:2048]), "gpre": f(inp["norm_pre_mix"][l]), "w1": f(w1), "w2": f(w2), "w3": f(w3),
            "cosT": cosT, "sinT": sinT, "rc": _ret_consts(g), "lruv": f(lv.astype(np.float32)), "lruw": f(lw),
            "bf": f(inp["fox_b_f"][l][2 * g:2 * g + 2]), "consts": con}


def post_norm_residual(cx, PZ, xin, GT, modb, out, junk, ssa, ssb, rs, tt_):
    cx.act(junk[:, :], PZ[0][:, :], AF.Square, [PZ[0].b], [junk.b, ssa.b], accum=ssa[:, 0:1])
    cx.act(junk[:, :], PZ[1][:, :], AF.Square, [PZ[1].b], [junk.b, ssb.b], accum=ssb[:, 0:1])
    cx.tt("dve", ssa[:, 0:1], ssa[:, 0:1], ssb[:, 0:1], ALU.add, [ssa.b, ssb.b], [ssa.b])
    cx.act(rs[:, 0:1], ssa[:, 0:1], AF.Sqrt, [ssa.b], [rs.b], bias=EPS, scale=1.0 / D)
    cx.recip(rs[:, 0:1], rs[:, 0:1], [rs.b], [rs.b])
    for n in range(2):
        cx.stt("dve", tt_[:, n * 512:(n + 1) * 512], PZ[n][:, :], rs[:, 0:1], GT[:, n * 512:(n + 1) * 512], ALU.mult, ALU.mult,
               [PZ[n].b, rs.b, modb], [tt_.b])
    cx.tt("pool", out[:, :], xin[:, :], tt_[:, :], ALU.add, [xin.b, tt_.b], [out.b])


def declare_C(nc, T, sfx="", out_internal=False):
    def din(name, shape, dt=F32):
        DECL.append((name + sfx, tuple(shape)))
        return nc.dram_tensor(name + sfx, list(shape), dt, kind="ExternalInput").ap()
    A = {}
    A["cvec"] = din("cvec", [D])
    A["adaw"] = din("adaw", [D, 6144])
    A["adab"] = din("adab", [6144])
    A["gains"] = din("gains", [4, D])
    A["wg"] = din("wg", [D, 3072])
    A["wo"] = din("wo", [3072, D])
    A["wout"] = din("wout", [D, D])
    A["w1"] = din("w1", [D, 4096])
    A["w2"] = din("w2", [4096, D])
    A["xo"] = nc.dram_tensor("xo" + sfx, [T, D], F32, kind=("Internal" if out_internal else "ExternalOutput")).ap()
    A["mTd"] = nc.dram_tensor("mTd" + sfx, [D, T], BF16, kind="Internal").ap()
    A["x1d"] = nc.dram_tensor("x1d" + sfx, [T, D], F32, kind="Internal").ap()
    A["h2Td"] = nc.dram_tensor("h2Td" + sfx, [D, T], BF16, kind="Internal").ap()
    A["f1Td"] = nc.dram_tensor("f1Td" + sfx, [4096, T], BF16, kind="Internal").ap()
    return A


def emit_C(cx, T, A, K, parts="1234"):
    nc, S, base = cx.nc, cx.S, cx.stack
    cvec, adaw, adab, gains = A["cvec"], A["adaw"], A["adab"], A["gains"]
    wgd, wod, woutd, w1d, w2d = A["wg"], A["wo"], A["wout"], A["w1"], A["w2"]
    xo, mTd, x1d, h2Td, f1Td = A["xo"], A["mTd"], A["x1d"], A["h2Td"], A["f1Td"]
    x_tile, y_tiles = A["x_tile"], A["y_tiles"]
    x_rd, y_rd = list(A.get("x_rd", ())), list(A.get("y_rd", ()))
    CON, IDB = K["CON"], K["IDB"]
    mTd_b, x1d_b, h2Td_b, f1Td_b = Buf("mTd"), Buf("x1d"), Buf("h2Td"), Buf("f1Td")
    xo_b = A.setdefault("xo_b", Buf("xo"))
    mTv = mTd.rearrange("(c p) t -> p c t", p=128)
    h2Tv = h2Td.rearrange("(c p) t -> p c t", p=128)
    f1Tv = f1Td.rearrange("(c p) t -> p c t", p=128)
    NB = T // 128
    NT = T // 512
    lmark = S.lane_mark()
    with ExitStack() as root:
        cx.stack = root
        MOD = cx.sb([128, 6144], F32, "MOD")
        with ExitStack() as st:
            cx.stack = st
            stg = cx.sb([128, 8, 512], F32, "stg", n=2)
            pb = cx.ps([128, 512], F32, "pmod", n=2)
            GB = cx.sb([128, D], F32, "GB", n=2)
            compute_mod(cx, cvec, adaw, adab, 0, 6144, MOD, CON, stg, pb)
            for gi, (col, plus1) in enumerate(((1024, True), (2048, False), (4096, True), (5120, False))):
                gb = GB[gi % 2]
                cx.load(gb[:, :], gains[gi].partition_broadcast(128), gb)
                if plus1:
                    cx.stt("dve", MOD[:, col:col + 1024], MOD[:, col:col + 1024], 1.0, gb[:, :], ALU.add, ALU.mult, [MOD.b, gb.b], [MOD.b])
                else:
                    cx.tt("dve", MOD[:, col:col + 1024], MOD[:, col:col + 1024], gb[:, :], ALU.mult, [MOD.b, gb.b], [MOD.b])
            S.barrier()
        cx.stack = root
        SH_m, G2_m, GT_m = MOD[:, 0:1024], MOD[:, 1024:2048], MOD[:, 2048:3072]
        SH_f, G2_f, GT_f = MOD[:, 3072:4096], MOD[:, 4096:5120], MOD[:, 5120:6144]

        with ExitStack() as st:
          if "1" in parts:
            cx.stack = st
            WG = cx.sb([128, 8, 3072], BF16, "WG")
            WO = cx.sb([128, 24, 1024], BF16, "WO")
            with ExitStack() as st2:
                cx.stack = st2
                stg = cx.sb([128, 8, 512], F32, "stg", n=2)
                load_w_bf16(cx, WG, 0, wgd, 8, 3072, stg)
                load_w_bf16(cx, WO, 0, wod, 24, 1024, stg)
                S.barrier()
            cx.stack = st
            xt = cx.sb([128, D], F32, "xt", n=2)
            ss = cx.sb([128, 1], F32, "ss", n=2)
            rs = cx.sb([128, 1], F32, "rs", n=2)
            t1 = cx.sb([128, D], F32, "t1", n=2)
            hb = cx.sb([128, D], BF16, "hb", n=2)
            hTt = cx.sb([128, 8, 128], BF16, "hTt", n=2)
            yTt = cx.sb([128, 24, 128], BF16, "yTt", n=2)
            gS = cx.sb([128, 512], F32, "gS", n=2)
            mg = cx.sb([128, D], F32, "mg", n=2)
            tmp = cx.sb([128, 512], F32, "tmp", n=2)
            mgb = cx.sb([128, D], BF16, "mgb", n=2)
            mTt = cx.sb([128, 8, 512], BF16, "mTt", n=2)
            pG = cx.ps([128, 512], F32, "pG", n=2)
            pM = cx.ps([128, 512], F32, "pM", n=2)
            pTh = cx.ps([128, 8, 128], BF16, "pTh")
            pTm = cx.ps([128, 8, 128], BF16, "pTm")

            def ld(i):
                cx.load(xt[i % 2][:, :], x_tile(i), xt[i % 2], rd=x_rd)
                for c0, ncn, src in y_tiles(i):
                    cx.load(yTt[i % 2][:, c0:c0 + ncn, :], src, yTt[i % 2], rd=y_rd)
            ld(0)
            k = 0
            for i in range(NB):
                if i + 1 < NB:
                    ld(i + 1)
                x_, h_, ht, yt_, m_, mb_ = xt[i % 2], hb[i % 2], hTt[i % 2], yTt[i % 2], mg[i % 2], mgb[i % 2]
                norm_mod_tile(cx, x_, G2_m, SH_m, MOD.b, h_, ss[i % 2], rs[i % 2], t1[i % 2], h_)
                for c in range(8):
                    cx.tr(pTh[:, c, :], h_[:, c * 128:(c + 1) * 128], IDB[:, :], [h_.b, IDB.b], [pTh.b])
                cx.cp("act", ht[:, :, :], pTh[:, :, :], [pTh.b], [ht.b])
                for br in range(3):
                    for n in range(2):
                        pg, pm, g_, tm = pG[k % 2], pM[k % 2], gS[k % 2], tmp[k % 2]
                        k += 1
                        for c in range(8):
                            cx.mm(pg[:, :], ht[:, c, :], WG[:, c, br * 1024 + n * 512:br * 1024 + (n + 1) * 512], c == 0, c == 7, [ht.b, WG.b], [pg.b])
                        cx.act(g_[:, :], pg[:, :], AF.Sigmoid, [pg.b], [g_.b])
                        for c in range(8):
                            cx.mm(pm[:, :], yt_[:, br * 8 + c, :], WO[:, br * 8 + c, n * 512:(n + 1) * 512], c == 0, c == 7, [yt_.b, WO.b], [pm.b])
                        msl = m_[:, n * 512:(n + 1) * 512]
                        if br == 0:
                            cx.tt("dve", msl, g_[:, :], pm[:, :], ALU.mult, [g_.b, pm.b], [m_.b])
                        else:
                            cx.tt("dve", tm[:, :], g_[:, :], pm[:, :], ALU.mult, [g_.b, pm.b], [tm.b])
                            cx.tt("pool", msl, msl, tm[:, :], ALU.add, [m_.b, tm.b], [m_.b])
                cx.cp("pool", mb_[:, :], m_[:, :], [m_.b], [mb_.b])
                for c in range(8):
                    cx.tr(pTm[:, c, :], mb_[:, c * 128:(c + 1) * 128], IDB[:, :], [mb_.b, IDB.b], [pTm.b])
                mt = mTt[(i // 4) % 2]
                cx.cp("act", mt[:, :, (i % 4) * 128:(i % 4 + 1) * 128], pTm[:, :, :], [pTm.b], [mt.b])
                if i % 4 == 3:
                    j = i // 4
                    cx.store(mTv[:, :, j * 512:(j + 1) * 512], mt[:, :, :], mt, mTd_b)
            S.barrier()
        cx.stack = root

        with ExitStack() as st:
          if "2" in parts:
            cx.stack = st
            WOUT = cx.sb([128, 8, 1024], BF16, "WOUT")
            with ExitStack() as st2:
                cx.stack = st2
                stg = cx.sb([128, 8, 512], F32, "stg", n=2)
                load_w_bf16(cx, WOUT, 0, woutd, 8, 1024, stg)
                S.barrier()
            cx.stack = st
            xt = cx.sb([128, D], F32, "xt", n=2)
            mTt = cx.sb([128, 8, 512], BF16, "mTt", n=2)
            pZ = [cx.ps([128, 512], F32, "pZ", n=2) for _ in range(2)]
            pTh = cx.ps([128, 8, 128], BF16, "pTh")
            junk = cx.sb([128, 512], BF16, "junk")
            ssa = cx.sb([128, 1], F32, "ssa", n=2)
            ssb = cx.sb([128, 1], F32, "ssb", n=2)
            rs = cx.sb([128, 1], F32, "rs", n=2)
            tt_ = cx.sb([128, D], F32, "tt", n=2)
            x1t = cx.sb([128, D], F32, "x1t", n=2)
            ss2 = cx.sb([128, 1], F32, "ss2", n=2)
            rs2 = cx.sb([128, 1], F32, "rs2", n=2)
            t1 = cx.sb([128, D], F32, "t1", n=2)
            hb = cx.sb([128, D], BF16, "hb", n=2)
            h2Tt = cx.sb([128, 8, 512], BF16, "h2Tt", n=2)

            def ld(i):
                cx.load(xt[i % 2][:, :], x_tile(i), xt[i % 2], rd=x_rd)
                if i % 4 == 0:
                    j = i // 4
                    cx.load(mTt[j % 2][:, :, :], mTv[:, :, j * 512:(j + 1) * 512], mTt[j % 2], rd=[mTd_b])
            ld(0)
            for i in range(NB):
                if i + 1 < NB:
                    ld(i + 1)
                x_, mt, z, x1 = xt[i % 2], mTt[(i // 4) % 2], pZ[i % 2], x1t[i % 2]
                sub = i % 4
                for n in range(2):
                    for c in range(8):
                        cx.mm(z[n][:, :], mt[:, c, sub * 128:(sub + 1) * 128], WOUT[:, c, n * 512:(n + 1) * 512], c == 0, c == 7, [mt.b, WOUT.b], [z[n].b])
                post_norm_residual(cx, z, x_, GT_m, MOD.b, x1, junk, ssa[i % 2], ssb[i % 2], rs[i % 2], tt_[i % 2])
                cx.store(x1d[i * 128:(i + 1) * 128, :], x1[:, :], x1, x1d_b)
                h_ = hb[i % 2]
                norm_mod_tile(cx, x1, G2_f, SH_f, MOD.b, h_, ss2[i % 2], rs2[i % 2], t1[i % 2], h_)
                for c in range(8):
                    cx.tr(pTh[:, c, :], h_[:, c * 128:(c + 1) * 128], IDB[:, :], [h_.b, IDB.b], [pTh.b])
                ht = h2Tt[(i // 4) % 2]
                cx.cp("act", ht[:, :, sub * 128:(sub + 1) * 128], pTh[:, :, :], [pTh.b], [ht.b])
                if sub == 3:
                    j = i // 4
                    cx.store(h2Tv[:, :, j * 512:(j + 1) * 512], ht[:, :, :], ht, h2Td_b)
            S.barrier()
        cx.stack = root

        with ExitStack() as st:
          if "3" in parts:
            cx.stack = st
            W1 = cx.sb([128, 8, 4096], BF16, "W1")
            with ExitStack() as st2:
                cx.stack = st2
                stg = cx.sb([128, 8, 512], F32, "stg", n=2)
                load_w_bf16(cx, W1, 0, w1d, 8, 4096, stg)
                S.barrier()
            cx.stack = st
            h2Tt = cx.sb([128, 8, 512], BF16, "h2Tt", n=2)
            f1Tt = cx.sb([128, 32, 512], BF16, "f1Tt", n=2)
            rS = cx.sb([128, 512], F32, "rS", n=3)
            pF = cx.ps([128, 512], F32, "pF", n=4)
            cx.load(h2Tt[0][:, :, :], h2Tv[:, :, 0:512], h2Tt[0], rd=[h2Td_b])
            k = 0
            for j in range(NT):
                if j + 1 < NT:
                    cx.load(h2Tt[(j + 1) % 2][:, :, :], h2Tv[:, :, (j + 1) * 512:(j + 2) * 512], h2Tt[(j + 1) % 2], rd=[h2Td_b])
                ht, ft = h2Tt[j % 2], f1Tt[j % 2]
                for fc in range(32):
                    p, r = pF[k % 4], rS[k % 3]
                    k += 1
                    for c in range(8):
                        cx.mm(p[:, :], W1[:, c, fc * 128:(fc + 1) * 128], ht[:, c, :], c == 0, c == 7, [W1.b, ht.b], [p.b])
                    cx.act(r[:, :], p[:, :], AF.Relu, [p.b], [r.b])
                    cx.tt("pool" if fc % 2 else "dve", ft[:, fc, :], r[:, :], r[:, :], ALU.mult, [r.b], [ft.b])
                cx.store(f1Tv[:, :, j * 512:(j + 1) * 512], ft[:, :, :], ft, f1Td_b)
            S.barrier()
        cx.stack = root

        with ExitStack() as st:
          if "4" in parts:
            cx.stack = st
            W2 = cx.sb([128, 32, 1024], BF16, "W2")
            with ExitStack() as st2:
                cx.stack = st2
                stg = cx.sb([128, 8, 512], F32, "stg", n=2)
                load_w_bf16(cx, W2, 0, w2d, 32, 1024, stg)
                S.barrier()
            cx.stack = st
            f1Tt = cx.sb([128, 32, 512], BF16, "f1Tt", n=2)
            x1t = cx.sb([128, D], F32, "x1t", n=2)
            pZ = [cx.ps([128, 512], F32, "pZ", n=2) for _ in range(2)]
            junk = cx.sb([128, 512], BF16, "junk")
            ssa = cx.sb([128, 1], F32, "ssa", n=2)
            ssb = cx.sb([128, 1], F32, "ssb", n=2)
            rs = cx.sb([128, 1], F32, "rs", n=2)
            tt_ = cx.sb([128, D], F32, "tt", n=2)
            ot = cx.sb([128, D], F32, "ot", n=2)

            def ld(i):
                cx.load(x1t[i % 2][:, :], x1d[i * 128:(i + 1) * 128, :], x1t[i % 2], rd=[x1d_b])
                if i % 4 == 0:
                    j = i // 4
                    cx.load(f1Tt[j % 2][:, :, :], f1Tv[:, :, j * 512:(j + 1) * 512], f1Tt[j % 2], rd=[f1Td_b])
            ld(0)
            for i in range(NB):
                if i + 1 < NB:
                    ld(i + 1)
                x1, ft, z, o_ = x1t[i % 2], f1Tt[(i // 4) % 2], pZ[i % 2], ot[i % 2]
                sub = i % 4
                for n in range(2):
                    for c in range(32):
                        cx.mm(z[n][:, :], ft[:, c, sub * 128:(sub + 1) * 128], W2[:, c, n * 512:(n + 1) * 512], c == 0, c == 31, [ft.b, W2.b], [z[n].b])
                post_norm_residual(cx, z, x1, GT_f, MOD.b, o_, junk, ssa[i % 2], ssb[i % 2], rs[i % 2], tt_[i % 2])
                cx.store(xo[i * 128:(i + 1) * 128, :], o_[:, :], o_, xo_b, is_output=True)
            S.barrier()
        cx.stack = root
    cx.stack = base
    S.barrier()
    S.lane_release(lmark)


def build_C(T, parts="1234"):
    nc = bass.Bass("TRN2", target_bir_lowering=False)
    A = declare_C(nc, T)
    xc = nc.dram_tensor("xc", [T, D], F32, kind="ExternalInput").ap()
    yTc = nc.dram_tensor("yTc", [3072, T], BF16, kind="ExternalInput").ap()
    yTv = yTc.rearrange("(c p) t -> p c t", p=128)
    A["x_tile"] = lambda i: xc[i * 128:(i + 1) * 128, :]
    A["y_tiles"] = lambda i: [(0, 24, yTv[:, :, i * 128:(i + 1) * 128])]
    cond = nc.dram_tensor("consts", [128, 512], F32, kind="ExternalInput").ap()
    with ExitStack() as root:
        cx = Cx(nc, root)
        K = make_consts(cx, cond)
        emit_C(cx, T, A, K, parts)
        cx.S.emit()
        build_C.stats = cx.S.stats()
    return nc


def prep_C(inp, l, b, xcur, yT_all, t0, T, con):
    f = np.ascontiguousarray
    w_in = inp["w_in"][l]
    gains = np.stack([inp["norm_pre_mix"][l], inp["norm_post_mix"][l], inp["norm_pre_mlp"][l], inp["norm_post_mlp"][l]])
    wo = np.concatenate([inp["ret_w_o"][l], inp["lru_w_o"][l], inp["fox_w_o"][l]], axis=0)
    return {"xc": None if xcur is None else f(xcur[b, t0:t0 + T]), "yTc": None if yT_all is None else f(yT_all[b][:, t0:t0 + T]), "cvec": f(inp["c"][b]), "adaw": f(inp["ada_w"][l]),
            "adab": f(inp["ada_b"][l]), "gains": f(gains), "wg": f(w_in[:, OFF["gates"]:OFF["gates"] + 3072]), "wo": f(wo),
            "wout": f(inp["w_out"][l]), "w1": f(inp["mlp_w1"][l]), "w2": f(inp["mlp_w2"][l]), "consts": con}


_PROG = {}


CH_BYTES = 1 << 20


def build_F(SQ, T, L=2):
    nc = bass.Bass("TRN2", target_bir_lowering=False)
    NG = SQ // T
    groups = [list(range(g * NG, (g + 1) * NG)) for g in range(8 // NG)]
    CW = min(T, CH_BYTES // (256 * 2))
    NCC = SQ // CW
    KPT = T // CW
    XR = min(T, CH_BYTES // (D * 4))
    NXC = T // XR
    xb0 = nc.dram_tensor("xb", [SQ, D], F32, kind="ExternalInput").ap()
    cosd = nc.dram_tensor("cosT", [128, SQ], F32, kind="ExternalInput").ap()
    sind = nc.dram_tensor("sinT", [128, SQ], F32, kind="ExternalInput").ap()
    cond = nc.dram_tensor("consts", [128, 512], F32, kind="ExternalInput").ap()
    XG = [None] + [nc.dram_tensor("XG%d" % l, [NXC, NG, XR, D], F32, kind="Internal").ap() for l in range(1, L)]
    YL = [nc.dram_tensor("YL%d" % l, [NCC, 3, 256, CW], BF16, kind="Internal").ap() for l in range(L)]
    YG = [nc.dram_tensor("YG%d" % l, [NCC, 3, NG * 256, CW], BF16, kind="Internal").ap() for l in range(L)]
    YQ = [nc.dram_tensor("YQ%d" % l, [3 * NG * 256, T], BF16, kind="Internal").ap() for l in range(L)]
    XQ = nc.dram_tensor("XQ", [T, D], F32, kind="Internal").ap()
    AB = [declare_B(nc, SQ, "_b%d" % l, x_ap=xb0, y_internal=True) for l in range(L)]
    AC = [declare_C(nc, T, "_c%d" % l, out_internal=(l < L - 1)) for l in range(L)]
    with ExitStack() as root:
        cx = Cx(nc, root)
        S = cx.S
        S.need_rank = True
        K = make_consts(cx, cond)
        XG_b = None
        for l in range(L):
            A, C = AB[l], AC[l]
            A["cosT"], A["sinT"] = cosd, sind
            if l > 0:
                A["x_rd"] = [XG_b]

                def xb_tile(i, xg=XG[l]):
                    t0 = i * 128
                    return xg[(t0 % T) // XR, t0 // T, (t0 % XR):(t0 % XR) + 128, :]
                A["xb_tile"] = xb_tile

            def y_dst(r0, nr, j, yl=YL[l]):
                t0 = j * 512
                return yl[t0 // CW, r0 // 256, (r0 % 256):(r0 % 256) + nr, (t0 % CW):(t0 % CW) + 512]
            A["y_dst"] = y_dst
            emit_B(cx, SQ, A, K, parts=("" if "B" in os.environ.get("K_FSKIP", "") else "0123"))
            S.new_epoch()
            YG_b = Buf("YG%d" % l)
            for cc in range(NCC):
                for br in range(3):
                    S.coll(lambda e, i_=YL[l][cc, br], o_=YG[l][cc, br]: e.collective_compute(
                        "AllGather", ALU.bypass, replica_groups=groups, ins=[i_.opt()], outs=[o_.opt()]),
                        YG_b, reads=[A["yT_b"]], writes=[YG_b])
            YQ_b = Buf("YQ%d" % l)
            yq3 = YQ[l].rearrange("(i r) t -> i r t", i=3)
            for k in range(KPT):
                for br in range(3):
                    S.dma(yq3[br:br + 1, :, k * CW:(k + 1) * CW],
                          (lambda S_, e, k=k, br=br, yg=YG[l]: yg[bass.ds(S_.q_rank * KPT + k, 1), br, :, :]),
                          YQ_b, reads=[YG_b], writes=[YQ_b])
            yqv = YQ[l].rearrange("(c p) t -> p c t", p=128)
            C["y_tiles"] = lambda i, yqv=yqv: [(0, 24, yqv[:, :, i * 128:(i + 1) * 128])]
            C["y_rd"] = [YQ_b]
            if l == 0:
                XQ_b = Buf("XQ")
                S.dma(XQ, (lambda S_, e: xb0[bass.ds(S_.q_rank * T, T), :]), XQ_b, writes=[XQ_b])
                C["x_tile"] = lambda i: XQ[i * 128:(i + 1) * 128, :]
                C["x_rd"] = [XQ_b]
            else:
                X1 = AC[l - 1]["xo"]
                C["x_tile"] = lambda i, X1=X1: X1[i * 128:(i + 1) * 128, :]
                C["x_rd"] = [AC[l - 1]["xo_b"]]
            emit_C(cx, T, C, K)
            S.new_epoch()
            if l < L - 1:
                XG_b = Buf("XG%d" % (l + 1))
                for k in range(NXC):
                    S.coll(lambda e, i_=C["xo"][k * XR:(k + 1) * XR, :], o_=XG[l + 1][k]: e.collective_compute(
                        "AllGather", ALU.bypass, replica_groups=groups, ins=[i_.opt()], outs=[o_.opt()]),
                        XG_b, reads=[C["xo_b"]], writes=[XG_b])
        S.emit()
        build_F.stats = S.stats()
    return nc


def _prog(kind, *n):
    key = (kind,) + tuple(n)
    if key not in _PROG:
        _PROG[key] = {"B": build_B, "C": build_C, "F": build_F}[kind](*n)
    return _PROG[key]


def kernel(**inp):
    inp = {k: np.asarray(v) for k, v in inp.items()}
    Bn, SQ, _ = inp["x"].shape
    L = inp["w_in"].shape[0]
    NG = 8 // Bn
    T = SQ // NG
    con = _consts()
    tables = (*_rope_tables(SQ), con)
    cores = list(range(8))
    nc = _prog("F", SQ, T, L)
    shared = ("xb", "cosT", "sinT", "consts")
    maps = []
    for c in cores:
        b, g = c // NG, c % NG
        m = {}
        for l in range(L):
            pb = prep_B(inp, l, b, g, SQ, tables)
            for k, v in pb.items():
                if k in shared:
                    m[k] = v
                else:
                    m[k + "_b%d" % l] = v
            pc = prep_C(inp, l, b, None, None, 0, T, con)
            for k, v in pc.items():
                if k not in ("xc", "yTc", "consts"):
                    m[k + "_c%d" % l] = v
        maps.append(m)
    res = run_bass_kernel_spmd(nc, maps, core_ids=cores)
    out = np.empty((Bn, SQ, D), np.float32)
    for c in cores:
        out[c // NG, (c % NG) * T:(c % NG + 1) * T] = np.asarray(res.results[c]["xo_c%d" % (L - 1)])
    return out
```
